# Optimizing a Trainium2 kernel written in Bass

```python
import functools
import jax
import jax.numpy as jnp
from jax import lax
import numpy as np

D_MODEL = 2048
BATCH = 32
SEQ = 256
DEPTH = 4
DEC_BATCH = 2
DEC_SEQ = 2048
PAST_LEN = 512

GRID_W = 64
MIX_W = D_MODEL
GROUP_W = MIX_W // 4
POOL_WINDOWS = (2, 4, 8, 16)
POOL_CH = GROUP_W // len(POOL_WINDOWS)
LRU_W = GROUP_W
LRU_BLOCKS = 8
LRU_BLOCK_W = LRU_W // LRU_BLOCKS
LRU_CONV = 4
LRU_C = 8.0
CM_W = GROUP_W
CM_CONV = 31
NA_HEADS = 8
NA_HEAD_DIM = GROUP_W // NA_HEADS
NA_KH = 8
NA_KW = 16
Q_BLOCK = 128
D_FF = ((8 * D_MODEL // 3 + 255) // 256) * 256
N_MOD = 6

P_POOL = 0
P_LRU_X = P_POOL + GROUP_W
P_LRU_G = P_LRU_X + LRU_W
P_CM = P_LRU_G + LRU_W
P_QKV = P_CM + 2 * CM_W
D_IN = P_QKV + 3 * GROUP_W

EPS = 1e-6
NEG_INF = -1e30

kernel_name = 'hybrid_pool_lru_conv_natten_prefix_step'


def rms_norm(x, g):
    xf = x.astype(jnp.float32)
    y = xf * lax.rsqrt(jnp.mean(xf * xf, axis=-1, keepdims=True) + EPS)
    return (y * g.astype(jnp.float32)).astype(x.dtype)


def layer_norm(x, g, b):
    xf = x.astype(jnp.float32)
    mu = jnp.mean(xf, axis=-1, keepdims=True)
    var = jnp.mean(jnp.square(xf - mu), axis=-1, keepdims=True)
    y = (xf - mu) * lax.rsqrt(var + EPS)
    return (y * g.astype(jnp.float32) + b.astype(jnp.float32)).astype(x.dtype)


def modulation(cond, w_ada, b_ada):
    m = jax.nn.silu(cond) @ w_ada + b_ada
    return jnp.split(m[:, None, :], N_MOD, axis=-1)


def modulate(h, shift, scale):
    return h * (1.0 + scale) + shift


def depthwise_conv(x, w, b, pad_left, pad_right):
    c = x.shape[-1]
    y = lax.conv_general_dilated(x, w[:, None, :].astype(x.dtype), window_strides=(1,),
                                 padding=[(pad_left, pad_right)],
                                 dimension_numbers=('NWC', 'WIO', 'NWC'),
                                 feature_group_count=c)
    return y + b


def pool_mixer(u, pool_w, pool_scale):
    b, t, _ = u.shape
    uf = u.astype(jnp.float32)
    cs = jnp.concatenate([jnp.zeros_like(uf[:, :1]), jnp.cumsum(uf, axis=1)], axis=1)
    pos = jnp.arange(t)
    outs = []
    for gi, w in enumerate(POOL_WINDOWS):
        lo = jnp.clip(pos - w // 2, 0, t)
        hi = jnp.clip(pos - w // 2 + w, 0, t)
        csg = cs[..., gi * POOL_CH:(gi + 1) * POOL_CH]
        s = jnp.take(csg, hi, axis=1) - jnp.take(csg, lo, axis=1)
        outs.append(s / (hi - lo).astype(jnp.float32)[None, :, None])
    pooled = (jnp.concatenate(outs, axis=-1) - uf).reshape(b, t, len(POOL_WINDOWS), POOL_CH)
    y = jnp.einsum('btgc,gcd->btgd', pooled, pool_w.astype(jnp.float32)).reshape(b, t, GROUP_W)
    return (y * pool_scale.astype(jnp.float32)).astype(u.dtype)


def lru_scan(a, bterm, h0):
    def combine(e1, e2):
        a1, b1 = e1
        a2, b2 = e2
        return a1 * a2, a2 * b1 + b2
    a_cum, b_cum = lax.associative_scan(combine, (a, bterm), axis=1)
    return b_cum + a_cum * h0[:, None, :]


def rg_lru_mixer(xb, gb, conv_w, conv_b, gate_w, gate_b, lam, h0):
    b, t, _ = xb.shape
    x = depthwise_conv(xb, conv_w, conv_b, LRU_CONV // 2, LRU_CONV - 1 - LRU_CONV // 2)
    xf = x.astype(jnp.float32)
    xh = xf.reshape(b, t, LRU_BLOCKS, LRU_BLOCK_W)
    gates = jnp.einsum('btkc,dgkce->dgbtke', xh, gate_w.astype(jnp.float32)).reshape(2, 2, b, t, LRU_W)
    gates = gates + gate_b.astype(jnp.float32)[:, :, None, None, :]
    r = jax.nn.sigmoid(gates[:, 0])
    i = jax.nn.sigmoid(gates[:, 1])
    log_a = -LRU_C * r * jax.nn.softplus(-lam.astype(jnp.float32))[:, None, None, :]
    a = jnp.exp(log_a)
    bterm = jnp.sqrt(jnp.maximum(-jnp.expm1(2.0 * log_a), 0.0)) * (i * xf[None])
    h0f = h0.astype(jnp.float32)
    h_f = lru_scan(a[0], bterm[0], h0f[:, 0])
    h_b = jnp.flip(lru_scan(jnp.flip(a[1], axis=1), jnp.flip(bterm[1], axis=1), h0f[:, 1]), axis=1)
    y = (h_f + h_b) * jax.nn.gelu(gb.astype(jnp.float32))
    return y.astype(xb.dtype), h_f, h_b


def conformer_conv(u2, dw_w, dw_b, ln_g, ln_b, pw_w, pw_b):
    a, g = jnp.split(u2, 2, axis=-1)
    u = a * jax.nn.sigmoid(g)
    u = depthwise_conv(u, dw_w, dw_b, CM_CONV // 2, CM_CONV // 2)
    u = jax.nn.silu(layer_norm(u, ln_g, ln_b))
    return u @ pw_w + pw_b


def context_attention(q, k, v):
    b, t, nh, hd = q.shape
    scale = hd ** -0.5
    qb = jnp.moveaxis(q.reshape(b, t // Q_BLOCK, Q_BLOCK, nh, hd), 1, 0)

    def one_block(q_blk):
        s = jnp.einsum('bqhd,bkhd->bhqk', q_blk, k).astype(jnp.float32) * scale
        p = jax.nn.softmax(s, axis=-1).astype(v.dtype)
        return jnp.einsum('bhqk,bkhd->bqhd', p, v)

    o = lax.map(one_block, qb)
    return jnp.moveaxis(o, 0, 1).reshape(b, t, nh * hd)


def latent_neighbourhood_attention(q, k, v, ctx_k, ctx_v, rpb):
    b, t, nh, hd = q.shape
    rows = t // GRID_W
    kh = min(NA_KH, rows)
    scale = hd ** -0.5
    kg = k.reshape(b, rows, GRID_W, nh, hd)
    vg = v.reshape(b, rows, GRID_W, nh, hd)
    qg = jnp.moveaxis(q.reshape(b, rows, GRID_W, nh, hd), 1, 0)
    col = jnp.arange(GRID_W)
    col_start = jnp.clip(col - NA_KW // 2, 0, GRID_W - NA_KW)
    col_in = (col[None, :] >= col_start[:, None]) & (col[None, :] < col_start[:, None] + NA_KW)
    col_idx = jnp.clip(col[None, :] - col[:, None] + NA_KW - 1, 0, 2 * NA_KW - 2)
    n_nb = kh * GRID_W

    def one_row(args):
        r, q_r = args
        r0 = jnp.clip(r - kh // 2, 0, rows - kh)
        k_r = lax.dynamic_slice_in_dim(kg, r0, kh, axis=1)
        v_r = lax.dynamic_slice_in_dim(vg, r0, kh, axis=1)
        row_idx = r0 + jnp.arange(kh) - r + NA_KH - 1
        bias = rpb[:, row_idx][:, :, col_idx].transpose(0, 2, 1, 3).astype(jnp.float32)
        s_nb = jnp.einsum('bqhd,bjkhd->bhqjk', q_r, k_r).astype(jnp.float32) * scale + bias[None]
        s_nb = jnp.where(col_in[None, None, :, None, :], s_nb, NEG_INF)
        s_ctx = jnp.einsum('bqhd,blhd->bhql', q_r, ctx_k).astype(jnp.float32) * scale
        s = jnp.concatenate([s_nb.reshape(b, nh, GRID_W, n_nb), s_ctx], axis=-1)
        p = jax.nn.softmax(s, axis=-1).astype(v.dtype)
        p_nb = p[..., :n_nb].reshape(b, nh, GRID_W, kh, GRID_W)
        return (jnp.einsum('bhqjk,bjkhd->bqhd', p_nb, v_r)
                + jnp.einsum('bhql,blhd->bqhd', p[..., n_nb:], ctx_v))

    o = lax.map(one_row, (jnp.arange(rows), qg))
    return jnp.moveaxis(o, 0, 1).reshape(b, t, nh * hd)


def mixer_heads(h, w_in, pool_w, pool_scale, lru_conv_w, lru_conv_b, lru_gate_w, lru_gate_b,
                lru_lambda, cm_dw_w, cm_dw_b, cm_ln_g, cm_ln_b, cm_pw_w, cm_pw_b, w_out, h0, attend):
    b, t, _ = h.shape
    u = h @ w_in
    y_pool = pool_mixer(u[..., P_POOL:P_LRU_X], pool_w, pool_scale)
    y_lru, h_f, h_b = rg_lru_mixer(u[..., P_LRU_X:P_LRU_G], u[..., P_LRU_G:P_CM], lru_conv_w, lru_conv_b,
                                   lru_gate_w, lru_gate_b, lru_lambda, h0)
    y_cm = conformer_conv(u[..., P_CM:P_QKV], cm_dw_w, cm_dw_b, cm_ln_g, cm_ln_b, cm_pw_w, cm_pw_b)
    qkv = u[..., P_QKV:D_IN].reshape(b, t, 3, NA_HEADS, NA_HEAD_DIM)
    q, k, v = qkv[:, :, 0], qkv[:, :, 1], qkv[:, :, 2]
    y_att = attend(q, k, v)
    out = jnp.concatenate([y_pool, y_lru, y_cm, y_att], axis=-1) @ w_out
    return out, k, v, h_f, h_b


def ffn_sublayer(x, shift, scale, gate, g_pre, g_post, w1, w3, w2):
    h = modulate(rms_norm(x, g_pre), shift, scale)
    y = (jax.nn.silu(h @ w1) * (h @ w3)) @ w2
    return x + gate * rms_norm(y, g_post)


def setup_inputs(seed: int = 0) -> dict:
    key = jax.random.key(seed)
    ks = jax.random.split(key, 40)
    f32 = jnp.float32
    nrm = lambda k, shape, s: jax.random.normal(k, shape, f32) * s
    D = D_MODEL
    return {
        'x_prompt': nrm(ks[0], (BATCH, SEQ, D), 1.0),
        'x_sample': nrm(ks[1], (DEC_BATCH, DEC_SEQ, D), 1.0),
        'cache_k': nrm(ks[2], (DEC_BATCH, DEPTH, PAST_LEN, NA_HEADS, NA_HEAD_DIM), 1.0),
        'cache_v': nrm(ks[3], (DEC_BATCH, DEPTH, PAST_LEN, NA_HEADS, NA_HEAD_DIM), 1.0),
        'state_lru': nrm(ks[4], (DEC_BATCH, DEPTH, 2, LRU_W), 0.5),
        'c': nrm(ks[5], (DEC_BATCH, D), 1.0),
        'c_ctx': nrm(ks[6], (D,), 1.0),
        'w_ada': nrm(ks[7], (DEPTH, D, N_MOD * D), 0.5 * D ** -0.5),
        'b_ada': nrm(ks[8], (DEPTH, N_MOD * D), 0.02),
        'g_pre_mix': 1.0 + nrm(ks[9], (DEPTH, D), 0.05),
        'g_post_mix': 1.0 + nrm(ks[10], (DEPTH, D), 0.05),
        'g_pre_ffn': 1.0 + nrm(ks[11], (DEPTH, D), 0.05),
        'g_post_ffn': 1.0 + nrm(ks[12], (DEPTH, D), 0.05),
        'w_in': nrm(ks[13], (DEPTH, D, D_IN), D ** -0.5),
        'pool_w': nrm(ks[14], (DEPTH, len(POOL_WINDOWS), POOL_CH, POOL_CH), POOL_CH ** -0.5),
        'pool_scale': 1.0 + nrm(ks[15], (DEPTH, GROUP_W), 0.05),
        'lru_conv_w': nrm(ks[16], (DEPTH, LRU_CONV, LRU_W), 0.5),
        'lru_conv_b': nrm(ks[17], (DEPTH, LRU_W), 0.02),
        'lru_gate_w': nrm(ks[18], (DEPTH, 2, 2, LRU_BLOCKS, LRU_BLOCK_W, LRU_BLOCK_W), LRU_BLOCK_W ** -0.5),
        'lru_gate_b': nrm(ks[19], (DEPTH, 2, 2, LRU_W), 0.02),
        'lru_lambda': jax.random.uniform(ks[20], (DEPTH, 2, LRU_W), f32, 4.0, 9.0),
        'cm_dw_w': nrm(ks[21], (DEPTH, CM_CONV, CM_W), CM_CONV ** -0.5),
        'cm_dw_b': nrm(ks[22], (DEPTH, CM_W), 0.02),
        'cm_ln_g': 1.0 + nrm(ks[23], (DEPTH, CM_W), 0.05),
        'cm_ln_b': nrm(ks[24], (DEPTH, CM_W), 0.02),
        'cm_pw_w': nrm(ks[25], (DEPTH, CM_W, CM_W), CM_W ** -0.5),
        'cm_pw_b': nrm(ks[26], (DEPTH, CM_W), 0.02),
        'na_rpb': nrm(ks[27], (DEPTH, NA_HEADS, 2 * NA_KH - 1, 2 * NA_KW - 1), 0.1),
        'w_out': nrm(ks[28], (DEPTH, MIX_W, D), MIX_W ** -0.5),
        'ffn_w1': nrm(ks[29], (DEPTH, D, D_FF), D ** -0.5),
        'ffn_w3': nrm(ks[30], (DEPTH, D, D_FF), D ** -0.5),
        'ffn_w2': nrm(ks[31], (DEPTH, D_FF, D), D_FF ** -0.5),
    }


def reference(x_prompt, x_sample, cache_k, cache_v, state_lru, c, c_ctx, w_ada, b_ada,
              g_pre_mix, g_post_mix, g_pre_ffn, g_post_ffn, w_in, pool_w, pool_scale,
              lru_conv_w, lru_conv_b, lru_gate_w, lru_gate_b, lru_lambda, cm_dw_w, cm_dw_b,
              cm_ln_g, cm_ln_b, cm_pw_w, cm_pw_b, na_rpb, w_out, ffn_w1, ffn_w3, ffn_w2):
    xp = x_prompt
    xs = x_sample
    ks_out, vs_out, st_out = [], [], []
    for l in range(DEPTH):
        mix_w = (w_in[l], pool_w[l], pool_scale[l], lru_conv_w[l], lru_conv_b[l], lru_gate_w[l],
                 lru_gate_b[l], lru_lambda[l], cm_dw_w[l], cm_dw_b[l], cm_ln_g[l], cm_ln_b[l],
                 cm_pw_w[l], cm_pw_b[l], w_out[l])
        ffn_w = (g_pre_ffn[l], g_post_ffn[l], ffn_w1[l], ffn_w3[l], ffn_w2[l])

        sh1, sc1, gt1, sh2, sc2, gt2 = modulation(c_ctx[None, :], w_ada[l], b_ada[l])
        hp = modulate(rms_norm(xp, g_pre_mix[l]), sh1, sc1)
        h0 = jnp.zeros((xp.shape[0], 2, LRU_W), jnp.float32)
        mix, k_c, v_c, hf_c, hb_c = mixer_heads(hp, *mix_w, h0=h0, attend=context_attention)
        xp = xp + gt1 * rms_norm(mix, g_post_mix[l])
        xp = ffn_sublayer(xp, sh2, sc2, gt2, *ffn_w)
        ks_out.append(k_c)
        vs_out.append(v_c)
        st_out.append(jnp.stack([hf_c[:, -1], hb_c[:, 0]], axis=1).astype(xp.dtype))

        sh1, sc1, gt1, sh2, sc2, gt2 = modulation(c, w_ada[l], b_ada[l])
        hs = modulate(rms_norm(xs, g_pre_mix[l]), sh1, sc1)
        attend_lat = functools.partial(latent_neighbourhood_attention, ctx_k=cache_k[:, l],
                                       ctx_v=cache_v[:, l], rpb=na_rpb[l])
        mix, _, _, _, _ = mixer_heads(hs, *mix_w, h0=state_lru[:, l], attend=attend_lat)
        xs = xs + gt1 * rms_norm(mix, g_post_mix[l])
        xs = ffn_sublayer(xs, sh2, sc2, gt2, *ffn_w)

    new_cache_k = jnp.stack(ks_out, axis=1)
    new_cache_v = jnp.stack(vs_out, axis=1)
    new_state_lru = jnp.stack(st_out, axis=1)
    return (xp, xs, new_cache_k, new_cache_v, new_state_lru)
```

```python
import numpy as np
from contextlib import ExitStack
import concourse.bass as bass
import concourse.mybir as mybir
from concourse.bass_utils import run_bass_kernel_spmd

F32 = mybir.dt.float32
BF16 = mybir.dt.bfloat16
AF = mybir.ActivationFunctionType
ALU = mybir.AluOpType

D = 2048
NCH = 16
DEPTH = 4
DIN = 4096
DFF = 5632
NF = 44
NMOD = 6
PAST = 512
NEG = -30000.0
EPS = 1e-6
NPSEQ = 4
LP = 256
LS = 2048
GRID_W = 64

PV_ROWS = {}
_lay = [
    [("b_ada", 96), ("g_pre_mix", 16), ("g_post_mix", 16)],
    [("g_pre_ffn", 16), ("g_post_ffn", 16), ("pool_scale", 4), ("lru_conv_w", 16), ("lru_conv_b", 4),
     ("lru_gate_b", 16), ("lru_lambda", 8), ("cm_dw_b", 4), ("cm_ln_g", 4), ("cm_ln_b", 4), ("cm_pw_b", 4)],
    [("cm_dw_w", 124)],
]
for _ti, _names in enumerate(_lay):
    _r = 0
    for _n, _k in _names:
        PV_ROWS[_n] = (_ti, _r, _k)
        _r += _k
    assert _r <= 128


class StopBuild(Exception):
    pass


class _CntEng:
    def __init__(self, eng):
        self._e = eng
        self.n = 0

    def __getattr__(self, name):
        f = getattr(self._e, name)
        if name in ('matmul', 'transpose'):
            def g(*a, **k):
                self.n += 1
                return f(*a, **k)
            return g
        return f


class Trk:
    NDMA = 10

    def __init__(self, nc, es):
        self.nc = nc
        self.eng = {'pe': _CntEng(nc.tensor), 'act': nc.scalar, 'dve': nc.vector, 'pool': nc.gpsimd, 'sp': nc.sync}
        self.phases = []
        self.sem = {}
        self.cnt = {}
        for k in self.eng:
            self.sem[k] = es.enter_context(nc.semaphore("s_" + k))
            self.cnt[k] = 0
        self.dq = {}
        for q in ('sp', 'pool'):
            keys = []
            for i in range(self.NDMA):
                k = "d_%s_%d" % (q, i)
                self.sem[k] = es.enter_context(nc.semaphore(k))
                self.cnt[k] = 0
                keys.append(k)
            self.dq[q] = [keys, 0]
        self.seen = {k: {} for k in self.eng}
        self.lw = {}
        self.rd = {}
        self.ninst = 0
        self.grp = {}
        self.stop_at = 0
        self.stopped = False

    def new(self, grp):
        l = self.grp.setdefault(grp, [])
        t = (grp, '#', len(l))
        l.append(t)
        return t

    def all(self, grp):
        return list(self.grp.get(grp, []))

    def _deps(self, e, reads, writes):
        deps = {}

        def add(ev, raw):
            k, v = ev
            if k == e and not raw:
                return
            if deps.get(k, 0) < v:
                deps[k] = v
        for t in reads:
            if t in self.lw:
                add(self.lw[t], True)
            if isinstance(t, tuple) and t[0] == 'ps':
                for ev in self.rd.get(t, {}).items():
                    add(ev, False)
        for t in writes:
            if t in self.lw:
                add(self.lw[t], False)
            for ev in self.rd.get(t, {}).items():
                add(ev, False)
        return deps

    def _wait(self, e, deps):
        eng = self.eng[e]
        seen = self.seen[e]
        for k, v in deps.items():
            if seen.get(k, 0) < v:
                eng.wait_ge(self.sem[k], v)
                seen[k] = v
                self.ninst += 1

    def _commit(self, ev, reads, writes):
        k, v = ev
        for t in writes:
            self.lw[t] = ev
            self.rd[t] = {}
        for t in reads:
            d = self.rd.setdefault(t, {})
            if d.get(k, 0) < v:
                d[k] = v

    def op(self, e, emit, reads=(), writes=()):
        if self.stopped:
            return
        self._wait(e, self._deps(e, reads, writes))
        inst = emit(self.eng[e])
        self.cnt[e] += 1
        inst.then_inc(self.sem[e], 1)
        self.ninst += 1
        self._commit((e, self.cnt[e]), reads, writes)

    def dma(self, q, out, in_, reads=(), writes=(), **kw):
        if self.stopped:
            return
        keys, i = self.dq[q]
        k = keys[i % len(keys)]
        self.dq[q][1] = i + 1
        deps = self._deps(None, reads, writes)
        if self.cnt[k] > 0:
            deps[k] = max(deps.get(k, 0), self.cnt[k])
        self._wait(q, deps)
        self.eng[q].dma_start(out=out, in_=in_, **kw).then_inc(self.sem[k], 16)
        self.cnt[k] += 16
        self.ninst += 1
        self._commit((k, self.cnt[k]), reads, writes)

    def barrier(self):
        if self.stopped:
            return
        self.nbar = getattr(self, 'nbar', 0) + 1
        self.phases.append((self.nbar, self.eng['pe'].n))
        for e in self.eng:
            self._wait(e, {k: v for k, v in self.cnt.items() if v > 0 and k != e})
        self.lw = {}
        self.rd = {}
        self.grp = {}
        if self.stop_at and self.nbar >= self.stop_at:
            self.stopped = True

    def mark(self, n):
        import os
        if int(os.environ.get('STOPM', '0')) == n:
            self.stopped = True

    def finish(self):
        self.stopped = False
        self._wait('sp', {k: v for k, v in self.cnt.items() if v > 0 and k != 'sp'})


class Seg:
    def __init__(self, name, T, L, cond):
        self.name = name
        self.T = T
        self.L = L
        self.nseq = T // L
        self.cond = cond
        self.ntile = T // 512
        self.sample = (name == 'S')


def host_consts():
    c = {}
    ident = np.eye(128, dtype=np.float32)
    flip = ident[::-1].copy()
    ones = np.ones((128, 128), np.float32)
    c['cst'] = np.concatenate([ident, flip, ones], axis=1)
    er = np.ones((4, 16), np.float32)
    Lh = 64
    for gi, w in enumerate((2, 4, 8, 16)):
        for t in range(8):
            lo = max(t - w // 2, 0)
            hi = min(t - w // 2 + w, Lh)
            er[gi, t] = w / float(hi - lo)
        for i in range(8):
            t = Lh - 8 + i
            lo = max(t - w // 2, 0)
            hi = min(t - w // 2 + w, Lh)
            er[gi, 8 + i] = w / float(hi - lo)
    c['edger'] = np.broadcast_to(er.reshape(1, 64), (128, 64)).copy()
    bases = [-6, -4, -2, 0, 2, 4, 6, -4, 4]
    col = np.arange(64)
    cs = np.clip(col - 8, 0, 48)
    mk = np.zeros((128, 9, 128), np.float32)
    for ty, base in enumerate(bases):
        interior = ty >= 7
        for pr in range(2):
            for qr in range(2):
                dr = base + pr - qr
                rowok = abs(dr) <= 7 and ((not interior) or (-4 <= dr <= 3))
                for pp in range(64):
                    kc = 63 - pp
                    ok = rowok & (kc >= cs) & (kc < cs + 16)
                    mk[(1 - pr) * 64 + pp, ty, qr * 64:(qr + 1) * 64] = np.where(ok, 0.0, NEG)
    c['maskr'] = mk
    return c


RT_BASES = [-6, -4, -2, 0, 2, 4, 6, -4, 4]


def nb_config(qc):
    edge = {-6: 0, -4: 1, -2: 2, 0: 3, 2: 4, 4: 5, 6: 6}
    if qc == 0:
        return [(j, edge[2 * j]) for j in range(4)]
    if qc == 1:
        return [(j, edge[2 * j - 2]) for j in range(4)]
    if qc == 14:
        return [(12 + j, edge[2 * j - 4]) for j in range(4)]
    if qc == 15:
        return [(12 + j, edge[2 * j - 6]) for j in range(4)]
    out = []
    for j in range(5):
        b = 2 * j - 4
        ty = 7 if j == 0 else (8 if j == 4 else edge[b])
        out.append((qc - 2 + j, ty))
    return out


def build_program(nl=DEPTH, do_sample=True, do_prompt=True, stop_at=0):
    nc = bass.Bass("TRN2", target_bir_lowering=False)
    din = {}

    def dram_in(name, shape):
        din[name] = nc.dram_tensor(name, list(shape), F32, kind="ExternalInput").ap()
        return din[name]

    xp_in = dram_in("xp", [NPSEQ * LP, D])
    xs_in = dram_in("xs", [LS, D])
    ck_in = dram_in("ck", [DEPTH, PAST, 512])
    cv_in = dram_in("cv", [DEPTH, PAST, 512])
    st_in = dram_in("st", [DEPTH * 2 * 512])
    cond_in = dram_in("cond", [2, D])
    w_ada = dram_in("w_ada", [DEPTH, D, NMOD * D])
    vec_in = {}
    for nm, n in [("b_ada", NMOD * D), ("g_pre_mix", D), ("g_post_mix", D), ("g_pre_ffn", D), ("g_post_ffn", D),
                  ("pool_scale", 512), ("lru_conv_w", 4 * 512), ("lru_conv_b", 512), ("lru_gate_b", 4 * 512),
                  ("lru_lambda", 2 * 512), ("cm_dw_w", 31 * 512), ("cm_dw_b", 512), ("cm_ln_g", 512),
                  ("cm_ln_b", 512), ("cm_pw_b", 512)]:
        vec_in[nm] = dram_in(nm, [DEPTH, n])
    w_in = dram_in("w_in", [DEPTH, D, DIN])
    pool_w = dram_in("pool_w", [DEPTH, 4, 128, 128])
    gate_w = dram_in("lru_gate_w", [DEPTH, 2, 2, 8, 64, 64])
    pw_w = dram_in("cm_pw_w", [DEPTH, 512, 512])
    rpb_in = dram_in("na_rpb", [DEPTH, 120, 31])
    w_out = dram_in("w_out", [DEPTH, D, D])
    w1_in = dram_in("ffn_w1", [DEPTH, D, DFF])
    w3_in = dram_in("ffn_w3", [DEPTH, D, DFF])
    w2_in = dram_in("ffn_w2", [DEPTH, DFF, D])
    cst_in = dram_in("cst", [128, 384])
    edger_in = dram_in("edger", [128, 64])
    maskr_in = dram_in("maskr", [128, 9, 128])

    yp_out = nc.dram_tensor("yp", [NPSEQ * LP, D], F32, kind="ExternalOutput").ap()
    ys_out = nc.dram_tensor("ys", [LS, D], F32, kind="ExternalOutput").ap()
    nk_out = nc.dram_tensor("nk", [NPSEQ, DEPTH, LP, 512], F32, kind="ExternalOutput").ap()
    nv_out = nc.dram_tensor("nv", [NPSEQ, DEPTH, LP, 512], F32, kind="ExternalOutput").ap()
    nst_out = nc.dram_tensor("nst", [NPSEQ, DEPTH, 2, 512], F32, kind="ExternalOutput").ap()

    segP = Seg('P', NPSEQ * LP, LP, 0)
    segS = Seg('S', LS, LS, 1)
    segs = ([segP] if do_prompt else []) + ([segS] if do_sample else [])
    xT = {s.name: nc.dram_tensor("xT_" + s.name, [D, s.T], F32, kind="Internal").ap() for s in (segP, segS)}
    yT = {s.name: nc.dram_tensor("yT_" + s.name, [D, s.T], BF16, kind="Internal").ap() for s in (segP, segS)}
    oT = nc.dram_tensor("oT", [D, 1024], F32, kind="Internal").ap()
    zpd = nc.dram_tensor("zpd", [120, 128], F32, kind="Internal").ap()
    x_in = {'P': xp_in, 'S': xs_in}
    y_out = {'P': yp_out, 'S': ys_out}

    es = ExitStack()
    with es:
        T = Trk(nc, es)
        T.stop_at = stop_at
        uid = [0]
        try:
            _emit_all(locals())
        except StopBuild:
            pass
        T.finish()
    nc._trk_ninst = T.ninst
    nc._trk_phases = T.phases
    return nc


def _emit_all(env):
    nc = env['nc']; T = env['T']; uid = env['uid']; es = env['es']
    segs = env['segs']; nl = env['nl']
    if True:
        xp_in = env['xp_in']; xs_in = env['xs_in']; ck_in = env['ck_in']; cv_in = env['cv_in']; st_in = env['st_in']
        cond_in = env['cond_in']; w_ada = env['w_ada']; vec_in = env['vec_in']; w_in = env['w_in']; pool_w = env['pool_w']
        gate_w = env['gate_w']; pw_w = env['pw_w']; rpb_in = env['rpb_in']; w_out = env['w_out']; w1_in = env['w1_in']
        w3_in = env['w3_in']; w2_in = env['w2_in']; cst_in = env['cst_in']; edger_in = env['edger_in']; maskr_in = env['maskr_in']
        yp_out = env['yp_out']; ys_out = env['ys_out']; nk_out = env['nk_out']; nv_out = env['nv_out']; nst_out = env['nst_out']
        segP = env['segP']; segS = env['segS']; xT = env['xT']; yT = env['yT']; oT = env['oT']; zpd = env['zpd']
        x_in = env['x_in']; y_out = env['y_out']

        def sb(stack, name, shape, dt):
            uid[0] += 1
            return stack.enter_context(nc.sbuf_tensor("%s_%d" % (name, uid[0]), list(shape), dt))

        psb = [es.enter_context(nc.psum_tensor("psb%d" % i, [128, 512], F32)) for i in range(8)]
        psi = [0]

        def bank():
            i = psi[0] % 6
            psi[0] += 1
            return psb[i], ('ps', i)

        cst = sb(es, "cst", [128, 384], F32)
        T.dma('sp', cst[:], cst_in[:, :], writes=['cst'])
        ident = cst[:, 0:128]
        ones_f = cst[:, 256:384]
        cstb = sb(es, "cstb", [128, 384], BF16)
        T.op('dve', lambda e: e.tensor_copy(out=cstb[:], in_=cst[:]), reads=['cst'], writes=['cstb'])
        ident_b = cstb[:, 0:128]
        flip_b = cstb[:, 128:256]
        epst = sb(es, "epst", [128, 1], F32)
        T.op('pool', lambda e: e.memset(epst[:], EPS), writes=['epst'])
        edger = sb(es, "edger", [128, 4, 16], F32)
        T.dma('sp', edger[:].rearrange("p a b -> p (a b)"), edger_in[:, :], writes=['edger'])
        pv = sb(es, "pv", [128, 3, 128], F32)
        mod = sb(es, "mod", [128, 96, 2], F32)
        modv = sb(es, "modv", [128, 4, 16, 2], F32)
        condT = sb(es, "condT", [128, 16, 2], BF16)
        h0all = sb(es, "h0all", [128, 32], F32)
        nsp = sb(es, "nsp", [128, 2, 8], F32)
        gw = sb(es, "gw", [128, 16, 128], BF16)

        def pvc(name, idx):
            ti, r0, n = PV_ROWS[name]
            return pv[:, ti, r0 + idx:r0 + idx + 1]

        def pvr(name):
            ti, r0, n = PV_ROWS[name]
            return pv[:, ti, r0:r0 + n]

        CONST_R = ['cst', 'cstb', 'epst']

        def evac_engine(i):
            return 'act' if i % 2 == 0 else 'dve'

        def copy_op(e, out, in_, reads, writes):
            if e == 'act':
                T.op('act', lambda g: g.activation(out=out, in_=in_, func=AF.Identity), reads=reads, writes=writes)
            else:
                T.op(e, lambda g: g.tensor_copy(out=out, in_=in_), reads=reads, writes=writes)

        with ExitStack() as ph:
            xin = [sb(ph, "xin", [128, D], F32) for _ in range(2)]
            xo = [sb(ph, "xo", [128, NCH, 128], F32) for _ in range(2)]
            it = 0
            for s in segs:
                xTv = xT[s.name].rearrange("(c p) t -> p c t", p=128)
                for tc in range(s.T // 128):
                    r = it % 2
                    it += 1
                    T.dma('sp', xin[r][:], x_in[s.name][tc * 128:(tc + 1) * 128, :], writes=[('xin', r)])
                    for g in range(4):
                        ps, pt = bank()

                        def emit(e, ps=ps, r=r, g=g):
                            ins = None
                            for j in range(4):
                                c = 4 * g + j
                                ins = e.transpose(out=ps[:, j * 128:(j + 1) * 128], in_=xin[r][:, c * 128:(c + 1) * 128],
                                                  identity=ident)
                            return ins
                        T.op('pe', emit, reads=[('xin', r), 'cst'], writes=[pt])
                        copy_op(evac_engine(g), xo[r][:, 4 * g:4 * g + 4, :],
                                ps[:, :].rearrange("p (a b) -> p a b", b=128), [pt], [('xo', r, g)])
                    T.dma('sp', xTv[:, :, tc * 128:(tc + 1) * 128], xo[r][:],
                          reads=[('xo', r, g) for g in range(4)], writes=[T.new('xT0')])
            cs = sb(ph, "cs", [2, D], F32)
            T.dma('sp', cs[:], cond_in[:, :], writes=['cs'])
            ps, pt = bank()

            def emit(e, ps=ps):
                ins = None
                for c in range(16):
                    ins = e.transpose(out=ps[:, c * 2:(c + 1) * 2], in_=cs[0:2, c * 128:(c + 1) * 128],
                                      identity=cst[0:2, 0:2])
                return ins
            T.op('pe', emit, reads=['cs', 'cst'], writes=[pt])
            T.op('act', lambda e: e.activation(out=condT[:].rearrange("p a b -> p (a b)"), in_=ps[:, 0:32], func=AF.Silu),
                 reads=[pt], writes=['condT'])
            sst = sb(ph, "sst", [32, 128], F32)
            T.dma('sp', sst[:], st_in.rearrange("(n p) -> n p", p=128), writes=['sst'])
            ps, pt = bank()
            T.op('pe', lambda e: e.transpose(out=ps[:, 0:32], in_=sst[:, :], identity=cst[0:32, 0:32]),
                 reads=['sst', 'cst'], writes=[pt])
            T.op('dve', lambda e: e.tensor_copy(out=h0all[:], in_=ps[:, 0:32]), reads=[pt], writes=['h0all'])
            T.barrier()

        def load_w(dst, src, tok):
            T.dma('pool', dst, src, writes=[tok])

        def norm_phase(stack, s, l, t0, ntile, gi, shi, hT, htok, NT=256, nsq=2):
            j = s.cond
            xTv = xT[s.name].rearrange("(c p) t -> p c t", p=128)
            xt = [sb(stack, "n_xt", [128, NCH, NT], F32) for _ in range(2)]
            sq = [sb(stack, "n_sq", [128, NCH, NT], F32) for _ in range(nsq)]
            rstd = [sb(stack, "n_rstd", [128, NT], F32) for _ in range(2)]
            npp = 512 // NT
            for pi in range(ntile * npp):
                ti = pi // npp
                tg = t0 + ti
                c0 = tg * 512 + (pi % npp) * NT
                h0 = ti * 512 + (pi % npp) * NT
                rx = pi % 2
                r = pi % nsq
                r2 = pi % 2
                for g4 in range(4):
                    T.dma('sp', xt[rx][:, 4 * g4:4 * g4 + 4, :], xTv[:, 4 * g4:4 * g4 + 4, c0:c0 + NT],
                          reads=[('xT', s.name, tg)], writes=[('n_xt', rx, g4)])
                xtoks = [('n_xt', rx, g4) for g4 in range(4)]
                T.op('act', lambda e, r=r, rx=rx: e.activation(out=sq[r][:], in_=xt[rx][:], func=AF.Square),
                     reads=xtoks, writes=[('n_sq', r)])
                ps, pt = bank()

                def emit(e, ps=ps, r=r):
                    ins = None
                    for c in range(NCH):
                        ins = e.matmul(ps[:, 0:NT], lhsT=ones_f, rhs=sq[r][:, c, :], start=(c == 0), stop=(c == NCH - 1))
                    return ins
                T.op('pe', emit, reads=[('n_sq', r), 'cst'], writes=[pt])
                T.op('act', lambda e, ps=ps, r2=r2: e.activation(out=rstd[r2][:], in_=ps[:, 0:NT], func=AF.Sqrt, bias=epst[:, 0:1],
                                                                 scale=1.0 / D), reads=[pt, 'epst'], writes=[('n_rstd', r2)])
                T.op('dve', lambda e, r2=r2: e.reciprocal(out=rstd[r2][:], in_=rstd[r2][:]),
                     reads=[('n_rstd', r2)], writes=[('n_rstd', r2)])
                T.op('dve', lambda e, r2=r2, r=r, rx=rx: e.tensor_tensor(out=sq[r][:], in0=xt[rx][:],
                                                             in1=rstd[r2][:, :].unsqueeze(1).to_broadcast([128, NCH, NT]),
                                                             op=ALU.mult),
                     reads=xtoks + [('n_rstd', r2)], writes=[('n_sq', r)])
                for c in range(NCH):
                    gsc = modv[:, gi, c, j:j + 1]
                    shc = mod[:, shi * 16 + c, j:j + 1]
                    dst = hT[:, c, h0:h0 + NT]
                    if c % 2 == 0:
                        T.op('act', lambda e, dst=dst, c=c, gsc=gsc, shc=shc, r=r: e.activation(
                            out=dst, in_=sq[r][:, c, :], func=AF.Identity, bias=shc, scale=gsc),
                            reads=[('n_sq', r), 'modv', 'mod'], writes=[(htok, ti)])
                    else:
                        T.op('dve', lambda e, dst=dst, c=c, gsc=gsc, shc=shc, r=r: e.tensor_scalar(
                            out=dst, in0=sq[r][:, c, :], scalar1=gsc, scalar2=shc, op0=ALU.mult, op1=ALU.add),
                            reads=[('n_sq', r), 'modv', 'mod'], writes=[(htok, ti)])

        def rstd_from_ps(ps, pt, dst, dtok, n=512, scale=1.0 / D):
            T.op('act', lambda e: e.activation(out=dst[:, 0:n], in_=ps[:, 0:n], func=AF.Sqrt, bias=epst[:, 0:1], scale=scale),
                 reads=[pt, 'epst'], writes=[dtok])
            T.op('dve', lambda e: e.reciprocal(out=dst[:, 0:n], in_=dst[:, 0:n]), reads=[dtok], writes=[dtok])

        for l in range(nl):
            with ExitStack() as ph:
                stg = sb(ph, "pstg", [128, 3, 128], F32)
                T.op('pool', lambda e: e.memset(stg[:], 0.0), writes=['pstg'])
                for nm, (ti, r0, n) in PV_ROWS.items():
                    T.dma('sp', stg[r0:r0 + n, ti, :], vec_in[nm][l].rearrange("(n p) -> n p", p=128),
                          reads=['pstg'], writes=[T.new('pstg_d')])
                for ti in range(3):
                    ps, pt = bank()
                    T.op('pe', lambda e, ps=ps, ti=ti: e.transpose(out=ps[:, 0:128], in_=stg[:, ti, :], identity=ident),
                         reads=['pstg', 'cst'] + T.all('pstg_d'), writes=[pt])
                    copy_op('dve', pv[:, ti, :], ps[:, 0:128], [pt], ['pv'])
                lam = pvr("lru_lambda")
                tmp = sb(ph, "lamtmp", [128, 8], F32)
                T.op('act', lambda e: e.activation(out=tmp[:], in_=lam, func=AF.Exp, scale=-1.0), reads=['pv'], writes=['lamtmp'])
                T.op('act', lambda e: e.activation(out=tmp[:], in_=tmp[:], func=AF.Ln, bias=1.0, scale=1.0),
                     reads=['lamtmp'], writes=['lamtmp'])
                T.op('dve', lambda e: e.tensor_scalar(out=nsp[:, 0, :], in0=tmp[:], scalar1=-8.0, scalar2=None, op0=ALU.mult),
                     reads=['lamtmp'], writes=['nsp'])
                T.op('dve', lambda e: e.tensor_scalar(out=nsp[:, 1, :], in0=tmp[:], scalar1=-16.0, scalar2=None, op0=ALU.mult),
                     reads=['lamtmp'], writes=['nsp'])
                T.op('pool', lambda e: e.memset(gw[:], 0.0), writes=['gw'])
                for d in range(2):
                    for g in range(2):
                        for k in range(8):
                            po = (k % 2) * 64
                            T.dma('pool', gw[po:po + 64, (d * 2 + g) * 4 + k // 2, po:po + 64], gate_w[l, d, g, k],
                                  reads=['gw'], writes=[T.new('gw_d')])
                wsl = [sb(ph, "wada", [128, 16, 1024], BF16) for _ in range(3)]
                wv = w_ada[l].rearrange("(k p) n -> p k n", p=128)
                psm, ptm = bank()
                for si in range(12):
                    r = si % 3
                    load_w(wsl[r][:], wv[:, :, si * 1024:(si + 1) * 1024], ('wada', r))

                    def emit(e, r=r, si=si):
                        ins = None
                        for jn in range(8):
                            n = si * 8 + jn
                            for k in range(16):
                                ins = e.matmul(psm[:, n * 2:(n + 1) * 2], lhsT=wsl[r][:, k, jn * 128:(jn + 1) * 128],
                                               rhs=condT[:, k, :], start=(k == 0), stop=(k == 15))
                        return ins
                    T.op('pe', emit, reads=[('wada', r), 'condT'], writes=[ptm])
                bada = pvr("b_ada")
                T.op('dve', lambda e: e.tensor_tensor(out=mod[:], in0=psm[:, 0:192].rearrange("p (a b) -> p a b", b=2),
                                                      in1=bada.unsqueeze(2).to_broadcast([128, 96, 2]), op=ALU.add),
                     reads=[ptm, 'pv'], writes=['mod'])
                for a, (sci, gname) in enumerate([(1, "g_pre_mix"), (2, "g_post_mix"), (4, "g_pre_ffn"), (5, "g_post_ffn")]):
                    gv = pvr(gname).unsqueeze(2).to_broadcast([128, 16, 2])
                    src = mod[:, sci * 16:(sci + 1) * 16, :]
                    if a % 2 == 0:
                        T.op('dve', lambda e, a=a, src=src, gv=gv: e.scalar_tensor_tensor(
                            out=modv[:, a, :, :], in0=src, scalar=1.0, in1=gv, op0=ALU.add, op1=ALU.mult),
                            reads=['mod', 'pv'], writes=['modv'])
                    else:
                        T.op('dve', lambda e, a=a, src=src, gv=gv: e.tensor_tensor(
                            out=modv[:, a, :, :], in0=src, in1=gv, op=ALU.mult), reads=['mod', 'pv'], writes=['modv'])
                T.barrier()

            for s in segs:
                j = s.cond
                Tn, L, nseq = s.T, s.L, s.nseq
                yTd = yT[s.name]
                spt = max(1, 512 // L)
                ntl = min(L, 512)

                def tview(buf3, padl, ti):
                    if L >= 512:
                        sq_ = (ti * 512) // L
                        off = (ti * 512) % L
                        return buf3[:, sq_, padl + off:padl + off + 512]
                    return buf3[:, ti * spt:(ti + 1) * spt, padl:padl + L]

                def pview(ps):
                    if L >= 512:
                        return ps[:, :]
                    return ps[:, :].rearrange("p (a b) -> p a b", b=L)

                with ExitStack() as ph:
                    hT = sb(ph, "hT", [128, NCH, Tn], BF16)
                    with ExitStack() as sub:
                        norm_phase(sub, s, l, 0, s.ntile, 0, 0, hT, 'hT')
                        T.barrier()
                    wslab = [sb(ph, "wslab", [128, 16, 512], BF16) for _ in range(2)]
                    wiv = w_in[l].rearrange("(k p) n -> p k n", p=128)
                    wcnt = [0]

                    def get_slab(si):
                        r = wcnt[0] % 2
                        wcnt[0] += 1
                        load_w(wslab[r][:], wiv[:, :, si * 512:(si + 1) * 512], ('wslab', r))
                        return wslab[r], ('wslab', r)

                    def proj_fm(slab, stok, jc, ti):
                        ps, pt = bank()

                        def emit(e):
                            ins = None
                            for k in range(16):
                                ins = e.matmul(ps[:, :], lhsT=slab[:, k, jc * 128:(jc + 1) * 128],
                                               rhs=hT[:, k, ti * 512:(ti + 1) * 512], start=(k == 0), stop=(k == 15))
                            return ins
                        T.op('pe', emit, reads=[stok, ('hT', ti)], writes=[pt])
                        return ps, pt

                    with ExitStack() as sub:
                        Lp = L + 16
                        up = sb(sub, "up", [128, nseq, Lp], F32)
                        lv = [sb(sub, "lv", [128, nseq, Lp], F32) for _ in range(2)]
                        pooled = sb(sub, "pooled", [128, Tn], BF16)
                        ych = sb(sub, "ychA", [128, Tn], BF16)
                        pwt = sb(sub, "poolw", [128, 4, 128], BF16)
                        load_w(pwt[:], pool_w[l].rearrange("g c d -> c g d"), 'poolw')
                        T.op('pool', lambda e: e.memset(up[:], 0.0), writes=['up'])
                        slab, stok = get_slab(0)
                        for g in range(4):
                            w = 2 << g
                            for ti in range(s.ntile):
                                ps, pt = proj_fm(slab, stok, g, ti)
                                copy_op(evac_engine(ti), tview(up, 8, ti), pview(ps), [pt], ['up'])
                            cur, ctok = up, 'up'
                            m = 1
                            lo, hi = 0, Lp
                            step = 0
                            while m < w:
                                dst = lv[step % 2]
                                dtok = ('lv', step % 2)
                                if m == 1:
                                    nlo, nhi = lo + 1, hi
                                    a0 = cur[:, :, nlo - 1:nhi - 1]
                                    a1 = cur[:, :, nlo:nhi]
                                else:
                                    h2 = m // 2
                                    nlo, nhi = lo + h2, hi - h2
                                    a0 = cur[:, :, nlo - h2:nhi - h2]
                                    a1 = cur[:, :, nlo + h2:nhi + h2]
                                T.op('dve', lambda e, dst=dst, a0=a0, a1=a1, nlo=nlo, nhi=nhi: e.tensor_tensor(
                                    out=dst[:, :, nlo:nhi], in0=a0, in1=a1, op=ALU.add), reads=[ctok], writes=[dtok])
                                cur, ctok = dst, dtok
                                lo, hi = nlo, nhi
                                m *= 2
                                step += 1
                            T.op('dve', lambda e, cur=cur, w=w: e.tensor_scalar(
                                out=cur[:, :, 8:8 + L], in0=cur[:, :, 8:8 + L], scalar1=1.0 / w, scalar2=None, op0=ALU.mult),
                                reads=[ctok], writes=[ctok])
                            T.op('dve', lambda e, cur=cur, g=g: e.tensor_tensor(
                                out=cur[:, :, 8:16], in0=cur[:, :, 8:16],
                                in1=edger[:, g, 0:8].unsqueeze(1).to_broadcast([128, nseq, 8]), op=ALU.mult),
                                reads=[ctok, 'edger'], writes=[ctok])
                            T.op('dve', lambda e, cur=cur, g=g: e.tensor_tensor(
                                out=cur[:, :, L:L + 8], in0=cur[:, :, L:L + 8],
                                in1=edger[:, g, 8:16].unsqueeze(1).to_broadcast([128, nseq, 8]), op=ALU.mult),
                                reads=[ctok, 'edger'], writes=[ctok])
                            T.op('dve', lambda e, cur=cur: e.tensor_tensor(
                                out=pooled[:, :].rearrange("p (a b) -> p a b", b=L), in0=cur[:, :, 8:8 + L],
                                in1=up[:, :, 8:8 + L], op=ALU.subtract), reads=[ctok, 'up'], writes=['pooled'])
                            for ti in range(s.ntile):
                                ps, pt = bank()
                                T.op('pe', lambda e, ps=ps, g=g, ti=ti: e.matmul(
                                    ps[:, :], lhsT=pwt[:, g, :], rhs=pooled[:, ti * 512:(ti + 1) * 512], start=True, stop=True),
                                    reads=['poolw', 'pooled'], writes=[pt])
                                T.op('act', lambda e, ps=ps, g=g, ti=ti: e.activation(
                                    out=ych[:, ti * 512:(ti + 1) * 512], in_=ps[:, :], func=AF.Identity,
                                    scale=pvc("pool_scale", g)), reads=[pt, 'pv'], writes=['ychA'])
                            T.dma('sp', yTd[g * 128:(g + 1) * 128, :], ych[:, :], reads=['ychA'], writes=[T.new('yT_' + s.name)])
                        T.barrier()

                    with ExitStack() as sub:
                        xbp = sb(sub, "xbp", [128, nseq, L + 3], F32)
                        xf = sb(sub, "xf", [128, nseq, L], F32)
                        xfb = sb(sub, "xfb", [128, Tn], BF16)
                        G = [sb(sub, "G%d" % i, [128, nseq, L], F32) for i in range(4)]
                        om = sb(sub, "om", [128, nseq, L], F32)
                        hf = sb(sub, "hf", [128, nseq, L], F32)
                        gg = sb(sub, "gg", [128, nseq, L], F32)
                        ych = sb(sub, "ychB", [128, Tn], BF16)
                        T.op('pool', lambda e: e.memset(xbp[:], 0.0), writes=['xbp'])
                        slab_x, stx = get_slab(1)
                        slab_g, stg_ = get_slab(2)

                        def flat(b):
                            return b[:].rearrange("p a b -> p (a b)")
                        for cb in range(4):
                            for ti in range(s.ntile):
                                ps, pt = proj_fm(slab_x, stx, cb, ti)
                                copy_op(evac_engine(ti), tview(xbp, 2, ti), pview(ps), [pt], ['xbp'])
                            for ti in range(s.ntile):
                                ps, pt = proj_fm(slab_g, stg_, cb, ti)
                                gv = tview(gg, 0, ti)
                                ov = tview(om, 0, ti)
                                T.op('act', lambda e, ov=ov, ps=ps: e.activation(out=ov, in_=pview(ps), func=AF.Square),
                                     reads=[pt], writes=['om'])
                                T.op('dve', lambda e, ov=ov: e.tensor_scalar(out=ov, in0=ov, scalar1=0.044715, scalar2=1.0,
                                                                            op0=ALU.mult, op1=ALU.add), reads=['om'], writes=['om'])
                                T.op('dve', lambda e, ov=ov, ps=ps: e.tensor_tensor(out=ov, in0=ov, in1=pview(ps), op=ALU.mult),
                                     reads=['om', pt], writes=['om'])
                                T.op('act', lambda e, ov=ov: e.activation(out=ov, in_=ov, func=AF.Sigmoid, scale=1.5957691216057308),
                                     reads=['om'], writes=['om'])
                                T.op('dve', lambda e, ov=ov, gv=gv, ps=ps: e.tensor_tensor(out=gv, in0=ov, in1=pview(ps), op=ALU.mult),
                                     reads=['om', pt], writes=['gg'])
                            T.op('dve', lambda e, cb=cb: e.tensor_scalar(
                                out=xf[:], in0=xbp[:, :, 0:L], scalar1=pvc("lru_conv_w", 0 * 4 + cb), scalar2=pvc("lru_conv_b", cb),
                                op0=ALU.mult, op1=ALU.add), reads=['xbp', 'pv'], writes=['xf'])
                            for jt in range(1, 4):
                                T.op('dve', lambda e, cb=cb, jt=jt: e.scalar_tensor_tensor(
                                    out=xf[:], in0=xbp[:, :, jt:jt + L], scalar=pvc("lru_conv_w", jt * 4 + cb), in1=xf[:],
                                    op0=ALU.mult, op1=ALU.add), reads=['xbp', 'xf', 'pv'], writes=['xf'])
                            T.op('act', lambda e: e.activation(out=xfb[:, :], in_=flat(xf), func=AF.Identity),
                                 reads=['xf'], writes=['xfb'])
                            for dg in range(4):
                                for ti in range(s.ntile):
                                    ps, pt = bank()
                                    T.op('pe', lambda e, ps=ps, dg=dg, ti=ti, cb=cb: e.matmul(
                                        ps[:, :], lhsT=gw[:, dg * 4 + cb, :], rhs=xfb[:, ti * 512:(ti + 1) * 512],
                                        start=True, stop=True), reads=['gw', 'xfb'], writes=[pt])
                                    T.op('act', lambda e, ps=ps, dg=dg, ti=ti, cb=cb: e.activation(
                                        out=tview(G[dg], 0, ti), in_=pview(ps), func=AF.Sigmoid,
                                        bias=pvc("lru_gate_b", dg * 4 + cb), scale=1.0), reads=[pt, 'pv'], writes=[('G', dg)])
                            for d in range(2):
                                R_, I_ = G[d * 2], G[d * 2 + 1]
                                rt, it_ = ('G', d * 2), ('G', d * 2 + 1)
                                T.op('act', lambda e, d=d, cb=cb, R_=R_: e.activation(
                                    out=om[:], in_=R_[:], func=AF.Exp, scale=nsp[:, 1, d * 4 + cb:d * 4 + cb + 1]),
                                    reads=[rt, 'nsp'], writes=['om'])
                                T.op('act', lambda e, d=d, cb=cb, R_=R_: e.activation(
                                    out=R_[:], in_=R_[:], func=AF.Exp, scale=nsp[:, 0, d * 4 + cb:d * 4 + cb + 1]),
                                    reads=[rt, 'nsp'], writes=[rt])
                                T.op('dve', lambda e: e.tensor_scalar(out=om[:], in0=om[:], scalar1=-1.0, scalar2=1.0,
                                                                      op0=ALU.mult, op1=ALU.add), reads=['om'], writes=['om'])
                                T.op('dve', lambda e: e.tensor_scalar(out=om[:], in0=om[:], scalar1=1e-30, scalar2=None,
                                                                      op0=ALU.max), reads=['om'], writes=['om'])
                                T.op('act', lambda e: e.activation(out=om[:], in_=om[:], func=AF.Sqrt), reads=['om'], writes=['om'])
                                T.op('dve', lambda e, I_=I_: e.tensor_tensor(out=I_[:], in0=I_[:], in1=om[:], op=ALU.mult),
                                     reads=[it_, 'om'], writes=[it_])
                                T.op('dve', lambda e, I_=I_: e.tensor_tensor(out=I_[:], in0=I_[:], in1=xf[:], op=ALU.mult),
                                     reads=[it_, 'xf'], writes=[it_])
                                dst = hf if d == 0 else om
                                dtok = 'hf' if d == 0 else 'om'
                                for sq_ in range(nseq):
                                    if s.sample:
                                        col = l * 8 + d * 4 + cb
                                        init = h0all[:, col:col + 1]
                                    else:
                                        init = 0.0
                                    if d == 0:
                                        T.op('dve', lambda e, sq_=sq_, init=init, R_=R_, I_=I_, dst=dst: e.tensor_tensor_scan(
                                            out=dst[:, sq_, :], data0=R_[:, sq_, :], data1=I_[:, sq_, :], initial=init,
                                            op0=ALU.mult, op1=ALU.add), reads=[rt, it_, 'h0all'], writes=[dtok])
                                    else:
                                        T.op('dve', lambda e, sq_=sq_, init=init, R_=R_, I_=I_, dst=dst: e.tensor_tensor_scan(
                                            out=dst[:, sq_, ::-1], data0=R_[:, sq_, ::-1], data1=I_[:, sq_, ::-1], initial=init,
                                            op0=ALU.mult, op1=ALU.add), reads=[rt, it_, 'h0all'], writes=[dtok])
                                if not s.sample:
                                    src = hf[:, :, L - 1] if d == 0 else om[:, :, 0]
                                    dsto = bass.AP(nst_out.tensor, l * 1024 + d * 512 + cb * 128, [[1, 128], [DEPTH * 1024, nseq]])
                                    T.dma('sp', dsto, src, reads=[dtok], writes=[], allow_slow_non_contiguous=True)
                            T.op('dve', lambda e: e.tensor_tensor(out=hf[:], in0=hf[:], in1=om[:], op=ALU.add),
                                 reads=['hf', 'om'], writes=['hf'])
                            T.op('dve', lambda e: e.tensor_tensor(out=ych[:, :], in0=flat(hf), in1=flat(gg), op=ALU.mult),
                                 reads=['hf', 'gg'], writes=['ychB'])
                            T.dma('sp', yTd[(4 + cb) * 128:(5 + cb) * 128, :], ych[:, :], reads=['ychB'], writes=[T.new('yT_' + s.name)])
                        T.barrier()

                    with ExitStack() as sub:
                        cpad = sb(sub, "cpad", [128, 4, nseq, L + 30], BF16)
                        dgm = sb(sub, "dgm", [128, 124, 128], BF16)
                        sg = [sb(sub, "sg", [128, 512], F32) for _ in range(2)]
                        NP_ = 256
                        cvt_r = [sb(sub, "cvt", [128, 4, NP_], F32) for _ in range(2)]
                        sqt = sb(sub, "sqt", [128, 4, NP_], F32)
                        mt_r = [sb(sub, "mt", [128, NP_], F32) for _ in range(2)]
                        m2_r = [sb(sub, "m2", [128, NP_], F32) for _ in range(2)]
                        rs_r = [sb(sub, "rsC", [128, NP_], F32) for _ in range(2)]
                        sl_r = [sb(sub, "sl", [128, 4, NP_], BF16) for _ in range(2)]
                        ychp = [sb(sub, "ychC", [128, 4, NP_], BF16) for _ in range(2)]
                        pww = sb(sub, "pww", [128, 4, 512], BF16)
                        load_w(pww[:], pw_w[l].rearrange("(c p) e -> p c e", p=128), 'pww')
                        T.op('pool', lambda e: e.memset(cpad[:], 0.0), writes=['cpad'])
                        dww = pvr("cm_dw_w")
                        for idx in range(124):
                            T.op('pool' if idx % 2 else 'dve', lambda e, idx=idx: e.tensor_scalar(
                                out=dgm[:, idx, :], in0=ident_b, scalar1=dww[:, idx:idx + 1], scalar2=None, op0=ALU.mult),
                                reads=['cstb', 'pv'], writes=['dgm'])
                        slab_a, sta = get_slab(3)
                        slab_g, stg_ = get_slab(4)
                        for cc in range(4):
                            for ti in range(s.ntile):
                                psa, pta = proj_fm(slab_a, sta, cc, ti)
                                psg, ptg = proj_fm(slab_g, stg_, cc, ti)
                                r = ti % 2
                                T.op('act', lambda e, r=r, psg=psg: e.activation(out=sg[r][:], in_=psg[:, :], func=AF.Sigmoid),
                                     reads=[ptg], writes=[('sg', r)])
                                sgv = sg[r][:, :] if L >= 512 else sg[r][:, :].rearrange("p (a b) -> p a b", b=L)
                                T.op('dve', lambda e, cc=cc, ti=ti, psa=psa, sgv=sgv: e.tensor_tensor(
                                    out=tview(cpad[:, cc], 15, ti), in0=pview(psa), in1=sgv, op=ALU.mult),
                                    reads=[pta, ('sg', r)], writes=['cpad'])
                        pieces = []
                        for sq_ in range(nseq):
                            for off in range(0, L, NP_):
                                pieces.append((sq_, off))
                        n = NP_

                        def cm_s1(pci):
                            sq_, off = pieces[pci]
                            rr = pci % 2
                            cvt = cvt_r[rr]
                            for cc in range(4):
                                ps, pt = bank()

                                def emit(e, ps=ps, cc=cc, sq_=sq_, off=off):
                                    ins = None
                                    for jt in range(31):
                                        ins = e.matmul(ps[:, 0:n], lhsT=dgm[:, jt * 4 + cc, :],
                                                       rhs=cpad[:, cc, sq_, off + jt:off + jt + n], start=(jt == 0), stop=(jt == 30))
                                    return ins
                                T.op('pe', emit, reads=['dgm', 'cpad'], writes=[pt])
                                T.op('act', lambda e, ps=ps, cc=cc, cvt=cvt: e.activation(
                                    out=cvt[:, cc, 0:n], in_=ps[:, 0:n], func=AF.Identity, bias=pvc("cm_dw_b", cc), scale=1.0),
                                    reads=[pt, 'pv'], writes=[('cvt', rr)])

                        def cm_s2(pci):
                            rr = pci % 2
                            cvt, mt, m2, rs, sl = cvt_r[rr], mt_r[rr], m2_r[rr], rs_r[rr], sl_r[rr]
                            ct, mtk, m2k, rsk, slk = ('cvt', rr), ('mt', rr), ('m2', rr), ('rsC', rr), ('sl', rr)
                            T.op('dve', lambda e: e.tensor_tensor(out=sqt[:, :, 0:n], in0=cvt[:, :, 0:n], in1=cvt[:, :, 0:n],
                                                                  op=ALU.mult), reads=[ct], writes=['sqt'])
                            psm_, ptm_ = bank()
                            pss_, pts_ = bank()

                            def emit(e):
                                ins = None
                                for cc in range(4):
                                    ins = e.matmul(psm_[:, 0:n], lhsT=ones_f, rhs=cvt[:, cc, 0:n], start=(cc == 0), stop=(cc == 3))
                                return ins
                            T.op('pe', emit, reads=[ct, 'cst'], writes=[ptm_])

                            def emit(e):
                                ins = None
                                for cc in range(4):
                                    ins = e.matmul(pss_[:, 0:n], lhsT=ones_f, rhs=sqt[:, cc, 0:n], start=(cc == 0), stop=(cc == 3))
                                return ins
                            T.op('pe', emit, reads=['sqt', 'cst'], writes=[pts_])
                            T.op('act', lambda e: e.activation(out=mt[:, 0:n], in_=psm_[:, 0:n], func=AF.Identity,
                                                               scale=1.0 / 512), reads=[ptm_], writes=[mtk])
                            T.op('dve', lambda e: e.tensor_tensor(out=m2[:, 0:n], in0=mt[:, 0:n], in1=mt[:, 0:n], op=ALU.mult),
                                 reads=[mtk], writes=[m2k])
                            T.op('dve', lambda e: e.scalar_tensor_tensor(
                                out=m2[:, 0:n], in0=pss_[:, 0:n], scalar=1.0 / 512, in1=m2[:, 0:n], op0=ALU.mult, op1=ALU.subtract),
                                reads=[pts_, m2k], writes=[m2k])
                            T.op('act', lambda e: e.activation(out=rs[:, 0:n], in_=m2[:, 0:n], func=AF.Sqrt, bias=epst[:, 0:1],
                                                               scale=1.0), reads=[m2k, 'epst'], writes=[rsk])
                            T.op('dve', lambda e: e.reciprocal(out=rs[:, 0:n], in_=rs[:, 0:n]), reads=[rsk], writes=[rsk])
                            T.op('dve', lambda e: e.tensor_tensor(
                                out=cvt[:, :, 0:n], in0=cvt[:, :, 0:n], in1=mt[:, 0:n].unsqueeze(1).to_broadcast([128, 4, n]),
                                op=ALU.subtract), reads=[ct, mtk], writes=[ct])
                            T.op('dve', lambda e: e.tensor_tensor(
                                out=cvt[:, :, 0:n], in0=cvt[:, :, 0:n], in1=rs[:, 0:n].unsqueeze(1).to_broadcast([128, 4, n]),
                                op=ALU.mult), reads=[ct, rsk], writes=[ct])
                            for cc in range(4):
                                T.op('act', lambda e, cc=cc: e.activation(
                                    out=sl[:, cc, 0:n], in_=cvt[:, cc, 0:n], func=AF.Silu, bias=pvc("cm_ln_b", cc),
                                    scale=pvc("cm_ln_g", cc)), reads=[ct, 'pv'], writes=[slk])

                        def cm_s3(pci):
                            sq_, off = pieces[pci]
                            rr = pci % 2
                            sl = sl_r[rr]
                            slk = ('sl', rr)
                            t_abs = sq_ * L + off
                            ych = ychp[rr]
                            ytok = ('ychC', rr)
                            for ec in range(4):
                                ps, pt = bank()

                                def emit(e, ps=ps, ec=ec):
                                    ins = None
                                    for cc in range(4):
                                        ins = e.matmul(ps[:, 0:n], lhsT=pww[:, cc, ec * 128:(ec + 1) * 128], rhs=sl[:, cc, 0:n],
                                                       start=(cc == 0), stop=(cc == 3))
                                    return ins
                                T.op('pe', emit, reads=['pww', slk], writes=[pt])
                                T.op('act' if ec % 2 == 0 else 'dve', (lambda e, ps=ps, ec=ec: e.activation(
                                    out=ych[:, ec, 0:n], in_=ps[:, 0:n], func=AF.Identity, bias=pvc("cm_pw_b", ec), scale=1.0))
                                    if ec % 2 == 0 else (lambda e, ps=ps, ec=ec: e.tensor_scalar(
                                        out=ych[:, ec, 0:n], in0=ps[:, 0:n], scalar1=pvc("cm_pw_b", ec), scalar2=None,
                                        op0=ALU.add)), reads=[pt, 'pv'], writes=[ytok])
                            T.dma('sp', yTd[8 * 128:12 * 128, t_abs:t_abs + n].rearrange("(e p) t -> p e t", p=128), ych[:, :, 0:n],
                                  reads=[ytok], writes=[T.new('yT_' + s.name)])

                        cm_s1(0)
                        for pci in range(len(pieces)):
                            if pci + 1 < len(pieces):
                                cm_s1(pci + 1)
                            cm_s2(pci)
                            cm_s3(pci)
                        T.barrier()

                    with ExitStack() as sub:
                        nTk = Tn // 128
                        if s.sample:
                            Rt = sb(sub, "Rt", [128, 9, 8, 128], BF16)
                            ckT = sb(sub, "ckT", [128, 4, 512], BF16)
                            cvb = sb(sub, "cvb", [128, 4, 512], BF16)
                            T.dma('pool', cvb[:], cv_in[l].rearrange("(a p) f -> p a f", p=128), writes=['cvb'])
                            with ExitStack() as sub2:
                                ckf = sb(sub2, "ckf", [128, 4, 512], F32)
                                maskr = sb(sub2, "maskr", [128, 9, 128], F32)
                                rp = sb(sub2, "rp", [120, 31], F32)
                                zp = sb(sub2, "zp", [120, 128], F32)
                                HK = sb(sub2, "HK", [128, 120, 64], F32)
                                T.dma('sp', ckf[:], ck_in[l].rearrange("(a p) f -> p a f", p=128), writes=['ckf'])
                                T.dma('sp', maskr[:], maskr_in[:, :, :], writes=['maskr'])
                                for c in range(4):
                                    ps, pt = bank()

                                    def emit(e, ps=ps, c=c):
                                        ins = None
                                        for a_ in range(4):
                                            ins = e.transpose(out=ps[:, a_ * 128:(a_ + 1) * 128], in_=ckf[:, a_, c * 128:(c + 1) * 128],
                                                              identity=ident)
                                        return ins
                                    T.op('pe', emit, reads=['ckf', 'cst'], writes=[pt])
                                    copy_op('act', ckT[:, c, :], ps[:, :], [pt], ['ckT'])
                                T.dma('sp', rp[:], rpb_in[l], writes=['rp'])
                                T.op('pool', lambda e: e.memset(zp[:], 0.0), writes=['zp'])
                                T.op('dve', lambda e: e.tensor_copy(out=zp[:, 48:79], in_=rp[:, ::-1]), reads=['rp', 'zp'], writes=['zp'])
                                T.dma('sp', zpd[:, :], zp[:], reads=['zp'], writes=['zpd'])
                                for half in range(2):
                                    for q4 in range(4):
                                        src = bass.AP(zpd.tensor, q4 * 30 * 128, [[1, 64], [128, 30], [1, 64]])
                                        T.dma('sp', HK[half * 64:(half + 1) * 64, q4 * 30:(q4 + 1) * 30, :], src,
                                              reads=['zpd'], writes=[T.new('HK')])
                                HK4 = HK[:].rearrange("p (h r) q -> p h r q", r=15)
                                k_ = 0
                                for ty, base in enumerate(RT_BASES):
                                    for pr in range(2):
                                        for qr in range(2):
                                            dr = base + pr - qr
                                            pa = (1 - pr) * 64
                                            eng = 'dve' if k_ % 2 == 0 else 'pool'
                                            k_ += 1
                                            mv = maskr[pa:pa + 64, ty, qr * 64:(qr + 1) * 64].unsqueeze(1).to_broadcast([64, 8, 64])
                                            ov = Rt[pa:pa + 64, ty, :, qr * 64:(qr + 1) * 64]
                                            if abs(dr) <= 7:
                                                T.op(eng, lambda e, ov=ov, mv=mv, pa=pa, dr=dr: e.tensor_tensor(
                                                    out=ov, in0=HK4[pa:pa + 64, :, dr + 7, :], in1=mv, op=ALU.add),
                                                    reads=T.all('HK') + ['maskr'], writes=[('Rt', k_)])
                                            else:
                                                T.op(eng, lambda e, ov=ov, mv=mv: e.tensor_copy(out=ov, in_=mv),
                                                     reads=['maskr'], writes=[('Rt', k_)])
                                T.barrier()
                        qT = sb(sub, "qT", [128, 4, Tn], BF16)
                        kT = sb(sub, "kT", [128, 4, Tn], BF16)
                        Vt = sb(sub, "Vt", [128, nTk, 512], BF16)
                        stage = [sb(sub, "kvst", [128, 512], F32) for _ in range(2)]
                        stc = [0]
                        slab_q, stq = get_slab(5)
                        for c in range(4):
                            for ti in range(s.ntile):
                                ps, pt = proj_fm(slab_q, stq, c, ti)
                                T.op('act', lambda e, ps=ps, c=c, ti=ti: e.activation(
                                    out=qT[:, c, ti * 512:(ti + 1) * 512], in_=ps[:, :], func=AF.Identity, scale=0.125),
                                    reads=[pt], writes=['qT'])
                        slab_k, stk = get_slab(6)
                        for c in range(4):
                            for ti in range(s.ntile):
                                ps, pt = proj_fm(slab_k, stk, c, ti)
                                copy_op('dve', kT[:, c, ti * 512:(ti + 1) * 512], ps[:, :], [pt], ['kT'])
                        T.mark(1)

                        def proj_tm(slab, stok, tk):
                            ps, pt = bank()

                            def emit(e):
                                ins = None
                                for k in range(16):
                                    ins = e.matmul(ps[:, :], lhsT=hT[:, k, tk * 128:(tk + 1) * 128], rhs=slab[:, k, :],
                                                   start=(k == 0), stop=(k == 15))
                                return ins
                            T.op('pe', emit, reads=[stok, ('hT', tk // 4)], writes=[pt])
                            return ps, pt
                        if not s.sample:
                            for tk in range(nTk):
                                ps, pt = proj_tm(slab_k, stk, tk)
                                r = stc[0] % 2
                                stc[0] += 1
                                copy_op('act', stage[r][:], ps[:, :], [pt], [('kvst', r)])
                                T.dma('sp', nk_out[tk // 2, l, (tk % 2) * 128:(tk % 2 + 1) * 128, :], stage[r][:],
                                      reads=[('kvst', r)], writes=[])
                        T.mark(2)
                        slab_v, stv = get_slab(7)
                        for tk in range(nTk):
                            ps, pt = proj_tm(slab_v, stv, tk)
                            copy_op('dve', Vt[:, tk, :], ps[:, :], [pt], ['Vt'])
                            if not s.sample:
                                r = stc[0] % 2
                                stc[0] += 1
                                copy_op('act', stage[r][:], ps[:, :], [pt], [('kvst', r)])
                                T.dma('sp', nv_out[tk // 2, l, (tk % 2) * 128:(tk % 2 + 1) * 128, :], stage[r][:],
                                      reads=[('kvst', r)], writes=[])
                        T.mark(3)
                        pT = [sb(sub, "pT", [128, 9 * 128], BF16) for _ in range(2)]
                        rsd = [sb(sub, "rsd", [64, 256], F32) for _ in range(2)]
                        ones64 = cstb[:, 256:320]
                        if not s.sample:
                            ychs = [sb(sub, "ychD", [128, 4, 256], BF16) for _ in range(2)]
                            it2 = 0
                            for sq_ in range(nseq):
                                ych = ychs[sq_ % 2]
                                ytok = ('ychD', sq_ % 2)
                                for h in range(8):
                                    c, po = h // 2, (h % 2) * 64
                                    r = it2 % 2
                                    it2 += 1
                                    ps, pt = bank()

                                    def emit(e, ps=ps, c=c, po=po, sq_=sq_):
                                        ins = None
                                        for kc in range(2):
                                            ins = e.matmul(ps[:, kc * 256:(kc + 1) * 256],
                                                           lhsT=kT[po:po + 64, c, sq_ * 256 + kc * 128:sq_ * 256 + (kc + 1) * 128],
                                                           rhs=qT[po:po + 64, c, sq_ * 256:(sq_ + 1) * 256], start=True, stop=True)
                                        return ins
                                    T.op('pe', emit, reads=['kT', 'qT'], writes=[pt])
                                    T.op('act', lambda e, ps=ps, r=r: e.activation(out=pT[r][:, 0:512], in_=ps[:, :], func=AF.Exp),
                                         reads=[pt], writes=[('pT', r)])
                                    T.mark(4)
                                    pso, pto = bank()

                                    def emit(e, pso=pso, r=r, h=h, sq_=sq_):
                                        ins = None
                                        for kc in range(2):
                                            ins = e.matmul(pso[0:64, 0:256], lhsT=Vt[:, sq_ * 2 + kc, h * 64:(h + 1) * 64],
                                                           rhs=pT[r][:, kc * 256:(kc + 1) * 256], start=(kc == 0), stop=(kc == 1))
                                        for kc in range(2):
                                            ins = e.matmul(pso[0:64, 256:512], lhsT=ones64,
                                                           rhs=pT[r][:, kc * 256:(kc + 1) * 256], start=(kc == 0), stop=(kc == 1))
                                        return ins
                                    T.op('pe', emit, reads=['Vt', ('pT', r), 'cstb'], writes=[pto])
                                    T.mark(5)
                                    T.op('dve', lambda e, pso=pso, r=r: e.reciprocal(out=rsd[r][:, 0:256], in_=pso[0:64, 256:512]),
                                         reads=[pto], writes=[('rsd', r)])
                                    T.mark(6)
                                    T.op('dve', lambda e, pso=pso, r=r, c=c, po=po, ych=ych: e.tensor_tensor(
                                        out=ych[po:po + 64, c, :], in0=pso[0:64, 0:256],
                                        in1=rsd[r][:, 0:256], op=ALU.mult), reads=[pto, ('rsd', r)], writes=[ytok])
                                T.dma('sp', yTd[12 * 128:16 * 128, sq_ * 256:(sq_ + 1) * 256].rearrange("(e p) t -> p e t", p=128), ych[:],
                                      reads=[ytok], writes=[T.new('yT_' + s.name)])
                        else:
                            ychs = [sb(sub, "ychD", [128, 4, 128], BF16) for _ in range(2)]
                            rt_all = [('Rt', k_) for k_ in range(1, 37)]
                            items = [(qc, h) for qc in range(16) for h in range(8)]

                            def att_s1(ii):
                                qc, h = items[ii]
                                nb = nb_config(qc)
                                ntile_ = len(nb) + 4
                                c, po = h // 2, (h % 2) * 64
                                r = ii % 2
                                nbk = (ntile_ + 3) // 4
                                banks = [bank() for _ in range(nbk)]
                                qv = qT[po:po + 64, c, qc * 128:(qc + 1) * 128]
                                for bi, (ps, pt) in enumerate(banks):
                                    def emit(e, ps=ps, bi=bi):
                                        ins = None
                                        for i in range(bi * 4, min(ntile_, bi * 4 + 4)):
                                            o = ps[:, (i % 4) * 128:(i % 4 + 1) * 128]
                                            if i < len(nb):
                                                kc, ty = nb[i]
                                                e.matmul(o, lhsT=kT[po:po + 64, c, kc * 128:(kc + 1) * 128], rhs=qv, start=True, stop=False)
                                                ins = e.matmul(o, lhsT=flip_b, rhs=Rt[:, ty, h, :], start=False, stop=True)
                                            else:
                                                a_ = i - len(nb)
                                                ins = e.matmul(o, lhsT=ckT[po:po + 64, c, a_ * 128:(a_ + 1) * 128], rhs=qv, start=True, stop=True)
                                        return ins
                                    T.op('pe', emit, reads=['kT', 'qT', 'ckT', 'cstb'] + rt_all, writes=[pt])
                                    n_ = min(ntile_, bi * 4 + 4) - bi * 4
                                    T.op('act', lambda e, ps=ps, bi=bi, n_=n_: e.activation(
                                        out=pT[r][:, bi * 512:bi * 512 + n_ * 128], in_=ps[:, 0:n_ * 128], func=AF.Exp),
                                        reads=[pt], writes=[('pT', r)])

                            def att_s2(ii):
                                qc, h = items[ii]
                                nb = nb_config(qc)
                                ntile_ = len(nb) + 4
                                c, po = h // 2, (h % 2) * 64
                                r = ii % 2
                                ych = ychs[qc % 2]
                                ytok = ('ychD', qc % 2)
                                pso, pto = psb[6 + ii % 2], ('ps', 6 + ii % 2)

                                def emit(e):
                                    ins = None
                                    for i in range(ntile_):
                                        if i < len(nb):
                                            lhs = Vt[:, nb[i][0], h * 64:(h + 1) * 64]
                                        else:
                                            lhs = cvb[:, i - len(nb), h * 64:(h + 1) * 64]
                                        ins = e.matmul(pso[0:64, 0:128], lhsT=lhs, rhs=pT[r][:, i * 128:(i + 1) * 128],
                                                       start=(i == 0), stop=(i == ntile_ - 1))
                                    for i in range(ntile_):
                                        ins = e.matmul(pso[0:64, 128:256], lhsT=ones64, rhs=pT[r][:, i * 128:(i + 1) * 128],
                                                       start=(i == 0), stop=(i == ntile_ - 1))
                                    return ins
                                T.op('pe', emit, reads=['Vt', 'cvb', ('pT', r), 'cstb'], writes=[pto])
                                T.op('dve', lambda e: e.reciprocal(out=rsd[r][:, 0:128], in_=pso[0:64, 128:256]),
                                     reads=[pto], writes=[('rsd', r)])
                                T.op('dve', lambda e: e.tensor_tensor(
                                    out=ych[po:po + 64, c, :], in0=pso[0:64, 0:128],
                                    in1=rsd[r][:, 0:128], op=ALU.mult), reads=[pto, ('rsd', r)], writes=[ytok])
                                if h == 7:
                                    T.dma('sp', yTd[12 * 128:16 * 128, qc * 128:(qc + 1) * 128].rearrange("(e p) t -> p e t", p=128), ych[:],
                                          reads=[ytok], writes=[T.new('yT_' + s.name)])

                            att_s1(0)
                            for ii in range(len(items)):
                                if ii + 1 < len(items):
                                    att_s1(ii + 1)
                                att_s2(ii)
                        T.barrier()
                    T.barrier()

                with ExitStack() as ph:
                    wo = sb(ph, "wo", [128, 16, D], BF16)
                    wov = w_out[l].rearrange("(k p) n -> p k n", p=128)
                    for q4 in range(2):
                        load_w(wo[:, :, q4 * 1024:(q4 + 1) * 1024], wov[:, :, q4 * 1024:(q4 + 1) * 1024], ('wo', q4))
                    yt = [sb(ph, "yt", [128, NCH, 512], BF16) for _ in range(2)]
                    NXC = 6
                    xc = [sb(ph, "o_xc", [128, 512], F32) for _ in range(NXC)]
                    mix = [sb(ph, "mix", [128, NCH, 512], F32) for _ in range(2)]
                    sqn = [sb(ph, "sqn", [128, 512], F32) for _ in range(2)]
                    rstd = [sb(ph, "o_rstd", [128, 512], F32) for _ in range(2)]
                    xTv = xT[s.name].rearrange("(c p) t -> p c t", p=128)
                    yTv = yTd.rearrange("(c p) t -> p c t", p=128)
                    xci = 0
                    for ti in range(s.ntile):
                        r = ti % 2
                        T.dma('sp', yt[r][:], yTv[:, :, ti * 512:(ti + 1) * 512], reads=[], writes=[('yt', r)])
                        pss, ptss = psb[6 + r], ('ps', 6 + r)
                        for n in range(NCH):
                            ps, pt = bank()

                            def emit(e, ps=ps, n=n, r=r):
                                ins = None
                                for k in range(16):
                                    ins = e.matmul(ps[:, :], lhsT=wo[:, k, n * 128:(n + 1) * 128], rhs=yt[r][:, k, :],
                                                   start=(k == 0), stop=(k == 15))
                                return ins
                            T.op('pe', emit, reads=[('wo', n // 8), ('yt', r)], writes=[pt])
                            r2 = n % 2
                            T.op('act', lambda e, ps=ps, r2=r2: e.activation(out=sqn[r2][:], in_=ps[:, :], func=AF.Square),
                                 reads=[pt], writes=[('sqn', r2)])
                            copy_op('dve', mix[r][:, n, :], ps[:, :], [pt], [('mix', r)])
                            T.op('pe', lambda e, pss=pss, r2=r2, n=n: e.matmul(pss[:, :], lhsT=ones_f, rhs=sqn[r2][:],
                                                                             start=(n == 0), stop=(n == NCH - 1)),
                                 reads=[('sqn', r2), 'cst'], writes=[ptss])
                        rstd_from_ps(pss, ptss, rstd[r], ('o_rstd', r))
                        T.op('dve', lambda e, r=r: e.tensor_tensor(out=mix[r][:], in0=mix[r][:],
                                                                   in1=rstd[r][:, :].unsqueeze(1).to_broadcast([128, NCH, 512]),
                                                                   op=ALU.mult),
                             reads=[('mix', r), ('o_rstd', r)], writes=[('mix', r)])
                        for c in range(NCH):
                            q = xci % NXC
                            xci += 1
                            T.dma('sp', xc[q][:], xTv[:, c, ti * 512:(ti + 1) * 512], reads=[], writes=[('o_xc', q)])
                            T.op('dve', lambda e, c=c, r=r, q=q: e.scalar_tensor_tensor(
                                out=xc[q][:], in0=mix[r][:, c, :], scalar=modv[:, 1, c, j:j + 1], in1=xc[q][:],
                                op0=ALU.mult, op1=ALU.add), reads=[('mix', r), ('o_xc', q), 'modv'], writes=[('o_xc', q)])
                            T.dma('pool', xTv[:, c, ti * 512:(ti + 1) * 512], xc[q][:], reads=[('o_xc', q)], writes=[T.new('o_st')])
                    T.barrier()

                for blk in range(Tn // 1024):
                    with ExitStack() as ph:
                        h2T = sb(ph, "h2T", [128, NCH, 1024], BF16)
                        gT = sb(ph, "gT", [128, NF, 1024], BF16)
                        rstd2 = [sb(ph, "f_rstd", [128, 512], F32) for _ in range(2)]
                        with ExitStack() as sub:
                            norm_phase(sub, s, l, blk * 2, 2, 2, 3, h2T, 'h2T', NT=256, nsq=1)
                            T.barrier()
                        with ExitStack() as sub:
                            FSW = 256
                            w1s = [sb(sub, "w1s", [128, 16, FSW], BF16) for _ in range(2)]
                            w3s = [sb(sub, "w3s", [128, 16, FSW], BF16) for _ in range(2)]
                            sil = [sb(sub, "sil", [128, 512], F32) for _ in range(2)]
                            w1v = w1_in[l].rearrange("(k p) n -> p k n", p=128)
                            w3v = w3_in[l].rearrange("(k p) n -> p k n", p=128)
                            it2 = 0
                            for fs in range(DFF // FSW):
                                r = fs % 2
                                load_w(w1s[r][:], w1v[:, :, fs * FSW:(fs + 1) * FSW], ('w1s', r))
                                load_w(w3s[r][:], w3v[:, :, fs * FSW:(fs + 1) * FSW], ('w3s', r))
                                for fc in range(FSW // 128):
                                    f = fs * (FSW // 128) + fc
                                    for ti in range(2):
                                        psa, pta = bank()
                                        psb_, ptb = bank()

                                        def emit(e, ps=psa, w=w1s[r], fc=fc, ti=ti):
                                            ins = None
                                            for k in range(16):
                                                ins = e.matmul(ps[:, :], lhsT=w[:, k, fc * 128:(fc + 1) * 128],
                                                               rhs=h2T[:, k, ti * 512:(ti + 1) * 512], start=(k == 0), stop=(k == 15))
                                            return ins
                                        T.op('pe', emit, reads=[('w1s', r), ('h2T', ti)], writes=[pta])

                                        def emit(e, ps=psb_, w=w3s[r], fc=fc, ti=ti):
                                            ins = None
                                            for k in range(16):
                                                ins = e.matmul(ps[:, :], lhsT=w[:, k, fc * 128:(fc + 1) * 128],
                                                               rhs=h2T[:, k, ti * 512:(ti + 1) * 512], start=(k == 0), stop=(k == 15))
                                            return ins
                                        T.op('pe', emit, reads=[('w3s', r), ('h2T', ti)], writes=[ptb])
                                        r2 = it2 % 2
                                        it2 += 1
                                        T.op('act', lambda e, psa=psa, r2=r2: e.activation(out=sil[r2][:], in_=psa[:, :], func=AF.Silu),
                                             reads=[pta], writes=[('sil', r2)])
                                        T.op('dve', lambda e, psb_=psb_, r2=r2, f=f, ti=ti: e.tensor_tensor(
                                            out=gT[:, f, ti * 512:(ti + 1) * 512], in0=sil[r2][:], in1=psb_[:, :], op=ALU.mult),
                                            reads=[('sil', r2), ptb], writes=[('gT', ti)])
                            T.barrier()
                        with ExitStack() as sub:
                            w2s = [sb(sub, "w2s", [128, NF, 256], BF16) for _ in range(2)]
                            ost = [sb(sub, "ost", [128, 512], F32) for _ in range(2)]
                            sqs = [sb(sub, "sqs", [128, 512], F32) for _ in range(2)]
                            w2v = w2_in[l].rearrange("(f p) n -> p f n", p=128)
                            pss = [psb[6], psb[7]]
                            ptss = [('ps', 6), ('ps', 7)]
                            psi2 = [0]

                            def bank6():
                                i = psi2[0] % 6
                                psi2[0] += 1
                                return psb[i], ('ps', i)
                            it2 = 0
                            for ns in range(8):
                                r = ns % 2
                                load_w(w2s[r][:], w2v[:, :, ns * 256:(ns + 1) * 256], ('w2s', r))
                                for nci in range(2):
                                    n = ns * 2 + nci
                                    for ti in range(2):
                                        ps, pt = bank6()

                                        def emit(e, ps=ps, r=r, nci=nci, ti=ti):
                                            ins = None
                                            for f in range(NF):
                                                ins = e.matmul(ps[:, :], lhsT=w2s[r][:, f, nci * 128:(nci + 1) * 128],
                                                               rhs=gT[:, f, ti * 512:(ti + 1) * 512], start=(f == 0), stop=(f == NF - 1))
                                            return ins
                                        T.op('pe', emit, reads=[('w2s', r), ('gT', ti)], writes=[pt])
                                        r2 = it2 % 2
                                        it2 += 1
                                        T.op('act', lambda e, ps=ps, r2=r2: e.activation(out=sqs[r2][:], in_=ps[:, :], func=AF.Square),
                                             reads=[pt], writes=[('sqs', r2)])
                                        copy_op('dve', ost[r2][:], ps[:, :], [pt], [('ost', r2)])
                                        T.dma('sp', oT[n * 128:(n + 1) * 128, ti * 512:(ti + 1) * 512], ost[r2][:],
                                              reads=[('ost', r2)], writes=[('oT', ti)])
                                        T.op('pe', lambda e, ti=ti, r2=r2, n=n: e.matmul(pss[ti][:, :], lhsT=ones_f, rhs=sqs[r2][:],
                                                                                        start=(n == 0), stop=(n == NCH - 1)),
                                             reads=[('sqs', r2), 'cst'], writes=[ptss[ti]])
                            for ti in range(2):
                                rstd_from_ps(pss[ti], ptss[ti], rstd2[ti], ('f_rstd', ti))
                            T.barrier()
                        with ExitStack() as sub:
                            ot = sb(sub, "ot", [128, NCH, 512], F32)
                            xt = sb(sub, "f_xt", [128, NCH, 512], F32)
                            xTv = xT[s.name].rearrange("(c p) t -> p c t", p=128)
                            oTv = oT.rearrange("(c p) t -> p c t", p=128)
                            for ti in range(2):
                                tg = blk * 2 + ti
                                T.dma('sp', ot[:], oTv[:, :, ti * 512:(ti + 1) * 512], reads=[('oT', ti)], writes=['ot'])
                                T.dma('sp', xt[:], xTv[:, :, tg * 512:(tg + 1) * 512], reads=[('xT', s.name, tg)], writes=['f_xt'])
                                T.op('dve', lambda e, ti=ti: e.tensor_tensor(
                                    out=ot[:], in0=ot[:], in1=rstd2[ti][:, :].unsqueeze(1).to_broadcast([128, NCH, 512]), op=ALU.mult),
                                    reads=['ot', ('f_rstd', ti)], writes=['ot'])
                                for c in range(NCH):
                                    T.op('dve', lambda e, c=c: e.scalar_tensor_tensor(
                                        out=xt[:, c, :], in0=ot[:, c, :], scalar=modv[:, 3, c, j:j + 1], in1=xt[:, c, :],
                                        op0=ALU.mult, op1=ALU.add), reads=['ot', 'f_xt', 'modv'], writes=['f_xt'])
                                T.dma('sp', xTv[:, :, tg * 512:(tg + 1) * 512], xt[:], reads=['f_xt'], writes=[('xT', s.name, tg)])
                            T.barrier()
                        T.barrier()

        with ExitStack() as ph:
            xin = [sb(ph, "fxin", [128, NCH, 128], F32) for _ in range(2)]
            xo = [sb(ph, "fxo", [128, D], F32) for _ in range(2)]
            it = 0
            for s in segs:
                xTv = xT[s.name].rearrange("(c p) t -> p c t", p=128)
                for tc in range(s.T // 128):
                    r = it % 2
                    it += 1
                    T.dma('sp', xin[r][:], xTv[:, :, tc * 128:(tc + 1) * 128], reads=[('xT', s.name, tc // 4)], writes=[('fxin', r)])
                    for g in range(4):
                        ps, pt = bank()

                        def emit(e, ps=ps, r=r, g=g):
                            ins = None
                            for jj in range(4):
                                ins = e.transpose(out=ps[:, jj * 128:(jj + 1) * 128], in_=xin[r][:, 4 * g + jj, :], identity=ident)
                            return ins
                        T.op('pe', emit, reads=[('fxin', r), 'cst'], writes=[pt])
                        copy_op(evac_engine(g), xo[r][:, g * 512:(g + 1) * 512], ps[:, :], [pt], [('fxo', r, g)])
                    T.dma('sp', y_out[s.name][tc * 128:(tc + 1) * 128, :], xo[r][:],
                          reads=[('fxo', r, g) for g in range(4)], writes=[])
            T.barrier()


_CACHE = {}


def make_in_maps(inputs, ncores=8):
    hc = host_consts()
    f = lambda a: np.ascontiguousarray(np.asarray(a, dtype=np.float32))
    shared = {}
    for nm in ["w_ada", "b_ada", "g_pre_mix", "g_post_mix", "g_pre_ffn", "g_post_ffn", "w_in", "pool_w", "pool_scale",
               "lru_conv_b", "lru_gate_w", "cm_dw_b", "cm_ln_g", "cm_ln_b", "cm_pw_w", "cm_pw_b", "w_out",
               "ffn_w1", "ffn_w3", "ffn_w2"]:
        shared[nm] = f(inputs[nm])
    shared["lru_conv_w"] = f(inputs["lru_conv_w"]).reshape(DEPTH, 4 * 512)
    shared["lru_gate_b"] = f(inputs["lru_gate_b"]).reshape(DEPTH, 4 * 512)
    shared["lru_lambda"] = f(inputs["lru_lambda"]).reshape(DEPTH, 2 * 512)
    shared["cm_dw_w"] = f(inputs["cm_dw_w"]).reshape(DEPTH, 31 * 512)
    shared["na_rpb"] = f(inputs["na_rpb"]).reshape(DEPTH, 120, 31)
    shared.update(hc)
    xp = f(inputs["x_prompt"])
    xs = f(inputs["x_sample"])
    ck = f(inputs["cache_k"])
    cv = f(inputs["cache_v"])
    st = f(inputs["state_lru"])
    c = f(inputs["c"])
    cctx = f(inputs["c_ctx"])
    nsmp = xs.shape[0]
    maps = []
    for i in range(ncores):
        si = i % nsmp
        m = dict(shared)
        m["xp"] = np.ascontiguousarray(xp[i * NPSEQ:(i + 1) * NPSEQ].reshape(NPSEQ * LP, D))
        m["xs"] = np.ascontiguousarray(xs[si])
        m["ck"] = np.ascontiguousarray(ck[si].reshape(DEPTH, PAST, 512))
        m["cv"] = np.ascontiguousarray(cv[si].reshape(DEPTH, PAST, 512))
        m["st"] = np.ascontiguousarray(st[si].reshape(-1))
        m["cond"] = np.ascontiguousarray(np.stack([cctx, c[si]], axis=0))
        maps.append(m)
    return maps


def kernel(**inputs):
    key = "full"
    if key not in _CACHE:
        _CACHE[key] = build_program()
    nc = _CACHE[key]
    maps = make_in_maps(inputs, 8)
    res = run_bass_kernel_spmd(nc, maps, core_ids=list(range(8)))
    rs = res.results
    B = 32
    y_prompt = np.concatenate([rs[i]["yp"].reshape(NPSEQ, LP, D) for i in range(8)], axis=0).astype(np.float32)
    y_sample = np.stack([rs[0]["ys"], rs[1]["ys"]], axis=0).astype(np.float32)
    nk = np.concatenate([rs[i]["nk"] for i in range(8)], axis=0).reshape(B, DEPTH, LP, 8, 64).astype(np.float32)
    nv = np.concatenate([rs[i]["nv"] for i in range(8)], axis=0).reshape(B, DEPTH, LP, 8, 64).astype(np.float32)
    nst = np.concatenate([rs[i]["nst"] for i in range(8)], axis=0).reshape(B, DEPTH, 2, 512).astype(np.float32)
    return (y_prompt, y_sample, nk, nv, nst)
```

```python
import numpy as np
from contextlib import ExitStack
import concourse.bass as bass
import concourse.mybir as mybir
from concourse.bass_utils import run_bass_kernel_spmd

F32 = mybir.dt.float32
BF16 = mybir.dt.bfloat16
AF = mybir.ActivationFunctionType
ALU = mybir.AluOpType

D = 2048
NCH = 16
DEPTH = 4
DIN = 4096
DFF = 5632
NF = 44
NMOD = 6
PAST = 512
NEG = -30000.0
EPS = 1e-6
NPSEQ = 4
LP = 256
LS = 2048
GRID_W = 64

PV_ROWS = {}
_lay = [
    [("b_ada", 96), ("g_pre_mix", 16), ("g_post_mix", 16)],
    [("g_pre_ffn", 16), ("g_post_ffn", 16), ("pool_scale", 4), ("lru_conv_w", 16), ("lru_conv_b", 4),
     ("lru_gate_b", 16), ("lru_lambda", 8), ("cm_dw_b", 4), ("cm_ln_g", 4), ("cm_ln_b", 4), ("cm_pw_b", 4)],
    [("cm_dw_w", 124)],
]
for _ti, _names in enumerate(_lay):
    _r = 0
    for _n, _k in _names:
        PV_ROWS[_n] = (_ti, _r, _k)
        _r += _k
    assert _r <= 128


class StopBuild(Exception):
    pass


class _CntEng:
    def __init__(self, eng):
        self._e = eng
        self.n = 0

    def __getattr__(self, name):
        f = getattr(self._e, name)
        if name in ('matmul', 'transpose'):
            def g(*a, **k):
                self.n += 1
                return f(*a, **k)
            return g
        return f


class Trk:
    NDMA = 10

    def __init__(self, nc, es):
        self.nc = nc
        self.eng = {'pe': _CntEng(nc.tensor), 'act': nc.scalar, 'dve': nc.vector, 'pool': nc.gpsimd, 'sp': nc.sync}
        self.phases = []
        self.sem = {}
        self.cnt = {}
        for k in self.eng:
            self.sem[k] = es.enter_context(nc.semaphore("s_" + k))
            self.cnt[k] = 0
        self.dq = {}
        for q in ('sp', 'pool'):
            keys = []
            for i in range(self.NDMA):
                k = "d_%s_%d" % (q, i)
                self.sem[k] = es.enter_context(nc.semaphore(k))
                self.cnt[k] = 0
                keys.append(k)
            self.dq[q] = [keys, 0]
        self.seen = {k: {} for k in self.eng}
        self.lw = {}
        self.rd = {}
        self.ninst = 0
        self.grp = {}
        self.stop_at = 0
        self.stopped = False

    def new(self, grp):
        l = self.grp.setdefault(grp, [])
        t = (grp, '#', len(l))
        l.append(t)
        return t

    def all(self, grp):
        return list(self.grp.get(grp, []))

    def _deps(self, e, reads, writes):
        deps = {}

        def add(ev, raw):
            k, v = ev
            if k == e and not raw:
                return
            if deps.get(k, 0) < v:
                deps[k] = v
        for t in reads:
            if t in self.lw:
                add(self.lw[t], True)
            if isinstance(t, tuple) and t[0] == 'ps':
                for ev in self.rd.get(t, {}).items():
                    add(ev, False)
        for t in writes:
            if t in self.lw:
                add(self.lw[t], False)
            for ev in self.rd.get(t, {}).items():
                add(ev, False)
        return deps

    def _wait(self, e, deps):
        eng = self.eng[e]
        seen = self.seen[e]
        for k, v in deps.items():
            if seen.get(k, 0) < v:
                eng.wait_ge(self.sem[k], v)
                seen[k] = v
                self.ninst += 1

    def _commit(self, ev, reads, writes):
        k, v = ev
        for t in writes:
            self.lw[t] = ev
            self.rd[t] = {}
        for t in reads:
            d = self.rd.setdefault(t, {})
            if d.get(k, 0) < v:
                d[k] = v

    def op(self, e, emit, reads=(), writes=()):
        if self.stopped:
            return
        self._wait(e, self._deps(e, reads, writes))
        inst = emit(self.eng[e])
        self.cnt[e] += 1
        inst.then_inc(self.sem[e], 1)
        self.ninst += 1
        self._commit((e, self.cnt[e]), reads, writes)

    def dma(self, q, out, in_, reads=(), writes=(), **kw):
        if self.stopped:
            return
        keys, i = self.dq[q]
        k = keys[i % len(keys)]
        self.dq[q][1] = i + 1
        deps = self._deps(None, reads, writes)
        if self.cnt[k] > 0:
            deps[k] = max(deps.get(k, 0), self.cnt[k])
        self._wait(q, deps)
        self.eng[q].dma_start(out=out, in_=in_, **kw).then_inc(self.sem[k], 16)
        self.cnt[k] += 16
        self.ninst += 1
        self._commit((k, self.cnt[k]), reads, writes)

    def barrier(self):
        if self.stopped:
            return
        self.nbar = getattr(self, 'nbar', 0) + 1
        self.phases.append((self.nbar, self.eng['pe'].n))
        for e in self.eng:
            self._wait(e, {k: v for k, v in self.cnt.items() if v > 0 and k != e})
        self.lw = {}
        self.rd = {}
        self.grp = {}
        if self.stop_at and self.nbar >= self.stop_at:
            self.stopped = True

    def mark(self, n):
        import os
        if int(os.environ.get('STOPM', '0')) == n:
            self.stopped = True

    def finish(self):
        self.stopped = False
        self._wait('sp', {k: v for k, v in self.cnt.items() if v > 0 and k != 'sp'})


class Seg:
    def __init__(self, name, T, L, cond):
        self.name = name
        self.T = T
        self.L = L
        self.nseq = T // L
        self.cond = cond
        self.ntile = T // 512
        self.sample = (name == 'S')


def host_consts():
    c = {}
    ident = np.eye(128, dtype=np.float32)
    flip = ident[::-1].copy()
    ones = np.ones((128, 128), np.float32)
    c['cst'] = np.concatenate([ident, flip, ones], axis=1)
    er = np.ones((4, 16), np.float32)
    Lh = 64
    for gi, w in enumerate((2, 4, 8, 16)):
        for t in range(8):
            lo = max(t - w // 2, 0)
            hi = min(t - w // 2 + w, Lh)
            er[gi, t] = w / float(hi - lo)
        for i in range(8):
            t = Lh - 8 + i
            lo = max(t - w // 2, 0)
            hi = min(t - w // 2 + w, Lh)
            er[gi, 8 + i] = w / float(hi - lo)
    c['edger'] = np.broadcast_to(er.reshape(1, 64), (128, 64)).copy()
    bases = [-6, -4, -2, 0, 2, 4, 6, -4, 4]
    col = np.arange(64)
    cs = np.clip(col - 8, 0, 48)
    mk = np.zeros((128, 9, 128), np.float32)
    for ty, base in enumerate(bases):
        interior = ty >= 7
        for pr in range(2):
            for qr in range(2):
                dr = base + pr - qr
                rowok = abs(dr) <= 7 and ((not interior) or (-4 <= dr <= 3))
                for pp in range(64):
                    kc = 63 - pp
                    ok = rowok & (kc >= cs) & (kc < cs + 16)
                    mk[(1 - pr) * 64 + pp, ty, qr * 64:(qr + 1) * 64] = np.where(ok, 0.0, NEG)
    c['maskr'] = mk
    return c


RT_BASES = [-6, -4, -2, 0, 2, 4, 6, -4, 4]


def nb_config(qc):
    edge = {-6: 0, -4: 1, -2: 2, 0: 3, 2: 4, 4: 5, 6: 6}
    if qc == 0:
        return [(j, edge[2 * j]) for j in range(4)]
    if qc == 1:
        return [(j, edge[2 * j - 2]) for j in range(4)]
    if qc == 14:
        return [(12 + j, edge[2 * j - 4]) for j in range(4)]
    if qc == 15:
        return [(12 + j, edge[2 * j - 6]) for j in range(4)]
    out = []
    for j in range(5):
        b = 2 * j - 4
        ty = 7 if j == 0 else (8 if j == 4 else edge[b])
        out.append((qc - 2 + j, ty))
    return out


def build_program(nl=DEPTH, do_sample=True, do_prompt=True, stop_at=0):
    nc = bass.Bass("TRN2", target_bir_lowering=False)
    din = {}

    def dram_in(name, shape):
        din[name] = nc.dram_tensor(name, list(shape), F32, kind="ExternalInput").ap()
        return din[name]

    xp_in = dram_in("xp", [NPSEQ * LP, D])
    xs_in = dram_in("xs", [LS, D])
    ck_in = dram_in("ck", [DEPTH, PAST, 512])
    cv_in = dram_in("cv", [DEPTH, PAST, 512])
    st_in = dram_in("st", [DEPTH * 2 * 512])
    cond_in = dram_in("cond", [2, D])
    w_ada = dram_in("w_ada", [DEPTH, D, NMOD * D])
    vec_in = {}
    for nm, n in [("b_ada", NMOD * D), ("g_pre_mix", D), ("g_post_mix", D), ("g_pre_ffn", D), ("g_post_ffn", D),
                  ("pool_scale", 512), ("lru_conv_w", 4 * 512), ("lru_conv_b", 512), ("lru_gate_b", 4 * 512),
                  ("lru_lambda", 2 * 512), ("cm_dw_w", 31 * 512), ("cm_dw_b", 512), ("cm_ln_g", 512),
                  ("cm_ln_b", 512), ("cm_pw_b", 512)]:
        vec_in[nm] = dram_in(nm, [DEPTH, n])
    w_in = dram_in("w_in", [DEPTH, D, DIN])
    pool_w = dram_in("pool_w", [DEPTH, 4, 128, 128])
    gate_w = dram_in("lru_gate_w", [DEPTH, 2, 2, 8, 64, 64])
    pw_w = dram_in("cm_pw_w", [DEPTH, 512, 512])
    rpb_in = dram_in("na_rpb", [DEPTH, 120, 31])
    w_out = dram_in("w_out", [DEPTH, D, D])
    w1_in = dram_in("ffn_w1", [DEPTH, D, DFF])
    w3_in = dram_in("ffn_w3", [DEPTH, D, DFF])
    w2_in = dram_in("ffn_w2", [DEPTH, DFF, D])
    cst_in = dram_in("cst", [128, 384])
    edger_in = dram_in("edger", [128, 64])
    maskr_in = dram_in("maskr", [128, 9, 128])

    yp_out = nc.dram_tensor("yp", [NPSEQ * LP, D], F32, kind="ExternalOutput").ap()
    ys_out = nc.dram_tensor("ys", [LS, D], F32, kind="ExternalOutput").ap()
    nk_out = nc.dram_tensor("nk", [NPSEQ, DEPTH, LP, 512], F32, kind="ExternalOutput").ap()
    nv_out = nc.dram_tensor("nv", [NPSEQ, DEPTH, LP, 512], F32, kind="ExternalOutput").ap()
    nst_out = nc.dram_tensor("nst", [NPSEQ, DEPTH, 2, 512], F32, kind="ExternalOutput").ap()

    segP = Seg('P', NPSEQ * LP, LP, 0)
    segS = Seg('S', LS, LS, 1)
    segs = ([segP] if do_prompt else []) + ([segS] if do_sample else [])
    xT = {s.name: nc.dram_tensor("xT_" + s.name, [D, s.T], F32, kind="Internal").ap() for s in (segP, segS)}
    yT = {s.name: nc.dram_tensor("yT_" + s.name, [D, s.T], BF16, kind="Internal").ap() for s in (segP, segS)}
    oT = nc.dram_tensor("oT", [D, 1024], F32, kind="Internal").ap()
    zpd = nc.dram_tensor("zpd", [120, 128], F32, kind="Internal").ap()
    x_in = {'P': xp_in, 'S': xs_in}
    y_out = {'P': yp_out, 'S': ys_out}

    es = ExitStack()
    with es:
        T = Trk(nc, es)
        T.stop_at = stop_at
        uid = [0]
        try:
            _emit_all(locals())
        except StopBuild:
            pass
        T.finish()
    nc._trk_ninst = T.ninst
    nc._trk_phases = T.phases
    return nc


def _emit_all(env):
    nc = env['nc']; T = env['T']; uid = env['uid']; es = env['es']
    segs = env['segs']; nl = env['nl']
    if True:
        xp_in = env['xp_in']; xs_in = env['xs_in']; ck_in = env['ck_in']; cv_in = env['cv_in']; st_in = env['st_in']
        cond_in = env['cond_in']; w_ada = env['w_ada']; vec_in = env['vec_in']; w_in = env['w_in']; pool_w = env['pool_w']
        gate_w = env['gate_w']; pw_w = env['pw_w']; rpb_in = env['rpb_in']; w_out = env['w_out']; w1_in = env['w1_in']
        w3_in = env['w3_in']; w2_in = env['w2_in']; cst_in = env['cst_in']; edger_in = env['edger_in']; maskr_in = env['maskr_in']
        yp_out = env['yp_out']; ys_out = env['ys_out']; nk_out = env['nk_out']; nv_out = env['nv_out']; nst_out = env['nst_out']
        segP = env['segP']; segS = env['segS']; xT = env['xT']; yT = env['yT']; oT = env['oT']; zpd = env['zpd']
        x_in = env['x_in']; y_out = env['y_out']

        def sb(stack, name, shape, dt):
            uid[0] += 1
            return stack.enter_context(nc.sbuf_tensor("%s_%d" % (name, uid[0]), list(shape), dt))

        psb = [es.enter_context(nc.psum_tensor("psb%d" % i, [128, 512], F32)) for i in range(8)]
        psi = [0]

        def bank():
            i = psi[0] % 6
            psi[0] += 1
            return psb[i], ('ps', i)

        cst = sb(es, "cst", [128, 384], F32)
        T.dma('sp', cst[:], cst_in[:, :], writes=['cst'])
        ident = cst[:, 0:128]
        ones_f = cst[:, 256:384]
        cstb = sb(es, "cstb", [128, 384], BF16)
        T.op('dve', lambda e: e.tensor_copy(out=cstb[:], in_=cst[:]), reads=['cst'], writes=['cstb'])
        ident_b = cstb[:, 0:128]
        flip_b = cstb[:, 128:256]
        epst = sb(es, "epst", [128, 1], F32)
        T.op('pool', lambda e: e.memset(epst[:], EPS), writes=['epst'])
        edger = sb(es, "edger", [128, 4, 16], F32)
        T.dma('sp', edger[:].rearrange("p a b -> p (a b)"), edger_in[:, :], writes=['edger'])
        pv = sb(es, "pv", [128, 3, 128], F32)
        mod = sb(es, "mod", [128, 96, 2], F32)
        modv = sb(es, "modv", [128, 4, 16, 2], F32)
        condT = sb(es, "condT", [128, 16, 2], BF16)
        h0all = sb(es, "h0all", [128, 32], F32)
        nsp = sb(es, "nsp", [128, 2, 8], F32)
        gw = sb(es, "gw", [128, 16, 128], BF16)

        def pvc(name, idx):
            ti, r0, n = PV_ROWS[name]
            return pv[:, ti, r0 + idx:r0 + idx + 1]

        def pvr(name):
            ti, r0, n = PV_ROWS[name]
            return pv[:, ti, r0:r0 + n]

        CONST_R = ['cst', 'cstb', 'epst']

        def evac_engine(i):
            return 'act' if i % 2 == 0 else 'dve'

        def copy_op(e, out, in_, reads, writes):
            if e == 'act':
                T.op('act', lambda g: g.activation(out=out, in_=in_, func=AF.Identity), reads=reads, writes=writes)
            else:
                T.op(e, lambda g: g.tensor_copy(out=out, in_=in_), reads=reads, writes=writes)

        with ExitStack() as ph:
            xin = [sb(ph, "xin", [128, D], F32) for _ in range(2)]
            xo = [sb(ph, "xo", [128, NCH, 128], F32) for _ in range(2)]
            it = 0
            for s in segs:
                xTv = xT[s.name].rearrange("(c p) t -> p c t", p=128)
                for tc in range(s.T // 128):
                    r = it % 2
                    it += 1
                    T.dma('sp', xin[r][:], x_in[s.name][tc * 128:(tc + 1) * 128, :], writes=[('xin', r)])
                    for g in range(4):
                        ps, pt = bank()

                        def emit(e, ps=ps, r=r, g=g):
                            ins = None
                            for j in range(4):
                                c = 4 * g + j
                                ins = e.transpose(out=ps[:, j * 128:(j + 1) * 128], in_=xin[r][:, c * 128:(c + 1) * 128],
                                                  identity=ident)
                            return ins
                        T.op('pe', emit, reads=[('xin', r), 'cst'], writes=[pt])
                        copy_op(evac_engine(g), xo[r][:, 4 * g:4 * g + 4, :],
                                ps[:, :].rearrange("p (a b) -> p a b", b=128), [pt], [('xo', r, g)])
                    T.dma('sp', xTv[:, :, tc * 128:(tc + 1) * 128], xo[r][:],
                          reads=[('xo', r, g) for g in range(4)], writes=[T.new('xT0')])
            cs = sb(ph, "cs", [2, D], F32)
            T.dma('sp', cs[:], cond_in[:, :], writes=['cs'])
            ps, pt = bank()

            def emit(e, ps=ps):
                ins = None
                for c in range(16):
                    ins = e.transpose(out=ps[:, c * 2:(c + 1) * 2], in_=cs[0:2, c * 128:(c + 1) * 128],
                                      identity=cst[0:2, 0:2])
                return ins
            T.op('pe', emit, reads=['cs', 'cst'], writes=[pt])
            T.op('act', lambda e: e.activation(out=condT[:].rearrange("p a b -> p (a b)"), in_=ps[:, 0:32], func=AF.Silu),
                 reads=[pt], writes=['condT'])
            sst = sb(ph, "sst", [32, 128], F32)
            T.dma('sp', sst[:], st_in.rearrange("(n p) -> n p", p=128), writes=['sst'])
            ps, pt = bank()
            T.op('pe', lambda e: e.transpose(out=ps[:, 0:32], in_=sst[:, :], identity=cst[0:32, 0:32]),
                 reads=['sst', 'cst'], writes=[pt])
            T.op('dve', lambda e: e.tensor_copy(out=h0all[:], in_=ps[:, 0:32]), reads=[pt], writes=['h0all'])
            T.barrier()

        def load_w(dst, src, tok):
            T.dma('pool', dst, src, writes=[tok])

        def norm_phase(stack, s, l, t0, ntile, gi, shi, hT, htok, NT=256, nsq=2):
            j = s.cond
            xTv = xT[s.name].rearrange("(c p) t -> p c t", p=128)
            xt = [sb(stack, "n_xt", [128, NCH, NT], F32) for _ in range(2)]
            sq = [sb(stack, "n_sq", [128, NCH, NT], F32) for _ in range(nsq)]
            rstd = [sb(stack, "n_rstd", [128, NT], F32) for _ in range(2)]
            npp = 512 // NT
            for pi in range(ntile * npp):
                ti = pi // npp
                tg = t0 + ti
                c0 = tg * 512 + (pi % npp) * NT
                h0 = ti * 512 + (pi % npp) * NT
                rx = pi % 2
                r = pi % nsq
                r2 = pi % 2
                for g4 in range(4):
                    T.dma('sp', xt[rx][:, 4 * g4:4 * g4 + 4, :], xTv[:, 4 * g4:4 * g4 + 4, c0:c0 + NT],
                          reads=[('xT', s.name, tg)], writes=[('n_xt', rx, g4)])
                xtoks = [('n_xt', rx, g4) for g4 in range(4)]
                T.op('act', lambda e, r=r, rx=rx: e.activation(out=sq[r][:], in_=xt[rx][:], func=AF.Square),
                     reads=xtoks, writes=[('n_sq', r)])
                ps, pt = bank()

                def emit(e, ps=ps, r=r):
                    ins = None
                    for c in range(NCH):
                        ins = e.matmul(ps[:, 0:NT], lhsT=ones_f, rhs=sq[r][:, c, :], start=(c == 0), stop=(c == NCH - 1))
                    return ins
                T.op('pe', emit, reads=[('n_sq', r), 'cst'], writes=[pt])
                T.op('act', lambda e, ps=ps, r2=r2: e.activation(out=rstd[r2][:], in_=ps[:, 0:NT], func=AF.Sqrt, bias=epst[:, 0:1],
                                                                 scale=1.0 / D), reads=[pt, 'epst'], writes=[('n_rstd', r2)])
                T.op('dve', lambda e, r2=r2: e.reciprocal(out=rstd[r2][:], in_=rstd[r2][:]),
                     reads=[('n_rstd', r2)], writes=[('n_rstd', r2)])
                T.op('dve', lambda e, r2=r2, r=r, rx=rx: e.tensor_tensor(out=sq[r][:], in0=xt[rx][:],
                                                             in1=rstd[r2][:, :].unsqueeze(1).to_broadcast([128, NCH, NT]),
                                                             op=ALU.mult),
                     reads=xtoks + [('n_rstd', r2)], writes=[('n_sq', r)])
                for c in range(NCH):
                    gsc = modv[:, gi, c, j:j + 1]
                    shc = mod[:, shi * 16 + c, j:j + 1]
                    dst = hT[:, c, h0:h0 + NT]
                    if c % 2 == 0:
                        T.op('act', lambda e, dst=dst, c=c, gsc=gsc, shc=shc, r=r: e.activation(
                            out=dst, in_=sq[r][:, c, :], func=AF.Identity, bias=shc, scale=gsc),
                            reads=[('n_sq', r), 'modv', 'mod'], writes=[(htok, ti)])
                    else:
                        T.op('dve', lambda e, dst=dst, c=c, gsc=gsc, shc=shc, r=r: e.tensor_scalar(
                            out=dst, in0=sq[r][:, c, :], scalar1=gsc, scalar2=shc, op0=ALU.mult, op1=ALU.add),
                            reads=[('n_sq', r), 'modv', 'mod'], writes=[(htok, ti)])

        def rstd_from_ps(ps, pt, dst, dtok, n=512, scale=1.0 / D):
            T.op('act', lambda e: e.activation(out=dst[:, 0:n], in_=ps[:, 0:n], func=AF.Sqrt, bias=epst[:, 0:1], scale=scale),
                 reads=[pt, 'epst'], writes=[dtok])
            T.op('dve', lambda e: e.reciprocal(out=dst[:, 0:n], in_=dst[:, 0:n]), reads=[dtok], writes=[dtok])

        for l in range(nl):
            with ExitStack() as ph:
                stg = sb(ph, "pstg", [128, 3, 128], F32)
                T.op('pool', lambda e: e.memset(stg[:], 0.0), writes=['pstg'])
                for nm, (ti, r0, n) in PV_ROWS.items():
                    T.dma('sp', stg[r0:r0 + n, ti, :], vec_in[nm][l].rearrange("(n p) -> n p", p=128),
                          reads=['pstg'], writes=[T.new('pstg_d')])
                for ti in range(3):
                    ps, pt = bank()
                    T.op('pe', lambda e, ps=ps, ti=ti: e.transpose(out=ps[:, 0:128], in_=stg[:, ti, :], identity=ident),
                         reads=['pstg', 'cst'] + T.all('pstg_d'), writes=[pt])
                    copy_op('dve', pv[:, ti, :], ps[:, 0:128], [pt], ['pv'])
                lam = pvr("lru_lambda")
                tmp = sb(ph, "lamtmp", [128, 8], F32)
                T.op('act', lambda e: e.activation(out=tmp[:], in_=lam, func=AF.Exp, scale=-1.0), reads=['pv'], writes=['lamtmp'])
                T.op('act', lambda e: e.activation(out=tmp[:], in_=tmp[:], func=AF.Ln, bias=1.0, scale=1.0),
                     reads=['lamtmp'], writes=['lamtmp'])
                T.op('dve', lambda e: e.tensor_scalar(out=nsp[:, 0, :], in0=tmp[:], scalar1=-8.0, scalar2=None, op0=ALU.mult),
                     reads=['lamtmp'], writes=['nsp'])
                T.op('dve', lambda e: e.tensor_scalar(out=nsp[:, 1, :], in0=tmp[:], scalar1=-16.0, scalar2=None, op0=ALU.mult),
                     reads=['lamtmp'], writes=['nsp'])
                T.op('pool', lambda e: e.memset(gw[:], 0.0), writes=['gw'])
                for d in range(2):
                    for g in range(2):
                        for k in range(8):
                            po = (k % 2) * 64
                            T.dma('pool', gw[po:po + 64, (d * 2 + g) * 4 + k // 2, po:po + 64], gate_w[l, d, g, k],
                                  reads=['gw'], writes=[T.new('gw_d')])
                wsl = [sb(ph, "wada", [128, 16, 1024], BF16) for _ in range(3)]
                wv = w_ada[l].rearrange("(k p) n -> p k n", p=128)
                psm, ptm = bank()
                for si in range(12):
                    r = si % 3
                    load_w(wsl[r][:], wv[:, :, si * 1024:(si + 1) * 1024], ('wada', r))

                    def emit(e, r=r, si=si):
                        ins = None
                        for jn in range(8):
                            n = si * 8 + jn
                            for k in range(16):
                                ins = e.matmul(psm[:, n * 2:(n + 1) * 2], lhsT=wsl[r][:, k, jn * 128:(jn + 1) * 128],
                                               rhs=condT[:, k, :], start=(k == 0), stop=(k == 15))
                        return ins
                    T.op('pe', emit, reads=[('wada', r), 'condT'], writes=[ptm])
                bada = pvr("b_ada")
                T.op('dve', lambda e: e.tensor_tensor(out=mod[:], in0=psm[:, 0:192].rearrange("p (a b) -> p a b", b=2),
                                                      in1=bada.unsqueeze(2).to_broadcast([128, 96, 2]), op=ALU.add),
                     reads=[ptm, 'pv'], writes=['mod'])
                for a, (sci, gname) in enumerate([(1, "g_pre_mix"), (2, "g_post_mix"), (4, "g_pre_ffn"), (5, "g_post_ffn")]):
                    gv = pvr(gname).unsqueeze(2).to_broadcast([128, 16, 2])
                    src = mod[:, sci * 16:(sci + 1) * 16, :]
                    if a % 2 == 0:
                        T.op('dve', lambda e, a=a, src=src, gv=gv: e.scalar_tensor_tensor(
                            out=modv[:, a, :, :], in0=src, scalar=1.0, in1=gv, op0=ALU.add, op1=ALU.mult),
                            reads=['mod', 'pv'], writes=['modv'])
                    else:
                        T.op('dve', lambda e, a=a, src=src, gv=gv: e.tensor_tensor(
                            out=modv[:, a, :, :], in0=src, in1=gv, op=ALU.mult), reads=['mod', 'pv'], writes=['modv'])
                T.barrier()

            for s in segs:
                j = s.cond
                Tn, L, nseq = s.T, s.L, s.nseq
                yTd = yT[s.name]
                spt = max(1, 512 // L)
                ntl = min(L, 512)

                def tview(buf3, padl, ti):
                    if L >= 512:
                        sq_ = (ti * 512) // L
                        off = (ti * 512) % L
                        return buf3[:, sq_, padl + off:padl + off + 512]
                    return buf3[:, ti * spt:(ti + 1) * spt, padl:padl + L]

                def pview(ps):
                    if L >= 512:
                        return ps[:, :]
                    return ps[:, :].rearrange("p (a b) -> p a b", b=L)

                with ExitStack() as ph:
                    hT = sb(ph, "hT", [128, NCH, Tn], BF16)
                    with ExitStack() as sub:
                        norm_phase(sub, s, l, 0, s.ntile, 0, 0, hT, 'hT')
                        T.barrier()
                    wslab = [sb(ph, "wslab", [128, 16, 512], BF16) for _ in range(2)]
                    wiv = w_in[l].rearrange("(k p) n -> p k n", p=128)
                    wcnt = [0]

                    def get_slab(si):
                        r = wcnt[0] % 2
                        wcnt[0] += 1
                        load_w(wslab[r][:], wiv[:, :, si * 512:(si + 1) * 512], ('wslab', r))
                        return wslab[r], ('wslab', r)

                    def proj_fm(slab, stok, jc, ti):
                        ps, pt = bank()

                        def emit(e):
                            ins = None
                            for k in range(16):
                                ins = e.matmul(ps[:, :], lhsT=slab[:, k, jc * 128:(jc + 1) * 128],
                                               rhs=hT[:, k, ti * 512:(ti + 1) * 512], start=(k == 0), stop=(k == 15))
                            return ins
                        T.op('pe', emit, reads=[stok, ('hT', ti)], writes=[pt])
                        return ps, pt

                    with ExitStack() as sub:
                        Lp = L + 16
                        up = sb(sub, "up", [128, nseq, Lp], F32)
                        lv = [sb(sub, "lv", [128, nseq, Lp], F32) for _ in range(2)]
                        pooled = sb(sub, "pooled", [128, Tn], BF16)
                        ych = sb(sub, "ychA", [128, Tn], BF16)
                        pwt = sb(sub, "poolw", [128, 4, 128], BF16)
                        load_w(pwt[:], pool_w[l].rearrange("g c d -> c g d"), 'poolw')
                        T.op('pool', lambda e: e.memset(up[:], 0.0), writes=['up'])
                        slab, stok = get_slab(0)
                        for g in range(4):
                            w = 2 << g
                            for ti in range(s.ntile):
                                ps, pt = proj_fm(slab, stok, g, ti)
                                copy_op(evac_engine(ti), tview(up, 8, ti), pview(ps), [pt], ['up'])
                            cur, ctok = up, 'up'
                            m = 1
                            lo, hi = 0, Lp
                            step = 0
                            while m < w:
                                dst = lv[step % 2]
                                dtok = ('lv', step % 2)
                                if m == 1:
                                    nlo, nhi = lo + 1, hi
                                    a0 = cur[:, :, nlo - 1:nhi - 1]
                                    a1 = cur[:, :, nlo:nhi]
                                else:
                                    h2 = m // 2
                                    nlo, nhi = lo + h2, hi - h2
                                    a0 = cur[:, :, nlo - h2:nhi - h2]
                                    a1 = cur[:, :, nlo + h2:nhi + h2]
                                T.op('dve', lambda e, dst=dst, a0=a0, a1=a1, nlo=nlo, nhi=nhi: e.tensor_tensor(
                                    out=dst[:, :, nlo:nhi], in0=a0, in1=a1, op=ALU.add), reads=[ctok], writes=[dtok])
                                cur, ctok = dst, dtok
                                lo, hi = nlo, nhi
                                m *= 2
                                step += 1
                            T.op('dve', lambda e, cur=cur, w=w: e.tensor_scalar(
                                out=cur[:, :, 8:8 + L], in0=cur[:, :, 8:8 + L], scalar1=1.0 / w, scalar2=None, op0=ALU.mult),
                                reads=[ctok], writes=[ctok])
                            T.op('dve', lambda e, cur=cur, g=g: e.tensor_tensor(
                                out=cur[:, :, 8:16], in0=cur[:, :, 8:16],
                                in1=edger[:, g, 0:8].unsqueeze(1).to_broadcast([128, nseq, 8]), op=ALU.mult),
                                reads=[ctok, 'edger'], writes=[ctok])
                            T.op('dve', lambda e, cur=cur, g=g: e.tensor_tensor(
                                out=cur[:, :, L:L + 8], in0=cur[:, :, L:L + 8],
                                in1=edger[:, g, 8:16].unsqueeze(1).to_broadcast([128, nseq, 8]), op=ALU.mult),
                                reads=[ctok, 'edger'], writes=[ctok])
                            T.op('dve', lambda e, cur=cur: e.tensor_tensor(
                                out=pooled[:, :].rearrange("p (a b) -> p a b", b=L), in0=cur[:, :, 8:8 + L],
                                in1=up[:, :, 8:8 + L], op=ALU.subtract), reads=[ctok, 'up'], writes=['pooled'])
                            for ti in range(s.ntile):
                                ps, pt = bank()
                                T.op('pe', lambda e, ps=ps, g=g, ti=ti: e.matmul(
                                    ps[:, :], lhsT=pwt[:, g, :], rhs=pooled[:, ti * 512:(ti + 1) * 512], start=True, stop=True),
                                    reads=['poolw', 'pooled'], writes=[pt])
                                T.op('act', lambda e, ps=ps, g=g, ti=ti: e.activation(
                                    out=ych[:, ti * 512:(ti + 1) * 512], in_=ps[:, :], func=AF.Identity,
                                    scale=pvc("pool_scale", g)), reads=[pt, 'pv'], writes=['ychA'])
                            T.dma('sp', yTd[g * 128:(g + 1) * 128, :], ych[:, :], reads=['ychA'], writes=[T.new('yT_' + s.name)])
                        T.barrier()

                    with ExitStack() as sub:
                        xbp = sb(sub, "xbp", [128, nseq, L + 3], F32)
                        xf = sb(sub, "xf", [128, nseq, L], F32)
                        xfb = sb(sub, "xfb", [128, Tn], BF16)
                        G = [sb(sub, "G%d" % i, [128, nseq, L], F32) for i in range(4)]
                        om = sb(sub, "om", [128, nseq, L], F32)
                        hf = sb(sub, "hf", [128, nseq, L], F32)
                        gg = sb(sub, "gg", [128, nseq, L], F32)
                        ych = sb(sub, "ychB", [128, Tn], BF16)
                        T.op('pool', lambda e: e.memset(xbp[:], 0.0), writes=['xbp'])
                        slab_x, stx = get_slab(1)
                        slab_g, stg_ = get_slab(2)

                        def flat(b):
                            return b[:].rearrange("p a b -> p (a b)")
                        for cb in range(4):
                            for ti in range(s.ntile):
                                ps, pt = proj_fm(slab_x, stx, cb, ti)
                                copy_op(evac_engine(ti), tview(xbp, 2, ti), pview(ps), [pt], ['xbp'])
                            for ti in range(s.ntile):
                                ps, pt = proj_fm(slab_g, stg_, cb, ti)
                                gv = tview(gg, 0, ti)
                                ov = tview(om, 0, ti)
                                T.op('act', lambda e, ov=ov, ps=ps: e.activation(out=ov, in_=pview(ps), func=AF.Square),
                                     reads=[pt], writes=['om'])
                                T.op('dve', lambda e, ov=ov: e.tensor_scalar(out=ov, in0=ov, scalar1=0.044715, scalar2=1.0,
                                                                            op0=ALU.mult, op1=ALU.add), reads=['om'], writes=['om'])
                                T.op('dve', lambda e, ov=ov, ps=ps: e.tensor_tensor(out=ov, in0=ov, in1=pview(ps), op=ALU.mult),
                                     reads=['om', pt], writes=['om'])
                                T.op('act', lambda e, ov=ov: e.activation(out=ov, in_=ov, func=AF.Sigmoid, scale=1.5957691216057308),
                                     reads=['om'], writes=['om'])
                                T.op('dve', lambda e, ov=ov, gv=gv, ps=ps: e.tensor_tensor(out=gv, in0=ov, in1=pview(ps), op=ALU.mult),
                                     reads=['om', pt], writes=['gg'])
                            T.op('dve', lambda e, cb=cb: e.tensor_scalar(
                                out=xf[:], in0=xbp[:, :, 0:L], scalar1=pvc("lru_conv_w", 0 * 4 + cb), scalar2=pvc("lru_conv_b", cb),
                                op0=ALU.mult, op1=ALU.add), reads=['xbp', 'pv'], writes=['xf'])
                            for jt in range(1, 4):
                                T.op('dve', lambda e, cb=cb, jt=jt: e.scalar_tensor_tensor(
                                    out=xf[:], in0=xbp[:, :, jt:jt + L], scalar=pvc("lru_conv_w", jt * 4 + cb), in1=xf[:],
                                    op0=ALU.mult, op1=ALU.add), reads=['xbp', 'xf', 'pv'], writes=['xf'])
                            T.op('act', lambda e: e.activation(out=xfb[:, :], in_=flat(xf), func=AF.Identity),
                                 reads=['xf'], writes=['xfb'])
                            for dg in range(4):
                                for ti in range(s.ntile):
                                    ps, pt = bank()
                                    T.op('pe', lambda e, ps=ps, dg=dg, ti=ti, cb=cb: e.matmul(
                                        ps[:, :], lhsT=gw[:, dg * 4 + cb, :], rhs=xfb[:, ti * 512:(ti + 1) * 512],
                                        start=True, stop=True), reads=['gw', 'xfb'], writes=[pt])
                                    T.op('act', lambda e, ps=ps, dg=dg, ti=ti, cb=cb: e.activation(
                                        out=tview(G[dg], 0, ti), in_=pview(ps), func=AF.Sigmoid,
                                        bias=pvc("lru_gate_b", dg * 4 + cb), scale=1.0), reads=[pt, 'pv'], writes=[('G', dg)])
                            for d in range(2):
                                R_, I_ = G[d * 2], G[d * 2 + 1]
                                rt, it_ = ('G', d * 2), ('G', d * 2 + 1)
                                T.op('act', lambda e, d=d, cb=cb, R_=R_: e.activation(
                                    out=om[:], in_=R_[:], func=AF.Exp, scale=nsp[:, 1, d * 4 + cb:d * 4 + cb + 1]),
                                    reads=[rt, 'nsp'], writes=['om'])
                                T.op('act', lambda e, d=d, cb=cb, R_=R_: e.activation(
                                    out=R_[:], in_=R_[:], func=AF.Exp, scale=nsp[:, 0, d * 4 + cb:d * 4 + cb + 1]),
                                    reads=[rt, 'nsp'], writes=[rt])
                                T.op('dve', lambda e: e.tensor_scalar(out=om[:], in0=om[:], scalar1=-1.0, scalar2=1.0,
                                                                      op0=ALU.mult, op1=ALU.add), reads=['om'], writes=['om'])
                                T.op('dve', lambda e: e.tensor_scalar(out=om[:], in0=om[:], scalar1=1e-30, scalar2=None,
                                                                      op0=ALU.max), reads=['om'], writes=['om'])
                                T.op('act', lambda e: e.activation(out=om[:], in_=om[:], func=AF.Sqrt), reads=['om'], writes=['om'])
                                T.op('dve', lambda e, I_=I_: e.tensor_tensor(out=I_[:], in0=I_[:], in1=om[:], op=ALU.mult),
                                     reads=[it_, 'om'], writes=[it_])
                                T.op('dve', lambda e, I_=I_: e.tensor_tensor(out=I_[:], in0=I_[:], in1=xf[:], op=ALU.mult),
                                     reads=[it_, 'xf'], writes=[it_])
                                dst = hf if d == 0 else om
                                dtok = 'hf' if d == 0 else 'om'
                                for sq_ in range(nseq):
                                    if s.sample:
                                        col = l * 8 + d * 4 + cb
                                        init = h0all[:, col:col + 1]
                                    else:
                                        init = 0.0
                                    if d == 0:
                                        T.op('dve', lambda e, sq_=sq_, init=init, R_=R_, I_=I_, dst=dst: e.tensor_tensor_scan(
                                            out=dst[:, sq_, :], data0=R_[:, sq_, :], data1=I_[:, sq_, :], initial=init,
                                            op0=ALU.mult, op1=ALU.add), reads=[rt, it_, 'h0all'], writes=[dtok])
                                    else:
                                        T.op('dve', lambda e, sq_=sq_, init=init, R_=R_, I_=I_, dst=dst: e.tensor_tensor_scan(
                                            out=dst[:, sq_, ::-1], data0=R_[:, sq_, ::-1], data1=I_[:, sq_, ::-1], initial=init,
                                            op0=ALU.mult, op1=ALU.add), reads=[rt, it_, 'h0all'], writes=[dtok])
                                if not s.sample:
                                    src = hf[:, :, L - 1] if d == 0 else om[:, :, 0]
                                    dsto = bass.AP(nst_out.tensor, l * 1024 + d * 512 + cb * 128, [[1, 128], [DEPTH * 1024, nseq]])
                                    T.dma('sp', dsto, src, reads=[dtok], writes=[], allow_slow_non_contiguous=True)
                            T.op('dve', lambda e: e.tensor_tensor(out=hf[:], in0=hf[:], in1=om[:], op=ALU.add),
                                 reads=['hf', 'om'], writes=['hf'])
                            T.op('dve', lambda e: e.tensor_tensor(out=ych[:, :], in0=flat(hf), in1=flat(gg), op=ALU.mult),
                                 reads=['hf', 'gg'], writes=['ychB'])
                            T.dma('sp', yTd[(4 + cb) * 128:(5 + cb) * 128, :], ych[:, :], reads=['ychB'], writes=[T.new('yT_' + s.name)])
                        T.barrier()

                    with ExitStack() as sub:
                        cpad = sb(sub, "cpad", [128, 4, nseq, L + 30], BF16)
                        dgm = sb(sub, "dgm", [128, 124, 128], BF16)
                        sg = [sb(sub, "sg", [128, 512], F32) for _ in range(2)]
                        NP_ = 256
                        cvt_r = [sb(sub, "cvt", [128, 4, NP_], F32) for _ in range(2)]
                        sqt = sb(sub, "sqt", [128, 4, NP_], F32)
                        mt_r = [sb(sub, "mt", [128, NP_], F32) for _ in range(2)]
                        m2_r = [sb(sub, "m2", [128, NP_], F32) for _ in range(2)]
                        rs_r = [sb(sub, "rsC", [128, NP_], F32) for _ in range(2)]
                        sl_r = [sb(sub, "sl", [128, 4, NP_], BF16) for _ in range(2)]
                        ychp = [sb(sub, "ychC", [128, 4, NP_], BF16) for _ in range(2)]
                        pww = sb(sub, "pww", [128, 4, 512], BF16)
                        load_w(pww[:], pw_w[l].rearrange("(c p) e -> p c e", p=128), 'pww')
                        T.op('pool', lambda e: e.memset(cpad[:], 0.0), writes=['cpad'])
                        dww = pvr("cm_dw_w")
                        for idx in range(124):
                            T.op('pool' if idx % 2 else 'dve', lambda e, idx=idx: e.tensor_scalar(
                                out=dgm[:, idx, :], in0=ident_b, scalar1=dww[:, idx:idx + 1], scalar2=None, op0=ALU.mult),
                                reads=['cstb', 'pv'], writes=['dgm'])
                        slab_a, sta = get_slab(3)
                        slab_g, stg_ = get_slab(4)
                        for cc in range(4):
                            for ti in range(s.ntile):
                                psa, pta = proj_fm(slab_a, sta, cc, ti)
                                psg, ptg = proj_fm(slab_g, stg_, cc, ti)
                                r = ti % 2
                                T.op('act', lambda e, r=r, psg=psg: e.activation(out=sg[r][:], in_=psg[:, :], func=AF.Sigmoid),
                                     reads=[ptg], writes=[('sg', r)])
                                sgv = sg[r][:, :] if L >= 512 else sg[r][:, :].rearrange("p (a b) -> p a b", b=L)
                                T.op('dve', lambda e, cc=cc, ti=ti, psa=psa, sgv=sgv: e.tensor_tensor(
                                    out=tview(cpad[:, cc], 15, ti), in0=pview(psa), in1=sgv, op=ALU.mult),
                                    reads=[pta, ('sg', r)], writes=['cpad'])
                        pieces = []
                        for sq_ in range(nseq):
                            for off in range(0, L, NP_):
                                pieces.append((sq_, off))
                        n = NP_

                        def cm_s1(pci):
                            sq_, off = pieces[pci]
                            rr = pci % 2
                            cvt = cvt_r[rr]
                            for cc in range(4):
                                ps, pt = bank()

                                def emit(e, ps=ps, cc=cc, sq_=sq_, off=off):
                                    ins = None
                                    for jt in range(31):
                                        ins = e.matmul(ps[:, 0:n], lhsT=dgm[:, jt * 4 + cc, :],
                                                       rhs=cpad[:, cc, sq_, off + jt:off + jt + n], start=(jt == 0), stop=(jt == 30))
                                    return ins
                                T.op('pe', emit, reads=['dgm', 'cpad'], writes=[pt])
                                T.op('act', lambda e, ps=ps, cc=cc, cvt=cvt: e.activation(
                                    out=cvt[:, cc, 0:n], in_=ps[:, 0:n], func=AF.Identity, bias=pvc("cm_dw_b", cc), scale=1.0),
                                    reads=[pt, 'pv'], writes=[('cvt', rr)])

                        def cm_s2(pci):
                            rr = pci % 2
                            cvt, mt, m2, rs, sl = cvt_r[rr], mt_r[rr], m2_r[rr], rs_r[rr], sl_r[rr]
                            ct, mtk, m2k, rsk, slk = ('cvt', rr), ('mt', rr), ('m2', rr), ('rsC', rr), ('sl', rr)
                            T.op('dve', lambda e: e.tensor_tensor(out=sqt[:, :, 0:n], in0=cvt[:, :, 0:n], in1=cvt[:, :, 0:n],
                                                                  op=ALU.mult), reads=[ct], writes=['sqt'])
                            psm_, ptm_ = bank()
                            pss_, pts_ = bank()

                            def emit(e):
                                ins = None
                                for cc in range(4):
                                    ins = e.matmul(psm_[:, 0:n], lhsT=ones_f, rhs=cvt[:, cc, 0:n], start=(cc == 0), stop=(cc == 3))
                                return ins
                            T.op('pe', emit, reads=[ct, 'cst'], writes=[ptm_])

                            def emit(e):
                                ins = None
                                for cc in range(4):
                                    ins = e.matmul(pss_[:, 0:n], lhsT=ones_f, rhs=sqt[:, cc, 0:n], start=(cc == 0), stop=(cc == 3))
                                return ins
                            T.op('pe', emit, reads=['sqt', 'cst'], writes=[pts_])
                            T.op('act', lambda e: e.activation(out=mt[:, 0:n], in_=psm_[:, 0:n], func=AF.Identity,
                                                               scale=1.0 / 512), reads=[ptm_], writes=[mtk])
                            T.op('dve', lambda e: e.tensor_tensor(out=m2[:, 0:n], in0=mt[:, 0:n], in1=mt[:, 0:n], op=ALU.mult),
                                 reads=[mtk], writes=[m2k])
                            T.op('dve', lambda e: e.scalar_tensor_tensor(
                                out=m2[:, 0:n], in0=pss_[:, 0:n], scalar=1.0 / 512, in1=m2[:, 0:n], op0=ALU.mult, op1=ALU.subtract),
                                reads=[pts_, m2k], writes=[m2k])
                            T.op('act', lambda e: e.activation(out=rs[:, 0:n], in_=m2[:, 0:n], func=AF.Sqrt, bias=epst[:, 0:1],
                                                               scale=1.0), reads=[m2k, 'epst'], writes=[rsk])
                            T.op('dve', lambda e: e.reciprocal(out=rs[:, 0:n], in_=rs[:, 0:n]), reads=[rsk], writes=[rsk])
                            T.op('dve', lambda e: e.tensor_tensor(
                                out=cvt[:, :, 0:n], in0=cvt[:, :, 0:n], in1=mt[:, 0:n].unsqueeze(1).to_broadcast([128, 4, n]),
                                op=ALU.subtract), reads=[ct, mtk], writes=[ct])
                            T.op('dve', lambda e: e.tensor_tensor(
                                out=cvt[:, :, 0:n], in0=cvt[:, :, 0:n], in1=rs[:, 0:n].unsqueeze(1).to_broadcast([128, 4, n]),
                                op=ALU.mult), reads=[ct, rsk], writes=[ct])
                            for cc in range(4):
                                T.op('act', lambda e, cc=cc: e.activation(
                                    out=sl[:, cc, 0:n], in_=cvt[:, cc, 0:n], func=AF.Silu, bias=pvc("cm_ln_b", cc),
                                    scale=pvc("cm_ln_g", cc)), reads=[ct, 'pv'], writes=[slk])

                        def cm_s3(pci):
                            sq_, off = pieces[pci]
                            rr = pci % 2
                            sl = sl_r[rr]
                            slk = ('sl', rr)
                            t_abs = sq_ * L + off
                            ych = ychp[rr]
                            ytok = ('ychC', rr)
                            for ec in range(4):
                                ps, pt = bank()

                                def emit(e, ps=ps, ec=ec):
                                    ins = None
                                    for cc in range(4):
                                        ins = e.matmul(ps[:, 0:n], lhsT=pww[:, cc, ec * 128:(ec + 1) * 128], rhs=sl[:, cc, 0:n],
                                                       start=(cc == 0), stop=(cc == 3))
                                    return ins
                                T.op('pe', emit, reads=['pww', slk], writes=[pt])
                                T.op('act' if ec % 2 == 0 else 'dve', (lambda e, ps=ps, ec=ec: e.activation(
                                    out=ych[:, ec, 0:n], in_=ps[:, 0:n], func=AF.Identity, bias=pvc("cm_pw_b", ec), scale=1.0))
                                    if ec % 2 == 0 else (lambda e, ps=ps, ec=ec: e.tensor_scalar(
                                        out=ych[:, ec, 0:n], in0=ps[:, 0:n], scalar1=pvc("cm_pw_b", ec), scalar2=None,
                                        op0=ALU.add)), reads=[pt, 'pv'], writes=[ytok])
                            T.dma('sp', yTd[8 * 128:12 * 128, t_abs:t_abs + n].rearrange("(e p) t -> p e t", p=128), ych[:, :, 0:n],
                                  reads=[ytok], writes=[T.new('yT_' + s.name)])

                        cm_s1(0)
                        for pci in range(len(pieces)):
                            if pci + 1 < len(pieces):
                                cm_s1(pci + 1)
                            cm_s2(pci)
                            cm_s3(pci)
                        T.barrier()

                    with ExitStack() as sub:
                        nTk = Tn // 128
                        if s.sample:
                            Rt = sb(sub, "Rt", [128, 9, 8, 128], BF16)
                            ckT = sb(sub, "ckT", [128, 4, 512], BF16)
                            cvb = sb(sub, "cvb", [128, 4, 512], BF16)
                            T.dma('pool', cvb[:], cv_in[l].rearrange("(a p) f -> p a f", p=128), writes=['cvb'])
                            with ExitStack() as sub2:
                                ckf = sb(sub2, "ckf", [128, 4, 512], F32)
                                maskr = sb(sub2, "maskr", [128, 9, 128], F32)
                                rp = sb(sub2, "rp", [120, 31], F32)
                                zp = sb(sub2, "zp", [120, 128], F32)
                                HK = sb(sub2, "HK", [128, 120, 64], F32)
                                T.dma('sp', ckf[:], ck_in[l].rearrange("(a p) f -> p a f", p=128), writes=['ckf'])
                                T.dma('sp', maskr[:], maskr_in[:, :, :], writes=['maskr'])
                                for c in range(4):
                                    ps, pt = bank()

                                    def emit(e, ps=ps, c=c):
                                        ins = None
                                        for a_ in range(4):
                                            ins = e.transpose(out=ps[:, a_ * 128:(a_ + 1) * 128], in_=ckf[:, a_, c * 128:(c + 1) * 128],
                                                              identity=ident)
                                        return ins
                                    T.op('pe', emit, reads=['ckf', 'cst'], writes=[pt])
                                    copy_op('act', ckT[:, c, :], ps[:, :], [pt], ['ckT'])
                                T.dma('sp', rp[:], rpb_in[l], writes=['rp'])
                                T.op('pool', lambda e: e.memset(zp[:], 0.0), writes=['zp'])
                                T.op('dve', lambda e: e.tensor_copy(out=zp[:, 48:79], in_=rp[:, ::-1]), reads=['rp', 'zp'], writes=['zp'])
                                T.dma('sp', zpd[:, :], zp[:], reads=['zp'], writes=['zpd'])
                                for half in range(2):
                                    for q4 in range(4):
                                        src = bass.AP(zpd.tensor, q4 * 30 * 128, [[1, 64], [128, 30], [1, 64]])
                                        T.dma('sp', HK[half * 64:(half + 1) * 64, q4 * 30:(q4 + 1) * 30, :], src,
                                              reads=['zpd'], writes=[T.new('HK')])
                                HK4 = HK[:].rearrange("p (h r) q -> p h r q", r=15)
                                k_ = 0
                                for ty, base in enumerate(RT_BASES):
                                    for pr in range(2):
                                        for qr in range(2):
                                            dr = base + pr - qr
                                            pa = (1 - pr) * 64
                                            eng = 'dve' if k_ % 2 == 0 else 'pool'
                                            k_ += 1
                                            mv = maskr[pa:pa + 64, ty, qr * 64:(qr + 1) * 64].unsqueeze(1).to_broadcast([64, 8, 64])
                                            ov = Rt[pa:pa + 64, ty, :, qr * 64:(qr + 1) * 64]
                                            if abs(dr) <= 7:
                                                T.op(eng, lambda e, ov=ov, mv=mv, pa=pa, dr=dr: e.tensor_tensor(
                                                    out=ov, in0=HK4[pa:pa + 64, :, dr + 7, :], in1=mv, op=ALU.add),
                                                    reads=T.all('HK') + ['maskr'], writes=[('Rt', k_)])
                                            else:
                                                T.op(eng, lambda e, ov=ov, mv=mv: e.tensor_copy(out=ov, in_=mv),
                                                     reads=['maskr'], writes=[('Rt', k_)])
                                T.barrier()
                        qT = sb(sub, "qT", [128, 4, Tn], BF16)
                        kT = sb(sub, "kT", [128, 4, Tn], BF16)
                        Vt = sb(sub, "Vt", [128, nTk, 512], BF16)
                        stage = [sb(sub, "kvst", [128, 512], F32) for _ in range(2)]
                        stc = [0]
                        slab_q, stq = get_slab(5)
                        for c in range(4):
                            for ti in range(s.ntile):
                                ps, pt = proj_fm(slab_q, stq, c, ti)
                                T.op('act', lambda e, ps=ps, c=c, ti=ti: e.activation(
                                    out=qT[:, c, ti * 512:(ti + 1) * 512], in_=ps[:, :], func=AF.Identity, scale=0.125),
                                    reads=[pt], writes=['qT'])
                        slab_k, stk = get_slab(6)
                        for c in range(4):
                            for ti in range(s.ntile):
                                ps, pt = proj_fm(slab_k, stk, c, ti)
                                copy_op('dve', kT[:, c, ti * 512:(ti + 1) * 512], ps[:, :], [pt], ['kT'])
                        T.mark(1)

                        def proj_tm(slab, stok, tk):
                            ps, pt = bank()

                            def emit(e):
                                ins = None
                                for k in range(16):
                                    ins = e.matmul(ps[:, :], lhsT=hT[:, k, tk * 128:(tk + 1) * 128], rhs=slab[:, k, :],
                                                   start=(k == 0), stop=(k == 15))
                                return ins
                            T.op('pe', emit, reads=[stok, ('hT', tk // 4)], writes=[pt])
                            return ps, pt
                        if not s.sample:
                            for tk in range(nTk):
                                ps, pt = proj_tm(slab_k, stk, tk)
                                r = stc[0] % 2
                                stc[0] += 1
                                copy_op('act', stage[r][:], ps[:, :], [pt], [('kvst', r)])
                                T.dma('sp', nk_out[tk // 2, l, (tk % 2) * 128:(tk % 2 + 1) * 128, :], stage[r][:],
                                      reads=[('kvst', r)], writes=[])
                        T.mark(2)
                        slab_v, stv = get_slab(7)
                        for tk in range(nTk):
                            ps, pt = proj_tm(slab_v, stv, tk)
                            copy_op('dve', Vt[:, tk, :], ps[:, :], [pt], ['Vt'])
                            if not s.sample:
                                r = stc[0] % 2
                                stc[0] += 1
                                copy_op('act', stage[r][:], ps[:, :], [pt], [('kvst', r)])
                                T.dma('sp', nv_out[tk // 2, l, (tk % 2) * 128:(tk % 2 + 1) * 128, :], stage[r][:],
                                      reads=[('kvst', r)], writes=[])
                        T.mark(3)
                        pT = [sb(sub, "pT", [128, 9 * 128], BF16) for _ in range(2)]
                        rsd = [sb(sub, "rsd", [64, 256], F32) for _ in range(2)]
                        ones64 = cstb[:, 256:320]
                        if not s.sample:
                            ychs = [sb(sub, "ychD", [128, 4, 256], BF16) for _ in range(2)]
                            it2 = 0
                            for sq_ in range(nseq):
                                ych = ychs[sq_ % 2]
                                ytok = ('ychD', sq_ % 2)
                                for h in range(8):
                                    c, po = h // 2, (h % 2) * 64
                                    r = it2 % 2
                                    it2 += 1
                                    ps, pt = bank()

                                    def emit(e, ps=ps, c=c, po=po, sq_=sq_):
                                        ins = None
                                        for kc in range(2):
                                            ins = e.matmul(ps[:, kc * 256:(kc + 1) * 256],
                                                           lhsT=kT[po:po + 64, c, sq_ * 256 + kc * 128:sq_ * 256 + (kc + 1) * 128],
                                                           rhs=qT[po:po + 64, c, sq_ * 256:(sq_ + 1) * 256], start=True, stop=True)
                                        return ins
                                    T.op('pe', emit, reads=['kT', 'qT'], writes=[pt])
                                    T.op('act', lambda e, ps=ps, r=r: e.activation(out=pT[r][:, 0:512], in_=ps[:, :], func=AF.Exp),
                                         reads=[pt], writes=[('pT', r)])
                                    T.mark(4)
                                    pso, pto = bank()

                                    def emit(e, pso=pso, r=r, h=h, sq_=sq_):
                                        ins = None
                                        for kc in range(2):
                                            ins = e.matmul(pso[0:64, 0:256], lhsT=Vt[:, sq_ * 2 + kc, h * 64:(h + 1) * 64],
                                                           rhs=pT[r][:, kc * 256:(kc + 1) * 256], start=(kc == 0), stop=(kc == 1))
                                        for kc in range(2):
                                            ins = e.matmul(pso[0:64, 256:512], lhsT=ones64,
                                                           rhs=pT[r][:, kc * 256:(kc + 1) * 256], start=(kc == 0), stop=(kc == 1))
                                        return ins
                                    T.op('pe', emit, reads=['Vt', ('pT', r), 'cstb'], writes=[pto])
                                    T.mark(5)
                                    T.op('dve', lambda e, pso=pso, r=r: e.reciprocal(out=rsd[r][:, 0:256], in_=pso[0:64, 256:512]),
                                         reads=[pto], writes=[('rsd', r)])
                                    T.mark(6)
                                    T.op('dve', lambda e, pso=pso, r=r, c=c, po=po, ych=ych: e.tensor_tensor(
                                        out=ych[po:po + 64, c, :], in0=pso[0:64, 0:256],
                                        in1=rsd[r][:, 0:256], op=ALU.mult), reads=[pto, ('rsd', r)], writes=[ytok])
                                T.dma('sp', yTd[12 * 128:16 * 128, sq_ * 256:(sq_ + 1) * 256].rearrange("(e p) t -> p e t", p=128), ych[:],
                                      reads=[ytok], writes=[T.new('yT_' + s.name)])
                        else:
                            ychs = [sb(sub, "ychD", [128, 4, 128], BF16) for _ in range(2)]
                            rt_all = [('Rt', k_) for k_ in range(1, 37)]
                            items = [(qc, h) for qc in range(16) for h in range(8)]

                            def att_s1(ii):
                                qc, h = items[ii]
                                nb = nb_config(qc)
                                ntile_ = len(nb) + 4
                                c, po = h // 2, (h % 2) * 64
                                r = ii % 2
                                nbk = (ntile_ + 3) // 4
                                banks = [bank() for _ in range(nbk)]
                                qv = qT[po:po + 64, c, qc * 128:(qc + 1) * 128]
                                for bi, (ps, pt) in enumerate(banks):
                                    def emit(e, ps=ps, bi=bi):
                                        ins = None
                                        for i in range(bi * 4, min(ntile_, bi * 4 + 4)):
                                            o = ps[:, (i % 4) * 128:(i % 4 + 1) * 128]
                                            if i < len(nb):
                                                kc, ty = nb[i]
                                                e.matmul(o, lhsT=kT[po:po + 64, c, kc * 128:(kc + 1) * 128], rhs=qv, start=True, stop=False)
                                                ins = e.matmul(o, lhsT=flip_b, rhs=Rt[:, ty, h, :], start=False, stop=True)
                                            else:
                                                a_ = i - len(nb)
                                                ins = e.matmul(o, lhsT=ckT[po:po + 64, c, a_ * 128:(a_ + 1) * 128], rhs=qv, start=True, stop=True)
                                        return ins
                                    T.op('pe', emit, reads=['kT', 'qT', 'ckT', 'cstb'] + rt_all, writes=[pt])
                                    n_ = min(ntile_, bi * 4 + 4) - bi * 4
                                    T.op('act', lambda e, ps=ps, bi=bi, n_=n_: e.activation(
                                        out=pT[r][:, bi * 512:bi * 512 + n_ * 128], in_=ps[:, 0:n_ * 128], func=AF.Exp),
                                        reads=[pt], writes=[('pT', r)])

                            def att_s2(ii):
                                qc, h = items[ii]
                                nb = nb_config(qc)
                                ntile_ = len(nb) + 4
                                c, po = h // 2, (h % 2) * 64
                                r = ii % 2
                                ych = ychs[qc % 2]
                                ytok = ('ychD', qc % 2)
                                pso, pto = psb[6 + ii % 2], ('ps', 6 + ii % 2)

                                def emit(e):
                                    ins = None
                                    for i in range(ntile_):
                                        if i < len(nb):
                                            lhs = Vt[:, nb[i][0], h * 64:(h + 1) * 64]
                                        else:
                                            lhs = cvb[:, i - len(nb), h * 64:(h + 1) * 64]
                                        ins = e.matmul(pso[0:64, 0:128], lhsT=lhs, rhs=pT[r][:, i * 128:(i + 1) * 128],
                                                       start=(i == 0), stop=(i == ntile_ - 1))
                                    for i in range(ntile_):
                                        ins = e.matmul(pso[0:64, 128:256], lhsT=ones64, rhs=pT[r][:, i * 128:(i + 1) * 128],
                                                       start=(i == 0), stop=(i == ntile_ - 1))
                                    return ins
                                T.op('pe', emit, reads=['Vt', 'cvb', ('pT', r), 'cstb'], writes=[pto])
                                T.op('dve', lambda e: e.reciprocal(out=rsd[r][:, 0:128], in_=pso[0:64, 128:256]),
                                     reads=[pto], writes=[('rsd', r)])
                                T.op('dve', lambda e: e.tensor_tensor(
                                    out=ych[po:po + 64, c, :], in0=pso[0:64, 0:128],
                                    in1=rsd[r][:, 0:128], op=ALU.mult), reads=[pto, ('rsd', r)], writes=[ytok])
                                if h == 7:
                                    T.dma('sp', yTd[12 * 128:16 * 128, qc * 128:(qc + 1) * 128].rearrange("(e p) t -> p e t", p=128), ych[:],
                                          reads=[ytok], writes=[T.new('yT_' + s.name)])

                            att_s1(0)
                            for ii in range(len(items)):
                                if ii + 1 < len(items):
                                    att_s1(ii + 1)
                                att_s2(ii)
                        T.barrier()
                    T.barrier()

                with ExitStack() as ph:
                    wo = sb(ph, "wo", [128, 16, D], BF16)
                    wov = w_out[l].rearrange("(k p) n -> p k n", p=128)
                    for q4 in range(2):
                        load_w(wo[:, :, q4 * 1024:(q4 + 1) * 1024], wov[:, :, q4 * 1024:(q4 + 1) * 1024], ('wo', q4))
                    yt = [sb(ph, "yt", [128, NCH, 512], BF16) for _ in range(2)]
                    NXC = 6
                    xc = [sb(ph, "o_xc", [128, 512], F32) for _ in range(NXC)]
                    mix = [sb(ph, "mix", [128, NCH, 512], F32) for _ in range(2)]
                    sqn = [sb(ph, "sqn", [128, 512], F32) for _ in range(2)]
                    rstd = [sb(ph, "o_rstd", [128, 512], F32) for _ in range(2)]
                    xTv = xT[s.name].rearrange("(c p) t -> p c t", p=128)
                    yTv = yTd.rearrange("(c p) t -> p c t", p=128)
                    xci = [0]

                    def o_load(ti):
                        r = ti % 2
                        T.dma('sp', yt[r][:], yTv[:, :, ti * 512:(ti + 1) * 512], reads=[], writes=[('yt', r)])

                    def o_mm(ti, n):
                        r = ti % 2
                        pss, ptss = psb[6 + r], ('ps', 6 + r)
                        ps, pt = bank()

                        def emit(e, ps=ps, n=n, r=r):
                            ins = None
                            for k in range(16):
                                ins = e.matmul(ps[:, :], lhsT=wo[:, k, n * 128:(n + 1) * 128], rhs=yt[r][:, k, :],
                                               start=(k == 0), stop=(k == 15))
                            return ins
                        T.op('pe', emit, reads=[('wo', n // 8), ('yt', r)], writes=[pt])
                        r2 = n % 2
                        T.op('act', lambda e, ps=ps, r2=r2: e.activation(out=sqn[r2][:], in_=ps[:, :], func=AF.Square),
                             reads=[pt], writes=[('sqn', r2)])
                        copy_op('dve', mix[r][:, n, :], ps[:, :], [pt], [('mix', r)])
                        T.op('pe', lambda e, pss=pss, r2=r2, n=n: e.matmul(pss[:, :], lhsT=ones_f, rhs=sqn[r2][:],
                                                                         start=(n == 0), stop=(n == NCH - 1)),
                             reads=[('sqn', r2), 'cst'], writes=[ptss])
                        if n == NCH - 1:
                            rstd_from_ps(pss, ptss, rstd[r], ('o_rstd', r))

                    def o_post(ti, c):
                        r = ti % 2
                        q = xci[0] % NXC
                        xci[0] += 1
                        T.dma('sp', xc[q][:], xTv[:, c, ti * 512:(ti + 1) * 512], reads=[], writes=[('o_xc', q)])
                        T.op('dve', lambda e: e.tensor_tensor(out=mix[r][:, c, :], in0=mix[r][:, c, :], in1=rstd[r][:, :], op=ALU.mult),
                             reads=[('mix', r), ('o_rstd', r)], writes=[('mix', r)])
                        T.op('dve', lambda e: e.scalar_tensor_tensor(
                            out=xc[q][:], in0=mix[r][:, c, :], scalar=modv[:, 1, c, j:j + 1], in1=xc[q][:],
                            op0=ALU.mult, op1=ALU.add), reads=[('mix', r), ('o_xc', q), 'modv'], writes=[('o_xc', q)])
                        T.dma('pool', xTv[:, c, ti * 512:(ti + 1) * 512], xc[q][:], reads=[('o_xc', q)], writes=[T.new('o_st')])

                    o_load(0)
                    if s.ntile > 1:
                        o_load(1)
                    for n in range(NCH):
                        o_mm(0, n)
                    for ti in range(s.ntile):
                        if ti + 2 < s.ntile:
                            o_load(ti + 2)
                        for n in range(NCH):
                            if ti + 1 < s.ntile:
                                o_mm(ti + 1, n)
                            o_post(ti, n)
                    T.barrier()

                for blk in range(Tn // 1024):
                    with ExitStack() as ph:
                        h2T = sb(ph, "h2T", [128, NCH, 1024], BF16)
                        gT = sb(ph, "gT", [128, NF, 1024], BF16)
                        rstd2 = [sb(ph, "f_rstd", [128, 512], F32) for _ in range(2)]
                        with ExitStack() as sub:
                            norm_phase(sub, s, l, blk * 2, 2, 2, 3, h2T, 'h2T', NT=256, nsq=1)
                            T.barrier()
                        with ExitStack() as sub:
                            FSW = 256
                            w1s = [sb(sub, "w1s", [128, 16, FSW], BF16) for _ in range(2)]
                            w3s = [sb(sub, "w3s", [128, 16, FSW], BF16) for _ in range(2)]
                            sil = [sb(sub, "sil", [128, 512], F32) for _ in range(2)]
                            w1v = w1_in[l].rearrange("(k p) n -> p k n", p=128)
                            w3v = w3_in[l].rearrange("(k p) n -> p k n", p=128)
                            it2 = 0
                            for fs in range(DFF // FSW):
                                r = fs % 2
                                load_w(w1s[r][:], w1v[:, :, fs * FSW:(fs + 1) * FSW], ('w1s', r))
                                load_w(w3s[r][:], w3v[:, :, fs * FSW:(fs + 1) * FSW], ('w3s', r))
                                for fc in range(FSW // 128):
                                    f = fs * (FSW // 128) + fc
                                    for ti in range(2):
                                        psa, pta = bank()
                                        psb_, ptb = bank()

                                        def emit(e, ps=psa, w=w1s[r], fc=fc, ti=ti):
                                            ins = None
                                            for k in range(16):
                                                ins = e.matmul(ps[:, :], lhsT=w[:, k, fc * 128:(fc + 1) * 128],
                                                               rhs=h2T[:, k, ti * 512:(ti + 1) * 512], start=(k == 0), stop=(k == 15))
                                            return ins
                                        T.op('pe', emit, reads=[('w1s', r), ('h2T', ti)], writes=[pta])

                                        def emit(e, ps=psb_, w=w3s[r], fc=fc, ti=ti):
                                            ins = None
                                            for k in range(16):
                                                ins = e.matmul(ps[:, :], lhsT=w[:, k, fc * 128:(fc + 1) * 128],
                                                               rhs=h2T[:, k, ti * 512:(ti + 1) * 512], start=(k == 0), stop=(k == 15))
                                            return ins
                                        T.op('pe', emit, reads=[('w3s', r), ('h2T', ti)], writes=[ptb])
                                        r2 = it2 % 2
                                        it2 += 1
                                        T.op('act', lambda e, psa=psa, r2=r2: e.activation(out=sil[r2][:], in_=psa[:, :], func=AF.Silu),
                                             reads=[pta], writes=[('sil', r2)])
                                        T.op('dve', lambda e, psb_=psb_, r2=r2, f=f, ti=ti: e.tensor_tensor(
                                            out=gT[:, f, ti * 512:(ti + 1) * 512], in0=sil[r2][:], in1=psb_[:, :], op=ALU.mult),
                                            reads=[('sil', r2), ptb], writes=[('gT', ti)])
                            T.barrier()
                        with ExitStack() as sub:
                            w2s = [sb(sub, "w2s", [128, NF, 256], BF16) for _ in range(2)]
                            ost = [sb(sub, "ost", [128, 512], F32) for _ in range(2)]
                            sqs = [sb(sub, "sqs", [128, 512], F32) for _ in range(2)]
                            w2v = w2_in[l].rearrange("(f p) n -> p f n", p=128)
                            pss = [psb[6], psb[7]]
                            ptss = [('ps', 6), ('ps', 7)]
                            psi2 = [0]

                            def bank6():
                                i = psi2[0] % 6
                                psi2[0] += 1
                                return psb[i], ('ps', i)
                            it2 = 0
                            for ns in range(8):
                                r = ns % 2
                                load_w(w2s[r][:], w2v[:, :, ns * 256:(ns + 1) * 256], ('w2s', r))
                                for nci in range(2):
                                    n = ns * 2 + nci
                                    for ti in range(2):
                                        ps, pt = bank6()

                                        def emit(e, ps=ps, r=r, nci=nci, ti=ti):
                                            ins = None
                                            for f in range(NF):
                                                ins = e.matmul(ps[:, :], lhsT=w2s[r][:, f, nci * 128:(nci + 1) * 128],
                                                               rhs=gT[:, f, ti * 512:(ti + 1) * 512], start=(f == 0), stop=(f == NF - 1))
                                            return ins
                                        T.op('pe', emit, reads=[('w2s', r), ('gT', ti)], writes=[pt])
                                        r2 = it2 % 2
                                        it2 += 1
                                        T.op('act', lambda e, ps=ps, r2=r2: e.activation(out=sqs[r2][:], in_=ps[:, :], func=AF.Square),
                                             reads=[pt], writes=[('sqs', r2)])
                                        copy_op('dve', ost[r2][:], ps[:, :], [pt], [('ost', r2)])
                                        T.dma('sp', oT[n * 128:(n + 1) * 128, ti * 512:(ti + 1) * 512], ost[r2][:],
                                              reads=[('ost', r2)], writes=[('oT', ti)])
                                        T.op('pe', lambda e, ti=ti, r2=r2, n=n: e.matmul(pss[ti][:, :], lhsT=ones_f, rhs=sqs[r2][:],
                                                                                        start=(n == 0), stop=(n == NCH - 1)),
                                             reads=[('sqs', r2), 'cst'], writes=[ptss[ti]])
                            for ti in range(2):
                                rstd_from_ps(pss[ti], ptss[ti], rstd2[ti], ('f_rstd', ti))
                            T.barrier()
                        with ExitStack() as sub:
                            ot = sb(sub, "ot", [128, NCH, 512], F32)
                            xt = sb(sub, "f_xt", [128, NCH, 512], F32)
                            xTv = xT[s.name].rearrange("(c p) t -> p c t", p=128)
                            oTv = oT.rearrange("(c p) t -> p c t", p=128)
                            for ti in range(2):
                                tg = blk * 2 + ti
                                T.dma('sp', ot[:], oTv[:, :, ti * 512:(ti + 1) * 512], reads=[('oT', ti)], writes=['ot'])
                                T.dma('sp', xt[:], xTv[:, :, tg * 512:(tg + 1) * 512], reads=[('xT', s.name, tg)], writes=['f_xt'])
                                T.op('dve', lambda e, ti=ti: e.tensor_tensor(
                                    out=ot[:], in0=ot[:], in1=rstd2[ti][:, :].unsqueeze(1).to_broadcast([128, NCH, 512]), op=ALU.mult),
                                    reads=['ot', ('f_rstd', ti)], writes=['ot'])
                                for c in range(NCH):
                                    T.op('dve', lambda e, c=c: e.scalar_tensor_tensor(
                                        out=xt[:, c, :], in0=ot[:, c, :], scalar=modv[:, 3, c, j:j + 1], in1=xt[:, c, :],
                                        op0=ALU.mult, op1=ALU.add), reads=['ot', 'f_xt', 'modv'], writes=['f_xt'])
                                T.dma('sp', xTv[:, :, tg * 512:(tg + 1) * 512], xt[:], reads=['f_xt'], writes=[('xT', s.name, tg)])
                            T.barrier()
                        T.barrier()

        with ExitStack() as ph:
            xin = [sb(ph, "fxin", [128, NCH, 128], F32) for _ in range(2)]
            xo = [sb(ph, "fxo", [128, D], F32) for _ in range(2)]
            it = 0
            for s in segs:
                xTv = xT[s.name].rearrange("(c p) t -> p c t", p=128)
                for tc in range(s.T // 128):
                    r = it % 2
                    it += 1
                    T.dma('sp', xin[r][:], xTv[:, :, tc * 128:(tc + 1) * 128], reads=[('xT', s.name, tc // 4)], writes=[('fxin', r)])
                    for g in range(4):
                        ps, pt = bank()

                        def emit(e, ps=ps, r=r, g=g):
                            ins = None
                            for jj in range(4):
                                ins = e.transpose(out=ps[:, jj * 128:(jj + 1) * 128], in_=xin[r][:, 4 * g + jj, :], identity=ident)
                            return ins
                        T.op('pe', emit, reads=[('fxin', r), 'cst'], writes=[pt])
                        copy_op(evac_engine(g), xo[r][:, g * 512:(g + 1) * 512], ps[:, :], [pt], [('fxo', r, g)])
                    T.dma('sp', y_out[s.name][tc * 128:(tc + 1) * 128, :], xo[r][:],
                          reads=[('fxo', r, g) for g in range(4)], writes=[])
            T.barrier()


_CACHE = {}


def make_in_maps(inputs, ncores=8):
    hc = host_consts()
    f = lambda a: np.ascontiguousarray(np.asarray(a, dtype=np.float32))
    shared = {}
    for nm in ["w_ada", "b_ada", "g_pre_mix", "g_post_mix", "g_pre_ffn", "g_post_ffn", "w_in", "pool_w", "pool_scale",
               "lru_conv_b", "lru_gate_w", "cm_dw_b", "cm_ln_g", "cm_ln_b", "cm_pw_w", "cm_pw_b", "w_out",
               "ffn_w1", "ffn_w3", "ffn_w2"]:
        shared[nm] = f(inputs[nm])
    shared["lru_conv_w"] = f(inputs["lru_conv_w"]).reshape(DEPTH, 4 * 512)
    shared["lru_gate_b"] = f(inputs["lru_gate_b"]).reshape(DEPTH, 4 * 512)
    shared["lru_lambda"] = f(inputs["lru_lambda"]).reshape(DEPTH, 2 * 512)
    shared["cm_dw_w"] = f(inputs["cm_dw_w"]).reshape(DEPTH, 31 * 512)
    shared["na_rpb"] = f(inputs["na_rpb"]).reshape(DEPTH, 120, 31)
    shared.update(hc)
    xp = f(inputs["x_prompt"])
    xs = f(inputs["x_sample"])
    ck = f(inputs["cache_k"])
    cv = f(inputs["cache_v"])
    st = f(inputs["state_lru"])
    c = f(inputs["c"])
    cctx = f(inputs["c_ctx"])
    nsmp = xs.shape[0]
    maps = []
    for i in range(ncores):
        si = i % nsmp
        m = dict(shared)
        m["xp"] = np.ascontiguousarray(xp[i * NPSEQ:(i + 1) * NPSEQ].reshape(NPSEQ * LP, D))
        m["xs"] = np.ascontiguousarray(xs[si])
        m["ck"] = np.ascontiguousarray(ck[si].reshape(DEPTH, PAST, 512))
        m["cv"] = np.ascontiguousarray(cv[si].reshape(DEPTH, PAST, 512))
        m["st"] = np.ascontiguousarray(st[si].reshape(-1))
        m["cond"] = np.ascontiguousarray(np.stack([cctx, c[si]], axis=0))
        maps.append(m)
    return maps


def kernel(**inputs):
    key = "full"
    if key not in _CACHE:
        _CACHE[key] = build_program()
    nc = _CACHE[key]
    maps = make_in_maps(inputs, 8)
    res = run_bass_kernel_spmd(nc, maps, core_ids=list(range(8)))
    rs = res.results
    B = 32
    y_prompt = np.concatenate([rs[i]["yp"].reshape(NPSEQ, LP, D) for i in range(8)], axis=0).astype(np.float32)
    y_sample = np.stack([rs[0]["ys"], rs[1]["ys"]], axis=0).astype(np.float32)
    nk = np.concatenate([rs[i]["nk"] for i in range(8)], axis=0).reshape(B, DEPTH, LP, 8, 64).astype(np.float32)
    nv = np.concatenate([rs[i]["nv"] for i in range(8)], axis=0).reshape(B, DEPTH, LP, 8, 64).astype(np.float32)
    nst = np.concatenate([rs[i]["nst"] for i in range(8)], axis=0).reshape(B, DEPTH, 2, 512).astype(np.float32)
    return (y_prompt, y_sample, nk, nv, nst)
```

```python
import numpy as np
from contextlib import ExitStack
import concourse.bass as bass
import concourse.mybir as mybir
from concourse.bass_utils import run_bass_kernel_spmd

F32 = mybir.dt.float32
BF16 = mybir.dt.bfloat16
AF = mybir.ActivationFunctionType
ALU = mybir.AluOpType

D = 2048
NCH = 16
DEPTH = 4
DIN = 4096
DFF = 5632
NF = 44
NMOD = 6
PAST = 512
NEG = -30000.0
EPS = 1e-6
NPSEQ = 4
LP = 256
LS = 2048
GRID_W = 64

PV_ROWS = {}
_lay = [
    [("b_ada", 96), ("g_pre_mix", 16), ("g_post_mix", 16)],
    [("g_pre_ffn", 16), ("g_post_ffn", 16), ("pool_scale", 4), ("lru_conv_w", 16), ("lru_conv_b", 4),
     ("lru_gate_b", 16), ("lru_lambda", 8), ("cm_dw_b", 4), ("cm_ln_g", 4), ("cm_ln_b", 4), ("cm_pw_b", 4)],
    [("cm_dw_w", 124)],
]
for _ti, _names in enumerate(_lay):
    _r = 0
    for _n, _k in _names:
        PV_ROWS[_n] = (_ti, _r, _k)
        _r += _k
    assert _r <= 128


class StopBuild(Exception):
    pass


class _CntEng:
    def __init__(self, eng):
        self._e = eng
        self.n = 0

    def __getattr__(self, name):
        f = getattr(self._e, name)
        if name in ('matmul', 'transpose'):
            def g(*a, **k):
                self.n += 1
                return f(*a, **k)
            return g
        return f


class Trk:
    NDMA = 10

    def __init__(self, nc, es):
        self.nc = nc
        self.eng = {'pe': _CntEng(nc.tensor), 'act': nc.scalar, 'dve': nc.vector, 'pool': nc.gpsimd, 'sp': nc.sync}
        self.phases = []
        self.sem = {}
        self.cnt = {}
        for k in self.eng:
            self.sem[k] = es.enter_context(nc.semaphore("s_" + k))
            self.cnt[k] = 0
        self.dq = {}
        for q in ('sp', 'pool'):
            keys = []
            for i in range(self.NDMA):
                k = "d_%s_%d" % (q, i)
                self.sem[k] = es.enter_context(nc.semaphore(k))
                self.cnt[k] = 0
                keys.append(k)
            self.dq[q] = [keys, 0]
        self.seen = {k: {} for k in self.eng}
        self.lw = {}
        self.rd = {}
        self.ninst = 0
        self.grp = {}
        self.stop_at = 0
        self.stopped = False

    def new(self, grp):
        l = self.grp.setdefault(grp, [])
        t = (grp, '#', len(l))
        l.append(t)
        return t

    def all(self, grp):
        return list(self.grp.get(grp, []))

    def _deps(self, e, reads, writes):
        deps = {}

        def add(ev, raw):
            k, v = ev
            if k == e and not raw:
                return
            if deps.get(k, 0) < v:
                deps[k] = v
        for t in reads:
            if t in self.lw:
                add(self.lw[t], True)
            if isinstance(t, tuple) and t[0] == 'ps':
                for ev in self.rd.get(t, {}).items():
                    add(ev, False)
        for t in writes:
            if t in self.lw:
                add(self.lw[t], False)
            for ev in self.rd.get(t, {}).items():
                add(ev, False)
        return deps

    def _wait(self, e, deps):
        eng = self.eng[e]
        seen = self.seen[e]
        for k, v in deps.items():
            if seen.get(k, 0) < v:
                eng.wait_ge(self.sem[k], v)
                seen[k] = v
                self.ninst += 1

    def _commit(self, ev, reads, writes):
        k, v = ev
        for t in writes:
            self.lw[t] = ev
            self.rd[t] = {}
        for t in reads:
            d = self.rd.setdefault(t, {})
            if d.get(k, 0) < v:
                d[k] = v

    def op(self, e, emit, reads=(), writes=()):
        if self.stopped:
            return
        self._wait(e, self._deps(e, reads, writes))
        inst = emit(self.eng[e])
        self.cnt[e] += 1
        inst.then_inc(self.sem[e], 1)
        self.ninst += 1
        self._commit((e, self.cnt[e]), reads, writes)

    def dma(self, q, out, in_, reads=(), writes=(), **kw):
        if self.stopped:
            return
        keys, i = self.dq[q]
        k = keys[i % len(keys)]
        self.dq[q][1] = i + 1
        deps = self._deps(None, reads, writes)
        if self.cnt[k] > 0:
            deps[k] = max(deps.get(k, 0), self.cnt[k])
        self._wait(q, deps)
        self.eng[q].dma_start(out=out, in_=in_, **kw).then_inc(self.sem[k], 16)
        self.cnt[k] += 16
        self.ninst += 1
        self._commit((k, self.cnt[k]), reads, writes)

    def barrier(self):
        if self.stopped:
            return
        self.nbar = getattr(self, 'nbar', 0) + 1
        self.phases.append((self.nbar, self.eng['pe'].n))
        for e in self.eng:
            self._wait(e, {k: v for k, v in self.cnt.items() if v > 0 and k != e})
        self.lw = {}
        self.rd = {}
        self.grp = {}
        if self.stop_at and self.nbar >= self.stop_at:
            self.stopped = True

    def mark(self, n):
        import os
        if int(os.environ.get('STOPM', '0')) == n:
            self.stopped = True

    def finish(self):
        self.stopped = False
        self._wait('sp', {k: v for k, v in self.cnt.items() if v > 0 and k != 'sp'})


class Seg:
    def __init__(self, name, T, L, cond):
        self.name = name
        self.T = T
        self.L = L
        self.nseq = T // L
        self.cond = cond
        self.ntile = T // 512
        self.sample = (name == 'S')


def host_consts():
    c = {}
    ident = np.eye(128, dtype=np.float32)
    flip = ident[::-1].copy()
    ones = np.ones((128, 128), np.float32)
    c['cst'] = np.concatenate([ident, flip, ones], axis=1)
    er = np.ones((4, 16), np.float32)
    Lh = 64
    for gi, w in enumerate((2, 4, 8, 16)):
        for t in range(8):
            lo = max(t - w // 2, 0)
            hi = min(t - w // 2 + w, Lh)
            er[gi, t] = w / float(hi - lo)
        for i in range(8):
            t = Lh - 8 + i
            lo = max(t - w // 2, 0)
            hi = min(t - w // 2 + w, Lh)
            er[gi, 8 + i] = w / float(hi - lo)
    c['edger'] = np.broadcast_to(er.reshape(1, 64), (128, 64)).copy()
    bases = [-6, -4, -2, 0, 2, 4, 6, -4, 4]
    col = np.arange(64)
    cs = np.clip(col - 8, 0, 48)
    mk = np.zeros((128, 9, 128), np.float32)
    for ty, base in enumerate(bases):
        interior = ty >= 7
        for pr in range(2):
            for qr in range(2):
                dr = base + pr - qr
                rowok = abs(dr) <= 7 and ((not interior) or (-4 <= dr <= 3))
                for pp in range(64):
                    kc = 63 - pp
                    ok = rowok & (kc >= cs) & (kc < cs + 16)
                    mk[(1 - pr) * 64 + pp, ty, qr * 64:(qr + 1) * 64] = np.where(ok, 0.0, NEG)
    c['maskr'] = mk
    return c


RT_BASES = [-6, -4, -2, 0, 2, 4, 6, -4, 4]


def nb_config(qc):
    edge = {-6: 0, -4: 1, -2: 2, 0: 3, 2: 4, 4: 5, 6: 6}
    if qc == 0:
        return [(j, edge[2 * j]) for j in range(4)]
    if qc == 1:
        return [(j, edge[2 * j - 2]) for j in range(4)]
    if qc == 14:
        return [(12 + j, edge[2 * j - 4]) for j in range(4)]
    if qc == 15:
        return [(12 + j, edge[2 * j - 6]) for j in range(4)]
    out = []
    for j in range(5):
        b = 2 * j - 4
        ty = 7 if j == 0 else (8 if j == 4 else edge[b])
        out.append((qc - 2 + j, ty))
    return out


def build_program(nl=DEPTH, do_sample=True, do_prompt=True, stop_at=0):
    nc = bass.Bass("TRN2", target_bir_lowering=False)
    din = {}

    def dram_in(name, shape):
        din[name] = nc.dram_tensor(name, list(shape), F32, kind="ExternalInput").ap()
        return din[name]

    xp_in = dram_in("xp", [NPSEQ * LP, D])
    xs_in = dram_in("xs", [LS, D])
    ck_in = dram_in("ck", [DEPTH, PAST, 512])
    cv_in = dram_in("cv", [DEPTH, PAST, 512])
    st_in = dram_in("st", [DEPTH * 2 * 512])
    cond_in = dram_in("cond", [2, D])
    w_ada = dram_in("w_ada", [DEPTH, D, NMOD * D])
    vec_in = {}
    for nm, n in [("b_ada", NMOD * D), ("g_pre_mix", D), ("g_post_mix", D), ("g_pre_ffn", D), ("g_post_ffn", D),
                  ("pool_scale", 512), ("lru_conv_w", 4 * 512), ("lru_conv_b", 512), ("lru_gate_b", 4 * 512),
                  ("lru_lambda", 2 * 512), ("cm_dw_w", 31 * 512), ("cm_dw_b", 512), ("cm_ln_g", 512),
                  ("cm_ln_b", 512), ("cm_pw_b", 512)]:
        vec_in[nm] = dram_in(nm, [DEPTH, n])
    w_in = dram_in("w_in", [DEPTH, D, DIN])
    pool_w = dram_in("pool_w", [DEPTH, 4, 128, 128])
    gate_w = dram_in("lru_gate_w", [DEPTH, 2, 2, 8, 64, 64])
    pw_w = dram_in("cm_pw_w", [DEPTH, 512, 512])
    rpb_in = dram_in("na_rpb", [DEPTH, 120, 31])
    w_out = dram_in("w_out", [DEPTH, D, D])
    w1_in = dram_in("ffn_w1", [DEPTH, D, DFF])
    w3_in = dram_in("ffn_w3", [DEPTH, D, DFF])
    w2_in = dram_in("ffn_w2", [DEPTH, DFF, D])
    cst_in = dram_in("cst", [128, 384])
    edger_in = dram_in("edger", [128, 64])
    maskr_in = dram_in("maskr", [128, 9, 128])

    yp_out = nc.dram_tensor("yp", [NPSEQ * LP, D], F32, kind="ExternalOutput").ap()
    ys_out = nc.dram_tensor("ys", [LS, D], F32, kind="ExternalOutput").ap()
    nk_out = nc.dram_tensor("nk", [NPSEQ, DEPTH, LP, 512], F32, kind="ExternalOutput").ap()
    nv_out = nc.dram_tensor("nv", [NPSEQ, DEPTH, LP, 512], F32, kind="ExternalOutput").ap()
    nst_out = nc.dram_tensor("nst", [NPSEQ, DEPTH, 2, 512], F32, kind="ExternalOutput").ap()

    segP = Seg('P', NPSEQ * LP, LP, 0)
    segS = Seg('S', LS, LS, 1)
    segs = ([segP] if do_prompt else []) + ([segS] if do_sample else [])
    xT = {s.name: nc.dram_tensor("xT_" + s.name, [D, s.T], F32, kind="Internal").ap() for s in (segP, segS)}
    yT = {s.name: nc.dram_tensor("yT_" + s.name, [D, s.T], BF16, kind="Internal").ap() for s in (segP, segS)}
    oT = nc.dram_tensor("oT", [D, 1024], F32, kind="Internal").ap()
    zpd = nc.dram_tensor("zpd", [120, 128], F32, kind="Internal").ap()
    x_in = {'P': xp_in, 'S': xs_in}
    y_out = {'P': yp_out, 'S': ys_out}

    es = ExitStack()
    with es:
        T = Trk(nc, es)
        T.stop_at = stop_at
        uid = [0]
        try:
            _emit_all(locals())
        except StopBuild:
            pass
        T.finish()
    nc._trk_ninst = T.ninst
    nc._trk_phases = T.phases
    return nc


def _emit_all(env):
    nc = env['nc']; T = env['T']; uid = env['uid']; es = env['es']
    segs = env['segs']; nl = env['nl']
    if True:
        xp_in = env['xp_in']; xs_in = env['xs_in']; ck_in = env['ck_in']; cv_in = env['cv_in']; st_in = env['st_in']
        cond_in = env['cond_in']; w_ada = env['w_ada']; vec_in = env['vec_in']; w_in = env['w_in']; pool_w = env['pool_w']
        gate_w = env['gate_w']; pw_w = env['pw_w']; rpb_in = env['rpb_in']; w_out = env['w_out']; w1_in = env['w1_in']
        w3_in = env['w3_in']; w2_in = env['w2_in']; cst_in = env['cst_in']; edger_in = env['edger_in']; maskr_in = env['maskr_in']
        yp_out = env['yp_out']; ys_out = env['ys_out']; nk_out = env['nk_out']; nv_out = env['nv_out']; nst_out = env['nst_out']
        segP = env['segP']; segS = env['segS']; xT = env['xT']; yT = env['yT']; oT = env['oT']; zpd = env['zpd']
        x_in = env['x_in']; y_out = env['y_out']

        def sb(stack, name, shape, dt):
            uid[0] += 1
            return stack.enter_context(nc.sbuf_tensor("%s_%d" % (name, uid[0]), list(shape), dt))

        psb = [es.enter_context(nc.psum_tensor("psb%d" % i, [128, 512], F32)) for i in range(8)]
        psi = [0]

        def bank():
            i = psi[0] % 6
            psi[0] += 1
            return psb[i], ('ps', i)

        cst = sb(es, "cst", [128, 384], F32)
        T.dma('sp', cst[:], cst_in[:, :], writes=['cst'])
        ident = cst[:, 0:128]
        ones_f = cst[:, 256:384]
        cstb = sb(es, "cstb", [128, 384], BF16)
        T.op('dve', lambda e: e.tensor_copy(out=cstb[:], in_=cst[:]), reads=['cst'], writes=['cstb'])
        ident_b = cstb[:, 0:128]
        flip_b = cstb[:, 128:256]
        epst = sb(es, "epst", [128, 1], F32)
        T.op('pool', lambda e: e.memset(epst[:], EPS), writes=['epst'])
        edger = sb(es, "edger", [128, 4, 16], F32)
        T.dma('sp', edger[:].rearrange("p a b -> p (a b)"), edger_in[:, :], writes=['edger'])
        pv = sb(es, "pv", [128, 3, 128], F32)
        mod = sb(es, "mod", [128, 96, 2], F32)
        modv = sb(es, "modv", [128, 4, 16, 2], F32)
        condT = sb(es, "condT", [128, 16, 2], BF16)
        h0all = sb(es, "h0all", [128, 32], F32)
        nsp = sb(es, "nsp", [128, 2, 8], F32)
        gw = sb(es, "gw", [128, 16, 128], BF16)

        def pvc(name, idx):
            ti, r0, n = PV_ROWS[name]
            return pv[:, ti, r0 + idx:r0 + idx + 1]

        def pvr(name):
            ti, r0, n = PV_ROWS[name]
            return pv[:, ti, r0:r0 + n]

        CONST_R = ['cst', 'cstb', 'epst']

        def evac_engine(i):
            return 'act' if i % 2 == 0 else 'dve'

        def copy_op(e, out, in_, reads, writes):
            if e == 'act':
                T.op('act', lambda g: g.activation(out=out, in_=in_, func=AF.Identity), reads=reads, writes=writes)
            else:
                T.op(e, lambda g: g.tensor_copy(out=out, in_=in_), reads=reads, writes=writes)

        with ExitStack() as ph:
            xin = [sb(ph, "xin", [128, D], F32) for _ in range(2)]
            xo = [sb(ph, "xo", [128, NCH, 128], F32) for _ in range(2)]
            it = 0
            for s in segs:
                xTv = xT[s.name].rearrange("(c p) t -> p c t", p=128)
                for tc in range(s.T // 128):
                    r = it % 2
                    it += 1
                    T.dma('sp', xin[r][:], x_in[s.name][tc * 128:(tc + 1) * 128, :], writes=[('xin', r)])
                    for g in range(4):
                        ps, pt = bank()

                        def emit(e, ps=ps, r=r, g=g):
                            ins = None
                            for j in range(4):
                                c = 4 * g + j
                                ins = e.transpose(out=ps[:, j * 128:(j + 1) * 128], in_=xin[r][:, c * 128:(c + 1) * 128],
                                                  identity=ident)
                            return ins
                        T.op('pe', emit, reads=[('xin', r), 'cst'], writes=[pt])
                        copy_op(evac_engine(g), xo[r][:, 4 * g:4 * g + 4, :],
                                ps[:, :].rearrange("p (a b) -> p a b", b=128), [pt], [('xo', r, g)])
                    T.dma('sp', xTv[:, :, tc * 128:(tc + 1) * 128], xo[r][:],
                          reads=[('xo', r, g) for g in range(4)], writes=[T.new('xT0')])
            cs = sb(ph, "cs", [2, D], F32)
            T.dma('sp', cs[:], cond_in[:, :], writes=['cs'])
            ps, pt = bank()

            def emit(e, ps=ps):
                ins = None
                for c in range(16):
                    ins = e.transpose(out=ps[:, c * 2:(c + 1) * 2], in_=cs[0:2, c * 128:(c + 1) * 128],
                                      identity=cst[0:2, 0:2])
                return ins
            T.op('pe', emit, reads=['cs', 'cst'], writes=[pt])
            T.op('act', lambda e: e.activation(out=condT[:].rearrange("p a b -> p (a b)"), in_=ps[:, 0:32], func=AF.Silu),
                 reads=[pt], writes=['condT'])
            sst = sb(ph, "sst", [32, 128], F32)
            T.dma('sp', sst[:], st_in.rearrange("(n p) -> n p", p=128), writes=['sst'])
            ps, pt = bank()
            T.op('pe', lambda e: e.transpose(out=ps[:, 0:32], in_=sst[:, :], identity=cst[0:32, 0:32]),
                 reads=['sst', 'cst'], writes=[pt])
            T.op('dve', lambda e: e.tensor_copy(out=h0all[:], in_=ps[:, 0:32]), reads=[pt], writes=['h0all'])
            T.barrier()

        def load_w(dst, src, tok):
            T.dma('pool', dst, src, writes=[tok])

        def norm_phase(stack, s, l, t0, ntile, gi, shi, hT, htok, NT=256):
            j = s.cond
            xTv = xT[s.name].rearrange("(c p) t -> p c t", p=128)
            xt = [sb(stack, "n_xt", [128, NCH, NT], F32) for _ in range(2)]
            sq = [sb(stack, "n_sq", [128, NCH, NT], F32) for _ in range(2)]
            rstd = [sb(stack, "n_rstd", [128, NT], F32) for _ in range(2)]
            npp = 512 // NT
            npi = ntile * npp
            pbank = {}

            def n_a1(pi):
                ti = pi // npp
                tg = t0 + ti
                c0 = tg * 512 + (pi % npp) * NT
                r = pi % 2
                for g4 in range(4):
                    T.dma('sp', xt[r][:, 4 * g4:4 * g4 + 4, :], xTv[:, 4 * g4:4 * g4 + 4, c0:c0 + NT],
                          reads=[('xT', s.name, tg)], writes=[('n_xt', r, g4)])
                xtoks = [('n_xt', r, g4) for g4 in range(4)]
                T.op('act', lambda e: e.activation(out=sq[r][:], in_=xt[r][:], func=AF.Square),
                     reads=xtoks, writes=[('n_sq', r)])
                ps, pt = bank()
                pbank[pi] = (ps, pt)

                def emit(e):
                    ins = None
                    for c in range(NCH):
                        ins = e.matmul(ps[:, 0:NT], lhsT=ones_f, rhs=sq[r][:, c, :], start=(c == 0), stop=(c == NCH - 1))
                    return ins
                T.op('pe', emit, reads=[('n_sq', r), 'cst'], writes=[pt])

            def n_a2(pi):
                r = pi % 2
                ps, pt = pbank[pi]
                T.op('act', lambda e: e.activation(out=rstd[r][:], in_=ps[:, 0:NT], func=AF.Sqrt, bias=epst[:, 0:1],
                                                   scale=1.0 / D), reads=[pt, 'epst'], writes=[('n_rstd', r)])
                T.op('dve', lambda e: e.reciprocal(out=rstd[r][:], in_=rstd[r][:]),
                     reads=[('n_rstd', r)], writes=[('n_rstd', r)])

            def n_b(pi):
                ti = pi // npp
                h0 = ti * 512 + (pi % npp) * NT
                r = pi % 2
                xtoks = [('n_xt', r, g4) for g4 in range(4)]
                T.op('dve', lambda e: e.tensor_tensor(out=sq[r][:], in0=xt[r][:],
                                                      in1=rstd[r][:, :].unsqueeze(1).to_broadcast([128, NCH, NT]),
                                                      op=ALU.mult),
                     reads=xtoks + [('n_rstd', r)], writes=[('n_sq', r)])
                for c in range(NCH):
                    gsc = modv[:, gi, c, j:j + 1]
                    shc = mod[:, shi * 16 + c, j:j + 1]
                    dst = hT[:, c, h0:h0 + NT]
                    if c % 2 == 0:
                        T.op('act', lambda e, dst=dst, c=c, gsc=gsc, shc=shc: e.activation(
                            out=dst, in_=sq[r][:, c, :], func=AF.Identity, bias=shc, scale=gsc),
                            reads=[('n_sq', r), 'modv', 'mod'], writes=[(htok, ti)])
                    else:
                        T.op('dve', lambda e, dst=dst, c=c, gsc=gsc, shc=shc: e.tensor_scalar(
                            out=dst, in0=sq[r][:, c, :], scalar1=gsc, scalar2=shc, op0=ALU.mult, op1=ALU.add),
                            reads=[('n_sq', r), 'modv', 'mod'], writes=[(htok, ti)])

            n_a1(0)
            n_a2(0)
            for pi in range(npi):
                if pi + 1 < npi:
                    n_a1(pi + 1)
                n_b(pi)
                if pi + 1 < npi:
                    n_a2(pi + 1)

        def rstd_from_ps(ps, pt, dst, dtok, n=512, scale=1.0 / D):
            T.op('act', lambda e: e.activation(out=dst[:, 0:n], in_=ps[:, 0:n], func=AF.Sqrt, bias=epst[:, 0:1], scale=scale),
                 reads=[pt, 'epst'], writes=[dtok])
            T.op('dve', lambda e: e.reciprocal(out=dst[:, 0:n], in_=dst[:, 0:n]), reads=[dtok], writes=[dtok])

        for l in range(nl):
            with ExitStack() as ph:
                stg = sb(ph, "pstg", [128, 3, 128], F32)
                T.op('pool', lambda e: e.memset(stg[:], 0.0), writes=['pstg'])
                for nm, (ti, r0, n) in PV_ROWS.items():
                    T.dma('sp', stg[r0:r0 + n, ti, :], vec_in[nm][l].rearrange("(n p) -> n p", p=128),
                          reads=['pstg'], writes=[T.new('pstg_d')])
                for ti in range(3):
                    ps, pt = bank()
                    T.op('pe', lambda e, ps=ps, ti=ti: e.transpose(out=ps[:, 0:128], in_=stg[:, ti, :], identity=ident),
                         reads=['pstg', 'cst'] + T.all('pstg_d'), writes=[pt])
                    copy_op('dve', pv[:, ti, :], ps[:, 0:128], [pt], ['pv'])
                lam = pvr("lru_lambda")
                tmp = sb(ph, "lamtmp", [128, 8], F32)
                T.op('act', lambda e: e.activation(out=tmp[:], in_=lam, func=AF.Exp, scale=-1.0), reads=['pv'], writes=['lamtmp'])
                T.op('act', lambda e: e.activation(out=tmp[:], in_=tmp[:], func=AF.Ln, bias=1.0, scale=1.0),
                     reads=['lamtmp'], writes=['lamtmp'])
                T.op('dve', lambda e: e.tensor_scalar(out=nsp[:, 0, :], in0=tmp[:], scalar1=-8.0, scalar2=None, op0=ALU.mult),
                     reads=['lamtmp'], writes=['nsp'])
                T.op('dve', lambda e: e.tensor_scalar(out=nsp[:, 1, :], in0=tmp[:], scalar1=-16.0, scalar2=None, op0=ALU.mult),
                     reads=['lamtmp'], writes=['nsp'])
                T.op('pool', lambda e: e.memset(gw[:], 0.0), writes=['gw'])
                for d in range(2):
                    for g in range(2):
                        for k in range(8):
                            po = (k % 2) * 64
                            T.dma('pool', gw[po:po + 64, (d * 2 + g) * 4 + k // 2, po:po + 64], gate_w[l, d, g, k],
                                  reads=['gw'], writes=[T.new('gw_d')])
                wsl = [sb(ph, "wada", [128, 16, 1024], BF16) for _ in range(3)]
                wv = w_ada[l].rearrange("(k p) n -> p k n", p=128)
                psm, ptm = bank()
                for si in range(12):
                    r = si % 3
                    load_w(wsl[r][:], wv[:, :, si * 1024:(si + 1) * 1024], ('wada', r))

                    def emit(e, r=r, si=si):
                        ins = None
                        for jn in range(8):
                            n = si * 8 + jn
                            for k in range(16):
                                ins = e.matmul(psm[:, n * 2:(n + 1) * 2], lhsT=wsl[r][:, k, jn * 128:(jn + 1) * 128],
                                               rhs=condT[:, k, :], start=(k == 0), stop=(k == 15))
                        return ins
                    T.op('pe', emit, reads=[('wada', r), 'condT'], writes=[ptm])
                bada = pvr("b_ada")
                T.op('dve', lambda e: e.tensor_tensor(out=mod[:], in0=psm[:, 0:192].rearrange("p (a b) -> p a b", b=2),
                                                      in1=bada.unsqueeze(2).to_broadcast([128, 96, 2]), op=ALU.add),
                     reads=[ptm, 'pv'], writes=['mod'])
                for a, (sci, gname) in enumerate([(1, "g_pre_mix"), (2, "g_post_mix"), (4, "g_pre_ffn"), (5, "g_post_ffn")]):
                    gv = pvr(gname).unsqueeze(2).to_broadcast([128, 16, 2])
                    src = mod[:, sci * 16:(sci + 1) * 16, :]
                    if a % 2 == 0:
                        T.op('dve', lambda e, a=a, src=src, gv=gv: e.scalar_tensor_tensor(
                            out=modv[:, a, :, :], in0=src, scalar=1.0, in1=gv, op0=ALU.add, op1=ALU.mult),
                            reads=['mod', 'pv'], writes=['modv'])
                    else:
                        T.op('dve', lambda e, a=a, src=src, gv=gv: e.tensor_tensor(
                            out=modv[:, a, :, :], in0=src, in1=gv, op=ALU.mult), reads=['mod', 'pv'], writes=['modv'])
                T.barrier()

            for s in segs:
                j = s.cond
                Tn, L, nseq = s.T, s.L, s.nseq
                yTd = yT[s.name]
                spt = max(1, 512 // L)
                ntl = min(L, 512)

                def tview(buf3, padl, ti):
                    if L >= 512:
                        sq_ = (ti * 512) // L
                        off = (ti * 512) % L
                        return buf3[:, sq_, padl + off:padl + off + 512]
                    return buf3[:, ti * spt:(ti + 1) * spt, padl:padl + L]

                def pview(ps):
                    if L >= 512:
                        return ps[:, :]
                    return ps[:, :].rearrange("p (a b) -> p a b", b=L)

                with ExitStack() as ph:
                    hT = sb(ph, "hT", [128, NCH, Tn], BF16)
                    with ExitStack() as sub:
                        norm_phase(sub, s, l, 0, s.ntile, 0, 0, hT, 'hT')
                        T.barrier()
                    wslab = [sb(ph, "wslab", [128, 16, 512], BF16) for _ in range(2)]
                    wiv = w_in[l].rearrange("(k p) n -> p k n", p=128)
                    wcnt = [0]

                    def get_slab(si):
                        r = wcnt[0] % 2
                        wcnt[0] += 1
                        load_w(wslab[r][:], wiv[:, :, si * 512:(si + 1) * 512], ('wslab', r))
                        return wslab[r], ('wslab', r)

                    def proj_fm(slab, stok, jc, ti):
                        ps, pt = bank()

                        def emit(e):
                            ins = None
                            for k in range(16):
                                ins = e.matmul(ps[:, :], lhsT=slab[:, k, jc * 128:(jc + 1) * 128],
                                               rhs=hT[:, k, ti * 512:(ti + 1) * 512], start=(k == 0), stop=(k == 15))
                            return ins
                        T.op('pe', emit, reads=[stok, ('hT', ti)], writes=[pt])
                        return ps, pt

                    with ExitStack() as sub:
                        Lp = L + 16
                        up = sb(sub, "up", [128, nseq, Lp], F32)
                        lv = [sb(sub, "lv", [128, nseq, Lp], F32) for _ in range(2)]
                        pooled = sb(sub, "pooled", [128, Tn], BF16)
                        ych = sb(sub, "ychA", [128, Tn], BF16)
                        pwt = sb(sub, "poolw", [128, 4, 128], BF16)
                        load_w(pwt[:], pool_w[l].rearrange("g c d -> c g d"), 'poolw')
                        T.op('pool', lambda e: e.memset(up[:], 0.0), writes=['up'])
                        slab, stok = get_slab(0)
                        for g in range(4):
                            w = 2 << g
                            for ti in range(s.ntile):
                                ps, pt = proj_fm(slab, stok, g, ti)
                                copy_op(evac_engine(ti), tview(up, 8, ti), pview(ps), [pt], ['up'])
                            cur, ctok = up, 'up'
                            m = 1
                            lo, hi = 0, Lp
                            step = 0
                            while m < w:
                                dst = lv[step % 2]
                                dtok = ('lv', step % 2)
                                if m == 1:
                                    nlo, nhi = lo + 1, hi
                                    a0 = cur[:, :, nlo - 1:nhi - 1]
                                    a1 = cur[:, :, nlo:nhi]
                                else:
                                    h2 = m // 2
                                    nlo, nhi = lo + h2, hi - h2
                                    a0 = cur[:, :, nlo - h2:nhi - h2]
                                    a1 = cur[:, :, nlo + h2:nhi + h2]
                                T.op('dve', lambda e, dst=dst, a0=a0, a1=a1, nlo=nlo, nhi=nhi: e.tensor_tensor(
                                    out=dst[:, :, nlo:nhi], in0=a0, in1=a1, op=ALU.add), reads=[ctok], writes=[dtok])
                                cur, ctok = dst, dtok
                                lo, hi = nlo, nhi
                                m *= 2
                                step += 1
                            T.op('dve', lambda e, cur=cur, w=w: e.tensor_scalar(
                                out=cur[:, :, 8:8 + L], in0=cur[:, :, 8:8 + L], scalar1=1.0 / w, scalar2=None, op0=ALU.mult),
                                reads=[ctok], writes=[ctok])
                            T.op('dve', lambda e, cur=cur, g=g: e.tensor_tensor(
                                out=cur[:, :, 8:16], in0=cur[:, :, 8:16],
                                in1=edger[:, g, 0:8].unsqueeze(1).to_broadcast([128, nseq, 8]), op=ALU.mult),
                                reads=[ctok, 'edger'], writes=[ctok])
                            T.op('dve', lambda e, cur=cur, g=g: e.tensor_tensor(
                                out=cur[:, :, L:L + 8], in0=cur[:, :, L:L + 8],
                                in1=edger[:, g, 8:16].unsqueeze(1).to_broadcast([128, nseq, 8]), op=ALU.mult),
                                reads=[ctok, 'edger'], writes=[ctok])
                            T.op('dve', lambda e, cur=cur: e.tensor_tensor(
                                out=pooled[:, :].rearrange("p (a b) -> p a b", b=L), in0=cur[:, :, 8:8 + L],
                                in1=up[:, :, 8:8 + L], op=ALU.subtract), reads=[ctok, 'up'], writes=['pooled'])
                            for ti in range(s.ntile):
                                ps, pt = bank()
                                T.op('pe', lambda e, ps=ps, g=g, ti=ti: e.matmul(
                                    ps[:, :], lhsT=pwt[:, g, :], rhs=pooled[:, ti * 512:(ti + 1) * 512], start=True, stop=True),
                                    reads=['poolw', 'pooled'], writes=[pt])
                                T.op('act', lambda e, ps=ps, g=g, ti=ti: e.activation(
                                    out=ych[:, ti * 512:(ti + 1) * 512], in_=ps[:, :], func=AF.Identity,
                                    scale=pvc("pool_scale", g)), reads=[pt, 'pv'], writes=['ychA'])
                            T.dma('sp', yTd[g * 128:(g + 1) * 128, :], ych[:, :], reads=['ychA'], writes=[T.new('yT_' + s.name)])
                        T.barrier()

                    with ExitStack() as sub:
                        xbp = sb(sub, "xbp", [128, nseq, L + 3], F32)
                        xf = sb(sub, "xf", [128, nseq, L], F32)
                        xfb = sb(sub, "xfb", [128, Tn], BF16)
                        G = [sb(sub, "G%d" % i, [128, nseq, L], F32) for i in range(4)]
                        om = sb(sub, "om", [128, nseq, L], F32)
                        hf = sb(sub, "hf", [128, nseq, L], F32)
                        gg = sb(sub, "gg", [128, nseq, L], F32)
                        ych = sb(sub, "ychB", [128, Tn], BF16)
                        T.op('pool', lambda e: e.memset(xbp[:], 0.0), writes=['xbp'])
                        slab_x, stx = get_slab(1)
                        slab_g, stg_ = get_slab(2)

                        def flat(b):
                            return b[:].rearrange("p a b -> p (a b)")
                        for cb in range(4):
                            for ti in range(s.ntile):
                                ps, pt = proj_fm(slab_x, stx, cb, ti)
                                copy_op(evac_engine(ti), tview(xbp, 2, ti), pview(ps), [pt], ['xbp'])
                            for ti in range(s.ntile):
                                ps, pt = proj_fm(slab_g, stg_, cb, ti)
                                gv = tview(gg, 0, ti)
                                ov = tview(om, 0, ti)
                                T.op('act', lambda e, ov=ov, ps=ps: e.activation(out=ov, in_=pview(ps), func=AF.Square),
                                     reads=[pt], writes=['om'])
                                T.op('dve', lambda e, ov=ov: e.tensor_scalar(out=ov, in0=ov, scalar1=0.044715, scalar2=1.0,
                                                                            op0=ALU.mult, op1=ALU.add), reads=['om'], writes=['om'])
                                T.op('dve', lambda e, ov=ov, ps=ps: e.tensor_tensor(out=ov, in0=ov, in1=pview(ps), op=ALU.mult),
                                     reads=['om', pt], writes=['om'])
                                T.op('act', lambda e, ov=ov: e.activation(out=ov, in_=ov, func=AF.Sigmoid, scale=1.5957691216057308),
                                     reads=['om'], writes=['om'])
                                T.op('dve', lambda e, ov=ov, gv=gv, ps=ps: e.tensor_tensor(out=gv, in0=ov, in1=pview(ps), op=ALU.mult),
                                     reads=['om', pt], writes=['gg'])
                            T.op('dve', lambda e, cb=cb: e.tensor_scalar(
                                out=xf[:], in0=xbp[:, :, 0:L], scalar1=pvc("lru_conv_w", 0 * 4 + cb), scalar2=pvc("lru_conv_b", cb),
                                op0=ALU.mult, op1=ALU.add), reads=['xbp', 'pv'], writes=['xf'])
                            for jt in range(1, 4):
                                T.op('dve', lambda e, cb=cb, jt=jt: e.scalar_tensor_tensor(
                                    out=xf[:], in0=xbp[:, :, jt:jt + L], scalar=pvc("lru_conv_w", jt * 4 + cb), in1=xf[:],
                                    op0=ALU.mult, op1=ALU.add), reads=['xbp', 'xf', 'pv'], writes=['xf'])
                            T.op('act', lambda e: e.activation(out=xfb[:, :], in_=flat(xf), func=AF.Identity),
                                 reads=['xf'], writes=['xfb'])
                            for dg in range(4):
                                for ti in range(s.ntile):
                                    ps, pt = bank()
                                    T.op('pe', lambda e, ps=ps, dg=dg, ti=ti, cb=cb: e.matmul(
                                        ps[:, :], lhsT=gw[:, dg * 4 + cb, :], rhs=xfb[:, ti * 512:(ti + 1) * 512],
                                        start=True, stop=True), reads=['gw', 'xfb'], writes=[pt])
                                    T.op('act', lambda e, ps=ps, dg=dg, ti=ti, cb=cb: e.activation(
                                        out=tview(G[dg], 0, ti), in_=pview(ps), func=AF.Sigmoid,
                                        bias=pvc("lru_gate_b", dg * 4 + cb), scale=1.0), reads=[pt, 'pv'], writes=[('G', dg)])
                            for d in range(2):
                                R_, I_ = G[d * 2], G[d * 2 + 1]
                                rt, it_ = ('G', d * 2), ('G', d * 2 + 1)
                                T.op('act', lambda e, d=d, cb=cb, R_=R_: e.activation(
                                    out=om[:], in_=R_[:], func=AF.Exp, scale=nsp[:, 1, d * 4 + cb:d * 4 + cb + 1]),
                                    reads=[rt, 'nsp'], writes=['om'])
                                T.op('act', lambda e, d=d, cb=cb, R_=R_: e.activation(
                                    out=R_[:], in_=R_[:], func=AF.Exp, scale=nsp[:, 0, d * 4 + cb:d * 4 + cb + 1]),
                                    reads=[rt, 'nsp'], writes=[rt])
                                T.op('dve', lambda e: e.tensor_scalar(out=om[:], in0=om[:], scalar1=-1.0, scalar2=1.0,
                                                                      op0=ALU.mult, op1=ALU.add), reads=['om'], writes=['om'])
                                T.op('dve', lambda e: e.tensor_scalar(out=om[:], in0=om[:], scalar1=1e-30, scalar2=None,
                                                                      op0=ALU.max), reads=['om'], writes=['om'])
                                T.op('act', lambda e: e.activation(out=om[:], in_=om[:], func=AF.Sqrt), reads=['om'], writes=['om'])
                                T.op('dve', lambda e, I_=I_: e.tensor_tensor(out=I_[:], in0=I_[:], in1=om[:], op=ALU.mult),
                                     reads=[it_, 'om'], writes=[it_])
                                T.op('dve', lambda e, I_=I_: e.tensor_tensor(out=I_[:], in0=I_[:], in1=xf[:], op=ALU.mult),
                                     reads=[it_, 'xf'], writes=[it_])
                                dst = hf if d == 0 else om
                                dtok = 'hf' if d == 0 else 'om'
                                for sq_ in range(nseq):
                                    if s.sample:
                                        col = l * 8 + d * 4 + cb
                                        init = h0all[:, col:col + 1]
                                    else:
                                        init = 0.0
                                    if d == 0:
                                        T.op('dve', lambda e, sq_=sq_, init=init, R_=R_, I_=I_, dst=dst: e.tensor_tensor_scan(
                                            out=dst[:, sq_, :], data0=R_[:, sq_, :], data1=I_[:, sq_, :], initial=init,
                                            op0=ALU.mult, op1=ALU.add), reads=[rt, it_, 'h0all'], writes=[dtok])
                                    else:
                                        T.op('dve', lambda e, sq_=sq_, init=init, R_=R_, I_=I_, dst=dst: e.tensor_tensor_scan(
                                            out=dst[:, sq_, ::-1], data0=R_[:, sq_, ::-1], data1=I_[:, sq_, ::-1], initial=init,
                                            op0=ALU.mult, op1=ALU.add), reads=[rt, it_, 'h0all'], writes=[dtok])
                                if not s.sample:
                                    src = hf[:, :, L - 1] if d == 0 else om[:, :, 0]
                                    dsto = bass.AP(nst_out.tensor, l * 1024 + d * 512 + cb * 128, [[1, 128], [DEPTH * 1024, nseq]])
                                    T.dma('sp', dsto, src, reads=[dtok], writes=[], allow_slow_non_contiguous=True)
                            T.op('dve', lambda e: e.tensor_tensor(out=hf[:], in0=hf[:], in1=om[:], op=ALU.add),
                                 reads=['hf', 'om'], writes=['hf'])
                            T.op('dve', lambda e: e.tensor_tensor(out=ych[:, :], in0=flat(hf), in1=flat(gg), op=ALU.mult),
                                 reads=['hf', 'gg'], writes=['ychB'])
                            T.dma('sp', yTd[(4 + cb) * 128:(5 + cb) * 128, :], ych[:, :], reads=['ychB'], writes=[T.new('yT_' + s.name)])
                        T.barrier()

                    with ExitStack() as sub:
                        cpad = sb(sub, "cpad", [128, 4, nseq, L + 30], BF16)
                        dgm = sb(sub, "dgm", [128, 124, 128], BF16)
                        sg = [sb(sub, "sg", [128, 512], F32) for _ in range(2)]
                        NP_ = 256
                        cvt_r = [sb(sub, "cvt", [128, 4, NP_], F32) for _ in range(2)]
                        sqt = sb(sub, "sqt", [128, 4, NP_], F32)
                        mt_r = [sb(sub, "mt", [128, NP_], F32) for _ in range(2)]
                        m2_r = [sb(sub, "m2", [128, NP_], F32) for _ in range(2)]
                        rs_r = [sb(sub, "rsC", [128, NP_], F32) for _ in range(2)]
                        sl_r = [sb(sub, "sl", [128, 4, NP_], BF16) for _ in range(2)]
                        ychp = [sb(sub, "ychC", [128, 4, NP_], BF16) for _ in range(2)]
                        pww = sb(sub, "pww", [128, 4, 512], BF16)
                        load_w(pww[:], pw_w[l].rearrange("(c p) e -> p c e", p=128), 'pww')
                        T.op('pool', lambda e: e.memset(cpad[:], 0.0), writes=['cpad'])
                        dww = pvr("cm_dw_w")
                        for idx in range(124):
                            T.op('pool' if idx % 2 else 'dve', lambda e, idx=idx: e.tensor_scalar(
                                out=dgm[:, idx, :], in0=ident_b, scalar1=dww[:, idx:idx + 1], scalar2=None, op0=ALU.mult),
                                reads=['cstb', 'pv'], writes=['dgm'])
                        slab_a, sta = get_slab(3)
                        slab_g, stg_ = get_slab(4)
                        for cc in range(4):
                            for ti in range(s.ntile):
                                psa, pta = proj_fm(slab_a, sta, cc, ti)
                                psg, ptg = proj_fm(slab_g, stg_, cc, ti)
                                r = ti % 2
                                T.op('act', lambda e, r=r, psg=psg: e.activation(out=sg[r][:], in_=psg[:, :], func=AF.Sigmoid),
                                     reads=[ptg], writes=[('sg', r)])
                                sgv = sg[r][:, :] if L >= 512 else sg[r][:, :].rearrange("p (a b) -> p a b", b=L)
                                T.op('dve', lambda e, cc=cc, ti=ti, psa=psa, sgv=sgv: e.tensor_tensor(
                                    out=tview(cpad[:, cc], 15, ti), in0=pview(psa), in1=sgv, op=ALU.mult),
                                    reads=[pta, ('sg', r)], writes=['cpad'])
                        pieces = []
                        for sq_ in range(nseq):
                            for off in range(0, L, NP_):
                                pieces.append((sq_, off))
                        n = NP_

                        def cm_s1(pci):
                            sq_, off = pieces[pci]
                            rr = pci % 2
                            cvt = cvt_r[rr]
                            for cc in range(4):
                                ps, pt = bank()

                                def emit(e, ps=ps, cc=cc, sq_=sq_, off=off):
                                    ins = None
                                    for jt in range(31):
                                        ins = e.matmul(ps[:, 0:n], lhsT=dgm[:, jt * 4 + cc, :],
                                                       rhs=cpad[:, cc, sq_, off + jt:off + jt + n], start=(jt == 0), stop=(jt == 30))
                                    return ins
                                T.op('pe', emit, reads=['dgm', 'cpad'], writes=[pt])
                                T.op('act', lambda e, ps=ps, cc=cc, cvt=cvt: e.activation(
                                    out=cvt[:, cc, 0:n], in_=ps[:, 0:n], func=AF.Identity, bias=pvc("cm_dw_b", cc), scale=1.0),
                                    reads=[pt, 'pv'], writes=[('cvt', rr)])

                        def cm_s2(pci):
                            rr = pci % 2
                            cvt, mt, m2, rs, sl = cvt_r[rr], mt_r[rr], m2_r[rr], rs_r[rr], sl_r[rr]
                            ct, mtk, m2k, rsk, slk = ('cvt', rr), ('mt', rr), ('m2', rr), ('rsC', rr), ('sl', rr)
                            T.op('dve', lambda e: e.tensor_tensor(out=sqt[:, :, 0:n], in0=cvt[:, :, 0:n], in1=cvt[:, :, 0:n],
                                                                  op=ALU.mult), reads=[ct], writes=['sqt'])
                            psm_, ptm_ = bank()
                            pss_, pts_ = bank()

                            def emit(e):
                                ins = None
                                for cc in range(4):
                                    ins = e.matmul(psm_[:, 0:n], lhsT=ones_f, rhs=cvt[:, cc, 0:n], start=(cc == 0), stop=(cc == 3))
                                return ins
                            T.op('pe', emit, reads=[ct, 'cst'], writes=[ptm_])

                            def emit(e):
                                ins = None
                                for cc in range(4):
                                    ins = e.matmul(pss_[:, 0:n], lhsT=ones_f, rhs=sqt[:, cc, 0:n], start=(cc == 0), stop=(cc == 3))
                                return ins
                            T.op('pe', emit, reads=['sqt', 'cst'], writes=[pts_])
                            T.op('act', lambda e: e.activation(out=mt[:, 0:n], in_=psm_[:, 0:n], func=AF.Identity,
                                                               scale=1.0 / 512), reads=[ptm_], writes=[mtk])
                            T.op('dve', lambda e: e.tensor_tensor(out=m2[:, 0:n], in0=mt[:, 0:n], in1=mt[:, 0:n], op=ALU.mult),
                                 reads=[mtk], writes=[m2k])
                            T.op('dve', lambda e: e.scalar_tensor_tensor(
                                out=m2[:, 0:n], in0=pss_[:, 0:n], scalar=1.0 / 512, in1=m2[:, 0:n], op0=ALU.mult, op1=ALU.subtract),
                                reads=[pts_, m2k], writes=[m2k])
                            T.op('act', lambda e: e.activation(out=rs[:, 0:n], in_=m2[:, 0:n], func=AF.Sqrt, bias=epst[:, 0:1],
                                                               scale=1.0), reads=[m2k, 'epst'], writes=[rsk])
                            T.op('dve', lambda e: e.reciprocal(out=rs[:, 0:n], in_=rs[:, 0:n]), reads=[rsk], writes=[rsk])
                            T.op('dve', lambda e: e.tensor_tensor(
                                out=cvt[:, :, 0:n], in0=cvt[:, :, 0:n], in1=mt[:, 0:n].unsqueeze(1).to_broadcast([128, 4, n]),
                                op=ALU.subtract), reads=[ct, mtk], writes=[ct])
                            T.op('dve', lambda e: e.tensor_tensor(
                                out=cvt[:, :, 0:n], in0=cvt[:, :, 0:n], in1=rs[:, 0:n].unsqueeze(1).to_broadcast([128, 4, n]),
                                op=ALU.mult), reads=[ct, rsk], writes=[ct])
                            for cc in range(4):
                                T.op('act', lambda e, cc=cc: e.activation(
                                    out=sl[:, cc, 0:n], in_=cvt[:, cc, 0:n], func=AF.Silu, bias=pvc("cm_ln_b", cc),
                                    scale=pvc("cm_ln_g", cc)), reads=[ct, 'pv'], writes=[slk])

                        def cm_s3(pci):
                            sq_, off = pieces[pci]
                            rr = pci % 2
                            sl = sl_r[rr]
                            slk = ('sl', rr)
                            t_abs = sq_ * L + off
                            ych = ychp[rr]
                            ytok = ('ychC', rr)
                            for ec in range(4):
                                ps, pt = bank()

                                def emit(e, ps=ps, ec=ec):
                                    ins = None
                                    for cc in range(4):
                                        ins = e.matmul(ps[:, 0:n], lhsT=pww[:, cc, ec * 128:(ec + 1) * 128], rhs=sl[:, cc, 0:n],
                                                       start=(cc == 0), stop=(cc == 3))
                                    return ins
                                T.op('pe', emit, reads=['pww', slk], writes=[pt])
                                T.op('act' if ec % 2 == 0 else 'dve', (lambda e, ps=ps, ec=ec: e.activation(
                                    out=ych[:, ec, 0:n], in_=ps[:, 0:n], func=AF.Identity, bias=pvc("cm_pw_b", ec), scale=1.0))
                                    if ec % 2 == 0 else (lambda e, ps=ps, ec=ec: e.tensor_scalar(
                                        out=ych[:, ec, 0:n], in0=ps[:, 0:n], scalar1=pvc("cm_pw_b", ec), scalar2=None,
                                        op0=ALU.add)), reads=[pt, 'pv'], writes=[ytok])
                            T.dma('sp', yTd[8 * 128:12 * 128, t_abs:t_abs + n].rearrange("(e p) t -> p e t", p=128), ych[:, :, 0:n],
                                  reads=[ytok], writes=[T.new('yT_' + s.name)])

                        cm_s1(0)
                        for pci in range(len(pieces)):
                            if pci + 1 < len(pieces):
                                cm_s1(pci + 1)
                            cm_s2(pci)
                            cm_s3(pci)
                        T.barrier()

                    with ExitStack() as sub:
                        nTk = Tn // 128
                        if s.sample:
                            Rt = sb(sub, "Rt", [128, 9, 8, 128], BF16)
                            ckT = sb(sub, "ckT", [128, 4, 512], BF16)
                            cvb = sb(sub, "cvb", [128, 4, 512], BF16)
                            T.dma('pool', cvb[:], cv_in[l].rearrange("(a p) f -> p a f", p=128), writes=['cvb'])
                            with ExitStack() as sub2:
                                ckf = sb(sub2, "ckf", [128, 4, 512], F32)
                                maskr = sb(sub2, "maskr", [128, 9, 128], F32)
                                rp = sb(sub2, "rp", [120, 31], F32)
                                zp = sb(sub2, "zp", [120, 128], F32)
                                HK = sb(sub2, "HK", [128, 120, 64], F32)
                                T.dma('sp', ckf[:], ck_in[l].rearrange("(a p) f -> p a f", p=128), writes=['ckf'])
                                T.dma('sp', maskr[:], maskr_in[:, :, :], writes=['maskr'])
                                for c in range(4):
                                    ps, pt = bank()

                                    def emit(e, ps=ps, c=c):
                                        ins = None
                                        for a_ in range(4):
                                            ins = e.transpose(out=ps[:, a_ * 128:(a_ + 1) * 128], in_=ckf[:, a_, c * 128:(c + 1) * 128],
                                                              identity=ident)
                                        return ins
                                    T.op('pe', emit, reads=['ckf', 'cst'], writes=[pt])
                                    copy_op('act', ckT[:, c, :], ps[:, :], [pt], ['ckT'])
                                T.dma('sp', rp[:], rpb_in[l], writes=['rp'])
                                T.op('pool', lambda e: e.memset(zp[:], 0.0), writes=['zp'])
                                T.op('dve', lambda e: e.tensor_copy(out=zp[:, 48:79], in_=rp[:, ::-1]), reads=['rp', 'zp'], writes=['zp'])
                                T.dma('sp', zpd[:, :], zp[:], reads=['zp'], writes=['zpd'])
                                for half in range(2):
                                    for q4 in range(4):
                                        src = bass.AP(zpd.tensor, q4 * 30 * 128, [[1, 64], [128, 30], [1, 64]])
                                        T.dma('sp', HK[half * 64:(half + 1) * 64, q4 * 30:(q4 + 1) * 30, :], src,
                                              reads=['zpd'], writes=[T.new('HK')])
                                HK4 = HK[:].rearrange("p (h r) q -> p h r q", r=15)
                                k_ = 0
                                for ty, base in enumerate(RT_BASES):
                                    for pr in range(2):
                                        for qr in range(2):
                                            dr = base + pr - qr
                                            pa = (1 - pr) * 64
                                            eng = 'dve' if k_ % 2 == 0 else 'pool'
                                            k_ += 1
                                            mv = maskr[pa:pa + 64, ty, qr * 64:(qr + 1) * 64].unsqueeze(1).to_broadcast([64, 8, 64])
                                            ov = Rt[pa:pa + 64, ty, :, qr * 64:(qr + 1) * 64]
                                            if abs(dr) <= 7:
                                                T.op(eng, lambda e, ov=ov, mv=mv, pa=pa, dr=dr: e.tensor_tensor(
                                                    out=ov, in0=HK4[pa:pa + 64, :, dr + 7, :], in1=mv, op=ALU.add),
                                                    reads=T.all('HK') + ['maskr'], writes=[('Rt', k_)])
                                            else:
                                                T.op(eng, lambda e, ov=ov, mv=mv: e.tensor_copy(out=ov, in_=mv),
                                                     reads=['maskr'], writes=[('Rt', k_)])
                                T.barrier()
                        qT = sb(sub, "qT", [128, 4, Tn], BF16)
                        kT = sb(sub, "kT", [128, 4, Tn], BF16)
                        Vt = sb(sub, "Vt", [128, nTk, 512], BF16)
                        stage = [sb(sub, "kvst", [128, 512], F32) for _ in range(2)]
                        stc = [0]
                        slab_q, stq = get_slab(5)
                        for c in range(4):
                            for ti in range(s.ntile):
                                ps, pt = proj_fm(slab_q, stq, c, ti)
                                T.op('act', lambda e, ps=ps, c=c, ti=ti: e.activation(
                                    out=qT[:, c, ti * 512:(ti + 1) * 512], in_=ps[:, :], func=AF.Identity, scale=0.125),
                                    reads=[pt], writes=['qT'])
                        slab_k, stk = get_slab(6)
                        for c in range(4):
                            for ti in range(s.ntile):
                                ps, pt = proj_fm(slab_k, stk, c, ti)
                                copy_op('dve', kT[:, c, ti * 512:(ti + 1) * 512], ps[:, :], [pt], ['kT'])
                        T.mark(1)

                        def proj_tm(slab, stok, tk):
                            ps, pt = bank()

                            def emit(e):
                                ins = None
                                for k in range(16):
                                    ins = e.matmul(ps[:, :], lhsT=hT[:, k, tk * 128:(tk + 1) * 128], rhs=slab[:, k, :],
                                                   start=(k == 0), stop=(k == 15))
                                return ins
                            T.op('pe', emit, reads=[stok, ('hT', tk // 4)], writes=[pt])
                            return ps, pt
                        if not s.sample:
                            for tk in range(nTk):
                                ps, pt = proj_tm(slab_k, stk, tk)
                                r = stc[0] % 2
                                stc[0] += 1
                                copy_op('act', stage[r][:], ps[:, :], [pt], [('kvst', r)])
                                T.dma('sp', nk_out[tk // 2, l, (tk % 2) * 128:(tk % 2 + 1) * 128, :], stage[r][:],
                                      reads=[('kvst', r)], writes=[])
                        T.mark(2)
                        slab_v, stv = get_slab(7)
                        for tk in range(nTk):
                            ps, pt = proj_tm(slab_v, stv, tk)
                            copy_op('dve', Vt[:, tk, :], ps[:, :], [pt], ['Vt'])
                            if not s.sample:
                                r = stc[0] % 2
                                stc[0] += 1
                                copy_op('act', stage[r][:], ps[:, :], [pt], [('kvst', r)])
                                T.dma('sp', nv_out[tk // 2, l, (tk % 2) * 128:(tk % 2 + 1) * 128, :], stage[r][:],
                                      reads=[('kvst', r)], writes=[])
                        T.mark(3)
                        pT = [sb(sub, "pT", [128, 9 * 128], BF16) for _ in range(2)]
                        rsd = [sb(sub, "rsd", [64, 256], F32) for _ in range(2)]
                        ones64 = cstb[:, 256:320]
                        if not s.sample:
                            ychs = [sb(sub, "ychD", [128, 4, 256], BF16) for _ in range(2)]
                            it2 = 0
                            for sq_ in range(nseq):
                                ych = ychs[sq_ % 2]
                                ytok = ('ychD', sq_ % 2)
                                for h in range(8):
                                    c, po = h // 2, (h % 2) * 64
                                    r = it2 % 2
                                    it2 += 1
                                    ps, pt = bank()

                                    def emit(e, ps=ps, c=c, po=po, sq_=sq_):
                                        ins = None
                                        for kc in range(2):
                                            ins = e.matmul(ps[:, kc * 256:(kc + 1) * 256],
                                                           lhsT=kT[po:po + 64, c, sq_ * 256 + kc * 128:sq_ * 256 + (kc + 1) * 128],
                                                           rhs=qT[po:po + 64, c, sq_ * 256:(sq_ + 1) * 256], start=True, stop=True)
                                        return ins
                                    T.op('pe', emit, reads=['kT', 'qT'], writes=[pt])
                                    T.op('act', lambda e, ps=ps, r=r: e.activation(out=pT[r][:, 0:512], in_=ps[:, :], func=AF.Exp),
                                         reads=[pt], writes=[('pT', r)])
                                    T.mark(4)
                                    pso, pto = bank()

                                    def emit(e, pso=pso, r=r, h=h, sq_=sq_):
                                        ins = None
                                        for kc in range(2):
                                            ins = e.matmul(pso[0:64, 0:256], lhsT=Vt[:, sq_ * 2 + kc, h * 64:(h + 1) * 64],
                                                           rhs=pT[r][:, kc * 256:(kc + 1) * 256], start=(kc == 0), stop=(kc == 1))
                                        for kc in range(2):
                                            ins = e.matmul(pso[0:64, 256:512], lhsT=ones64,
                                                           rhs=pT[r][:, kc * 256:(kc + 1) * 256], start=(kc == 0), stop=(kc == 1))
                                        return ins
                                    T.op('pe', emit, reads=['Vt', ('pT', r), 'cstb'], writes=[pto])
                                    T.mark(5)
                                    T.op('dve', lambda e, pso=pso, r=r: e.reciprocal(out=rsd[r][:, 0:256], in_=pso[0:64, 256:512]),
                                         reads=[pto], writes=[('rsd', r)])
                                    T.mark(6)
                                    T.op('dve', lambda e, pso=pso, r=r, c=c, po=po, ych=ych: e.tensor_tensor(
                                        out=ych[po:po + 64, c, :], in0=pso[0:64, 0:256],
                                        in1=rsd[r][:, 0:256], op=ALU.mult), reads=[pto, ('rsd', r)], writes=[ytok])
                                T.dma('sp', yTd[12 * 128:16 * 128, sq_ * 256:(sq_ + 1) * 256].rearrange("(e p) t -> p e t", p=128), ych[:],
                                      reads=[ytok], writes=[T.new('yT_' + s.name)])
                        else:
                            ychs = [sb(sub, "ychD", [128, 4, 128], BF16) for _ in range(2)]
                            rt_all = [('Rt', k_) for k_ in range(1, 37)]
                            items = [(qc, h) for qc in range(16) for h in range(8)]

                            def att_s1(ii):
                                qc, h = items[ii]
                                nb = nb_config(qc)
                                ntile_ = len(nb) + 4
                                c, po = h // 2, (h % 2) * 64
                                r = ii % 2
                                nbk = (ntile_ + 3) // 4
                                banks = [bank() for _ in range(nbk)]
                                qv = qT[po:po + 64, c, qc * 128:(qc + 1) * 128]
                                for bi, (ps, pt) in enumerate(banks):
                                    def emit(e, ps=ps, bi=bi):
                                        ins = None
                                        for i in range(bi * 4, min(ntile_, bi * 4 + 4)):
                                            o = ps[:, (i % 4) * 128:(i % 4 + 1) * 128]
                                            if i < len(nb):
                                                kc, ty = nb[i]
                                                e.matmul(o, lhsT=kT[po:po + 64, c, kc * 128:(kc + 1) * 128], rhs=qv, start=True, stop=False)
                                                ins = e.matmul(o, lhsT=flip_b, rhs=Rt[:, ty, h, :], start=False, stop=True)
                                            else:
                                                a_ = i - len(nb)
                                                ins = e.matmul(o, lhsT=ckT[po:po + 64, c, a_ * 128:(a_ + 1) * 128], rhs=qv, start=True, stop=True)
                                        return ins
                                    T.op('pe', emit, reads=['kT', 'qT', 'ckT', 'cstb'] + rt_all, writes=[pt])
                                    n_ = min(ntile_, bi * 4 + 4) - bi * 4
                                    T.op('act', lambda e, ps=ps, bi=bi, n_=n_: e.activation(
                                        out=pT[r][:, bi * 512:bi * 512 + n_ * 128], in_=ps[:, 0:n_ * 128], func=AF.Exp),
                                        reads=[pt], writes=[('pT', r)])

                            def att_s2(ii):
                                qc, h = items[ii]
                                nb = nb_config(qc)
                                ntile_ = len(nb) + 4
                                c, po = h // 2, (h % 2) * 64
                                r = ii % 2
                                ych = ychs[qc % 2]
                                ytok = ('ychD', qc % 2)
                                pso, pto = psb[6 + ii % 2], ('ps', 6 + ii % 2)

                                def emit(e):
                                    ins = None
                                    for i in range(ntile_):
                                        if i < len(nb):
                                            lhs = Vt[:, nb[i][0], h * 64:(h + 1) * 64]
                                        else:
                                            lhs = cvb[:, i - len(nb), h * 64:(h + 1) * 64]
                                        ins = e.matmul(pso[0:64, 0:128], lhsT=lhs, rhs=pT[r][:, i * 128:(i + 1) * 128],
                                                       start=(i == 0), stop=(i == ntile_ - 1))
                                    for i in range(ntile_):
                                        ins = e.matmul(pso[0:64, 128:256], lhsT=ones64, rhs=pT[r][:, i * 128:(i + 1) * 128],
                                                       start=(i == 0), stop=(i == ntile_ - 1))
                                    return ins
                                T.op('pe', emit, reads=['Vt', 'cvb', ('pT', r), 'cstb'], writes=[pto])
                                T.op('dve', lambda e: e.reciprocal(out=rsd[r][:, 0:128], in_=pso[0:64, 128:256]),
                                     reads=[pto], writes=[('rsd', r)])
                                T.op('dve', lambda e: e.tensor_tensor(
                                    out=ych[po:po + 64, c, :], in0=pso[0:64, 0:128],
                                    in1=rsd[r][:, 0:128], op=ALU.mult), reads=[pto, ('rsd', r)], writes=[ytok])
                                if h == 7:
                                    T.dma('sp', yTd[12 * 128:16 * 128, qc * 128:(qc + 1) * 128].rearrange("(e p) t -> p e t", p=128), ych[:],
                                          reads=[ytok], writes=[T.new('yT_' + s.name)])

                            att_s1(0)
                            for ii in range(len(items)):
                                if ii + 1 < len(items):
                                    att_s1(ii + 1)
                                att_s2(ii)
                        T.barrier()
                    T.barrier()

                with ExitStack() as ph:
                    wo = sb(ph, "wo", [128, 16, D], BF16)
                    wov = w_out[l].rearrange("(k p) n -> p k n", p=128)
                    for q4 in range(2):
                        load_w(wo[:, :, q4 * 1024:(q4 + 1) * 1024], wov[:, :, q4 * 1024:(q4 + 1) * 1024], ('wo', q4))
                    yt = [sb(ph, "yt", [128, NCH, 512], BF16) for _ in range(2)]
                    NXC = 6
                    xc = [sb(ph, "o_xc", [128, 512], F32) for _ in range(NXC)]
                    mix = [sb(ph, "mix", [128, NCH, 512], F32) for _ in range(2)]
                    sqn = [sb(ph, "sqn", [128, 512], F32) for _ in range(2)]
                    rstd = [sb(ph, "o_rstd", [128, 512], F32) for _ in range(2)]
                    xTv = xT[s.name].rearrange("(c p) t -> p c t", p=128)
                    yTv = yTd.rearrange("(c p) t -> p c t", p=128)
                    xci = [0]

                    def o_load(ti):
                        r = ti % 2
                        T.dma('sp', yt[r][:], yTv[:, :, ti * 512:(ti + 1) * 512], reads=[], writes=[('yt', r)])

                    def o_mm(ti, n):
                        r = ti % 2
                        pss, ptss = psb[6 + r], ('ps', 6 + r)
                        ps, pt = bank()

                        def emit(e, ps=ps, n=n, r=r):
                            ins = None
                            for k in range(16):
                                ins = e.matmul(ps[:, :], lhsT=wo[:, k, n * 128:(n + 1) * 128], rhs=yt[r][:, k, :],
                                               start=(k == 0), stop=(k == 15))
                            return ins
                        T.op('pe', emit, reads=[('wo', n // 8), ('yt', r)], writes=[pt])
                        r2 = n % 2
                        T.op('act', lambda e, ps=ps, r2=r2: e.activation(out=sqn[r2][:], in_=ps[:, :], func=AF.Square),
                             reads=[pt], writes=[('sqn', r2)])
                        copy_op('dve', mix[r][:, n, :], ps[:, :], [pt], [('mix', r)])
                        T.op('pe', lambda e, pss=pss, r2=r2, n=n: e.matmul(pss[:, :], lhsT=ones_f, rhs=sqn[r2][:],
                                                                         start=(n == 0), stop=(n == NCH - 1)),
                             reads=[('sqn', r2), 'cst'], writes=[ptss])
                        if n == NCH - 1:
                            rstd_from_ps(pss, ptss, rstd[r], ('o_rstd', r))

                    def o_post(ti, c):
                        r = ti % 2
                        q = xci[0] % NXC
                        xci[0] += 1
                        T.dma('sp', xc[q][:], xTv[:, c, ti * 512:(ti + 1) * 512], reads=[], writes=[('o_xc', q)])
                        T.op('dve', lambda e: e.tensor_tensor(out=mix[r][:, c, :], in0=mix[r][:, c, :], in1=rstd[r][:, :], op=ALU.mult),
                             reads=[('mix', r), ('o_rstd', r)], writes=[('mix', r)])
                        T.op('dve', lambda e: e.scalar_tensor_tensor(
                            out=xc[q][:], in0=mix[r][:, c, :], scalar=modv[:, 1, c, j:j + 1], in1=xc[q][:],
                            op0=ALU.mult, op1=ALU.add), reads=[('mix', r), ('o_xc', q), 'modv'], writes=[('o_xc', q)])
                        T.dma('pool', xTv[:, c, ti * 512:(ti + 1) * 512], xc[q][:], reads=[('o_xc', q)], writes=[T.new('o_st')])

                    o_load(0)
                    if s.ntile > 1:
                        o_load(1)
                    for n in range(NCH):
                        o_mm(0, n)
                    for ti in range(s.ntile):
                        if ti + 2 < s.ntile:
                            o_load(ti + 2)
                        for n in range(NCH):
                            if ti + 1 < s.ntile:
                                o_mm(ti + 1, n)
                            o_post(ti, n)
                    T.barrier()

                for blk in range(Tn // 1024):
                    with ExitStack() as ph:
                        h2T = sb(ph, "h2T", [128, NCH, 1024], BF16)
                        gT = sb(ph, "gT", [128, NF, 1024], BF16)
                        rstd2 = [sb(ph, "f_rstd", [128, 512], F32) for _ in range(2)]
                        with ExitStack() as sub:
                            norm_phase(sub, s, l, blk * 2, 2, 2, 3, h2T, 'h2T', NT=128)
                            T.barrier()
                        with ExitStack() as sub:
                            FSW = 256
                            w1s = [sb(sub, "w1s", [128, 16, FSW], BF16) for _ in range(2)]
                            w3s = [sb(sub, "w3s", [128, 16, FSW], BF16) for _ in range(2)]
                            sil = [sb(sub, "sil", [128, 512], F32) for _ in range(2)]
                            w1v = w1_in[l].rearrange("(k p) n -> p k n", p=128)
                            w3v = w3_in[l].rearrange("(k p) n -> p k n", p=128)
                            it2 = 0
                            for fs in range(DFF // FSW):
                                r = fs % 2
                                load_w(w1s[r][:], w1v[:, :, fs * FSW:(fs + 1) * FSW], ('w1s', r))
                                load_w(w3s[r][:], w3v[:, :, fs * FSW:(fs + 1) * FSW], ('w3s', r))
                                for fc in range(FSW // 128):
                                    f = fs * (FSW // 128) + fc
                                    for ti in range(2):
                                        psa, pta = bank()
                                        psb_, ptb = bank()

                                        def emit(e, ps=psa, w=w1s[r], fc=fc, ti=ti):
                                            ins = None
                                            for k in range(16):
                                                ins = e.matmul(ps[:, :], lhsT=w[:, k, fc * 128:(fc + 1) * 128],
                                                               rhs=h2T[:, k, ti * 512:(ti + 1) * 512], start=(k == 0), stop=(k == 15))
                                            return ins
                                        T.op('pe', emit, reads=[('w1s', r), ('h2T', ti)], writes=[pta])

                                        def emit(e, ps=psb_, w=w3s[r], fc=fc, ti=ti):
                                            ins = None
                                            for k in range(16):
                                                ins = e.matmul(ps[:, :], lhsT=w[:, k, fc * 128:(fc + 1) * 128],
                                                               rhs=h2T[:, k, ti * 512:(ti + 1) * 512], start=(k == 0), stop=(k == 15))
                                            return ins
                                        T.op('pe', emit, reads=[('w3s', r), ('h2T', ti)], writes=[ptb])
                                        r2 = it2 % 2
                                        it2 += 1
                                        T.op('act', lambda e, psa=psa, r2=r2: e.activation(out=sil[r2][:], in_=psa[:, :], func=AF.Silu),
                                             reads=[pta], writes=[('sil', r2)])
                                        T.op('dve', lambda e, psb_=psb_, r2=r2, f=f, ti=ti: e.tensor_tensor(
                                            out=gT[:, f, ti * 512:(ti + 1) * 512], in0=sil[r2][:], in1=psb_[:, :], op=ALU.mult),
                                            reads=[('sil', r2), ptb], writes=[('gT', ti)])
                            T.barrier()
                        with ExitStack() as sub:
                            w2s = [sb(sub, "w2s", [128, NF, 256], BF16) for _ in range(2)]
                            ost = [sb(sub, "ost", [128, 512], F32) for _ in range(2)]
                            sqs = [sb(sub, "sqs", [128, 512], F32) for _ in range(2)]
                            w2v = w2_in[l].rearrange("(f p) n -> p f n", p=128)
                            pss = [psb[6], psb[7]]
                            ptss = [('ps', 6), ('ps', 7)]
                            psi2 = [0]

                            def bank6():
                                i = psi2[0] % 6
                                psi2[0] += 1
                                return psb[i], ('ps', i)
                            it2 = 0
                            for ns in range(8):
                                r = ns % 2
                                load_w(w2s[r][:], w2v[:, :, ns * 256:(ns + 1) * 256], ('w2s', r))
                                for nci in range(2):
                                    n = ns * 2 + nci
                                    for ti in range(2):
                                        ps, pt = bank6()

                                        def emit(e, ps=ps, r=r, nci=nci, ti=ti):
                                            ins = None
                                            for f in range(NF):
                                                ins = e.matmul(ps[:, :], lhsT=w2s[r][:, f, nci * 128:(nci + 1) * 128],
                                                               rhs=gT[:, f, ti * 512:(ti + 1) * 512], start=(f == 0), stop=(f == NF - 1))
                                            return ins
                                        T.op('pe', emit, reads=[('w2s', r), ('gT', ti)], writes=[pt])
                                        r2 = it2 % 2
                                        it2 += 1
                                        T.op('act', lambda e, ps=ps, r2=r2: e.activation(out=sqs[r2][:], in_=ps[:, :], func=AF.Square),
                                             reads=[pt], writes=[('sqs', r2)])
                                        copy_op('dve', ost[r2][:], ps[:, :], [pt], [('ost', r2)])
                                        T.dma('sp', oT[n * 128:(n + 1) * 128, ti * 512:(ti + 1) * 512], ost[r2][:],
                                              reads=[('ost', r2)], writes=[('oT', ti)])
                                        T.op('pe', lambda e, ti=ti, r2=r2, n=n: e.matmul(pss[ti][:, :], lhsT=ones_f, rhs=sqs[r2][:],
                                                                                        start=(n == 0), stop=(n == NCH - 1)),
                                             reads=[('sqs', r2), 'cst'], writes=[ptss[ti]])
                            for ti in range(2):
                                rstd_from_ps(pss[ti], ptss[ti], rstd2[ti], ('f_rstd', ti))
                            T.barrier()
                        with ExitStack() as sub:
                            ot = sb(sub, "ot", [128, NCH, 512], F32)
                            xt = sb(sub, "f_xt", [128, NCH, 512], F32)
                            xTv = xT[s.name].rearrange("(c p) t -> p c t", p=128)
                            oTv = oT.rearrange("(c p) t -> p c t", p=128)
                            for ti in range(2):
                                tg = blk * 2 + ti
                                T.dma('sp', ot[:], oTv[:, :, ti * 512:(ti + 1) * 512], reads=[('oT', ti)], writes=['ot'])
                                T.dma('sp', xt[:], xTv[:, :, tg * 512:(tg + 1) * 512], reads=[('xT', s.name, tg)], writes=['f_xt'])
                                T.op('dve', lambda e, ti=ti: e.tensor_tensor(
                                    out=ot[:], in0=ot[:], in1=rstd2[ti][:, :].unsqueeze(1).to_broadcast([128, NCH, 512]), op=ALU.mult),
                                    reads=['ot', ('f_rstd', ti)], writes=['ot'])
                                for c in range(NCH):
                                    T.op('dve', lambda e, c=c: e.scalar_tensor_tensor(
                                        out=xt[:, c, :], in0=ot[:, c, :], scalar=modv[:, 3, c, j:j + 1], in1=xt[:, c, :],
                                        op0=ALU.mult, op1=ALU.add), reads=['ot', 'f_xt', 'modv'], writes=['f_xt'])
                                T.dma('sp', xTv[:, :, tg * 512:(tg + 1) * 512], xt[:], reads=['f_xt'], writes=[('xT', s.name, tg)])
                            T.barrier()
                        T.barrier()

        with ExitStack() as ph:
            xin = [sb(ph, "fxin", [128, NCH, 128], F32) for _ in range(2)]
            xo = [sb(ph, "fxo", [128, D], F32) for _ in range(2)]
            it = 0
            for s in segs:
                xTv = xT[s.name].rearrange("(c p) t -> p c t", p=128)
                for tc in range(s.T // 128):
                    r = it % 2
                    it += 1
                    T.dma('sp', xin[r][:], xTv[:, :, tc * 128:(tc + 1) * 128], reads=[('xT', s.name, tc // 4)], writes=[('fxin', r)])
                    for g in range(4):
                        ps, pt = bank()

                        def emit(e, ps=ps, r=r, g=g):
                            ins = None
                            for jj in range(4):
                                ins = e.transpose(out=ps[:, jj * 128:(jj + 1) * 128], in_=xin[r][:, 4 * g + jj, :], identity=ident)
                            return ins
                        T.op('pe', emit, reads=[('fxin', r), 'cst'], writes=[pt])
                        copy_op(evac_engine(g), xo[r][:, g * 512:(g + 1) * 512], ps[:, :], [pt], [('fxo', r, g)])
                    T.dma('sp', y_out[s.name][tc * 128:(tc + 1) * 128, :], xo[r][:],
                          reads=[('fxo', r, g) for g in range(4)], writes=[])
            T.barrier()


_CACHE = {}


def make_in_maps(inputs, ncores=8):
    hc = host_consts()
    f = lambda a: np.ascontiguousarray(np.asarray(a, dtype=np.float32))
    shared = {}
    for nm in ["w_ada", "b_ada", "g_pre_mix", "g_post_mix", "g_pre_ffn", "g_post_ffn", "w_in", "pool_w", "pool_scale",
               "lru_conv_b", "lru_gate_w", "cm_dw_b", "cm_ln_g", "cm_ln_b", "cm_pw_w", "cm_pw_b", "w_out",
               "ffn_w1", "ffn_w3", "ffn_w2"]:
        shared[nm] = f(inputs[nm])
    shared["lru_conv_w"] = f(inputs["lru_conv_w"]).reshape(DEPTH, 4 * 512)
    shared["lru_gate_b"] = f(inputs["lru_gate_b"]).reshape(DEPTH, 4 * 512)
    shared["lru_lambda"] = f(inputs["lru_lambda"]).reshape(DEPTH, 2 * 512)
    shared["cm_dw_w"] = f(inputs["cm_dw_w"]).reshape(DEPTH, 31 * 512)
    shared["na_rpb"] = f(inputs["na_rpb"]).reshape(DEPTH, 120, 31)
    shared.update(hc)
    xp = f(inputs["x_prompt"])
    xs = f(inputs["x_sample"])
    ck = f(inputs["cache_k"])
    cv = f(inputs["cache_v"])
    st = f(inputs["state_lru"])
    c = f(inputs["c"])
    cctx = f(inputs["c_ctx"])
    nsmp = xs.shape[0]
    maps = []
    for i in range(ncores):
        si = i % nsmp
        m = dict(shared)
        m["xp"] = np.ascontiguousarray(xp[i * NPSEQ:(i + 1) * NPSEQ].reshape(NPSEQ * LP, D))
        m["xs"] = np.ascontiguousarray(xs[si])
        m["ck"] = np.ascontiguousarray(ck[si].reshape(DEPTH, PAST, 512))
        m["cv"] = np.ascontiguousarray(cv[si].reshape(DEPTH, PAST, 512))
        m["st"] = np.ascontiguousarray(st[si].reshape(-1))
        m["cond"] = np.ascontiguousarray(np.stack([cctx, c[si]], axis=0))
        maps.append(m)
    return maps


def kernel(**inputs):
    key = "full"
    if key not in _CACHE:
        _CACHE[key] = build_program()
    nc = _CACHE[key]
    maps = make_in_maps(inputs, 8)
    res = run_bass_kernel_spmd(nc, maps, core_ids=list(range(8)))
    rs = res.results
    B = 32
    y_prompt = np.concatenate([rs[i]["yp"].reshape(NPSEQ, LP, D) for i in range(8)], axis=0).astype(np.float32)
    y_sample = np.stack([rs[0]["ys"], rs[1]["ys"]], axis=0).astype(np.float32)
    nk = np.concatenate([rs[i]["nk"] for i in range(8)], axis=0).reshape(B, DEPTH, LP, 8, 64).astype(np.float32)
    nv = np.concatenate([rs[i]["nv"] for i in range(8)], axis=0).reshape(B, DEPTH, LP, 8, 64).astype(np.float32)
    nst = np.concatenate([rs[i]["nst"] for i in range(8)], axis=0).reshape(B, DEPTH, 2, 512).astype(np.float32)
    return (y_prompt, y_sample, nk, nv, nst)
```

```python
import numpy as np
from contextlib import ExitStack
import concourse.bass as bass
import concourse.mybir as mybir
from concourse.bass_utils import run_bass_kernel_spmd

F32 = mybir.dt.float32
BF16 = mybir.dt.bfloat16
AF = mybir.ActivationFunctionType
ALU = mybir.AluOpType

D = 2048
NCH = 16
DEPTH = 4
DIN = 4096
DFF = 5632
NF = 44
NMOD = 6
PAST = 512
NEG = -30000.0
EPS = 1e-6
NPSEQ = 4
LP = 256
LS = 2048
GRID_W = 64

PV_ROWS = {}
_lay = [
    [("b_ada", 96), ("g_pre_mix", 16), ("g_post_mix", 16)],
    [("g_pre_ffn", 16), ("g_post_ffn", 16), ("pool_scale", 4), ("lru_conv_w", 16), ("lru_conv_b", 4),
     ("lru_gate_b", 16), ("lru_lambda", 8), ("cm_dw_b", 4), ("cm_ln_g", 4), ("cm_ln_b", 4), ("cm_pw_b", 4)],
    [("cm_dw_w", 124)],
]
for _ti, _names in enumerate(_lay):
    _r = 0
    for _n, _k in _names:
        PV_ROWS[_n] = (_ti, _r, _k)
        _r += _k
    assert _r <= 128


class StopBuild(Exception):
    pass


class _CntEng:
    def __init__(self, eng):
        self._e = eng
        self.n = 0

    def __getattr__(self, name):
        f = getattr(self._e, name)
        if name in ('matmul', 'transpose'):
            def g(*a, **k):
                self.n += 1
                return f(*a, **k)
            return g
        return f


class Trk:
    NDMA = 10

    def __init__(self, nc, es):
        self.nc = nc
        self.eng = {'pe': _CntEng(nc.tensor), 'act': nc.scalar, 'dve': nc.vector, 'pool': nc.gpsimd, 'sp': nc.sync}
        self.phases = []
        self.sem = {}
        self.cnt = {}
        for k in self.eng:
            self.sem[k] = es.enter_context(nc.semaphore("s_" + k))
            self.cnt[k] = 0
        self.dq = {}
        for q in ('sp', 'pool'):
            keys = []
            for i in range(self.NDMA):
                k = "d_%s_%d" % (q, i)
                self.sem[k] = es.enter_context(nc.semaphore(k))
                self.cnt[k] = 0
                keys.append(k)
            self.dq[q] = [keys, 0]
        self.seen = {k: {} for k in self.eng}
        self.lw = {}
        self.rd = {}
        self.ninst = 0
        self.grp = {}
        self.stop_at = 0
        self.stopped = False

    def new(self, grp):
        l = self.grp.setdefault(grp, [])
        t = (grp, '#', len(l))
        l.append(t)
        return t

    def all(self, grp):
        return list(self.grp.get(grp, []))

    def _deps(self, e, reads, writes):
        deps = {}

        def add(ev, raw):
            k, v = ev
            if k == e and not raw:
                return
            if deps.get(k, 0) < v:
                deps[k] = v
        for t in reads:
            if t in self.lw:
                add(self.lw[t], True)
            if isinstance(t, tuple) and t[0] == 'ps':
                for ev in self.rd.get(t, {}).items():
                    add(ev, False)
        for t in writes:
            if t in self.lw:
                add(self.lw[t], False)
            for ev in self.rd.get(t, {}).items():
                add(ev, False)
        return deps

    def _wait(self, e, deps):
        eng = self.eng[e]
        seen = self.seen[e]
        for k, v in deps.items():
            if seen.get(k, 0) < v:
                eng.wait_ge(self.sem[k], v)
                seen[k] = v
                self.ninst += 1

    def _commit(self, ev, reads, writes):
        k, v = ev
        for t in writes:
            self.lw[t] = ev
            self.rd[t] = {}
        for t in reads:
            d = self.rd.setdefault(t, {})
            if d.get(k, 0) < v:
                d[k] = v

    def op(self, e, emit, reads=(), writes=()):
        if self.stopped:
            return
        self._wait(e, self._deps(e, reads, writes))
        inst = emit(self.eng[e])
        self.cnt[e] += 1
        inst.then_inc(self.sem[e], 1)
        self.ninst += 1
        self._commit((e, self.cnt[e]), reads, writes)

    def dma(self, q, out, in_, reads=(), writes=(), **kw):
        if self.stopped:
            return
        keys, i = self.dq[q]
        k = keys[i % len(keys)]
        self.dq[q][1] = i + 1
        deps = self._deps(None, reads, writes)
        if self.cnt[k] > 0:
            deps[k] = max(deps.get(k, 0), self.cnt[k])
        self._wait(q, deps)
        self.eng[q].dma_start(out=out, in_=in_, **kw).then_inc(self.sem[k], 16)
        self.cnt[k] += 16
        self.ninst += 1
        self._commit((k, self.cnt[k]), reads, writes)

    def barrier(self):
        if self.stopped:
            return
        self.nbar = getattr(self, 'nbar', 0) + 1
        self.phases.append((self.nbar, self.eng['pe'].n))
        for e in self.eng:
            self._wait(e, {k: v for k, v in self.cnt.items() if v > 0 and k != e})
        self.lw = {}
        self.rd = {}
        self.grp = {}
        if self.stop_at and self.nbar >= self.stop_at:
            self.stopped = True

    def mark(self, n):
        import os
        if int(os.environ.get('STOPM', '0')) == n:
            self.stopped = True

    def finish(self):
        self.stopped = False
        self._wait('sp', {k: v for k, v in self.cnt.items() if v > 0 and k != 'sp'})


class Seg:
    def __init__(self, name, T, L, cond):
        self.name = name
        self.T = T
        self.L = L
        self.nseq = T // L
        self.cond = cond
        self.ntile = T // 512
        self.sample = (name == 'S')


def host_consts():
    c = {}
    ident = np.eye(128, dtype=np.float32)
    flip = ident[::-1].copy()
    ones = np.ones((128, 128), np.float32)
    c['cst'] = np.concatenate([ident, flip, ones], axis=1)
    er = np.ones((4, 16), np.float32)
    Lh = 64
    for gi, w in enumerate((2, 4, 8, 16)):
        for t in range(8):
            lo = max(t - w // 2, 0)
            hi = min(t - w // 2 + w, Lh)
            er[gi, t] = w / float(hi - lo)
        for i in range(8):
            t = Lh - 8 + i
            lo = max(t - w // 2, 0)
            hi = min(t - w // 2 + w, Lh)
            er[gi, 8 + i] = w / float(hi - lo)
    c['edger'] = np.broadcast_to(er.reshape(1, 64), (128, 64)).copy()
    bases = [-6, -4, -2, 0, 2, 4, 6, -4, 4]
    col = np.arange(64)
    cs = np.clip(col - 8, 0, 48)
    mk = np.zeros((128, 9, 128), np.float32)
    for ty, base in enumerate(bases):
        interior = ty >= 7
        for pr in range(2):
            for qr in range(2):
                dr = base + pr - qr
                rowok = abs(dr) <= 7 and ((not interior) or (-4 <= dr <= 3))
                for pp in range(64):
                    kc = 63 - pp
                    ok = rowok & (kc >= cs) & (kc < cs + 16)
                    mk[(1 - pr) * 64 + pp, ty, qr * 64:(qr + 1) * 64] = np.where(ok, 0.0, NEG)
    c['maskr'] = mk
    return c


RT_BASES = [-6, -4, -2, 0, 2, 4, 6, -4, 4]


def nb_config(qc):
    edge = {-6: 0, -4: 1, -2: 2, 0: 3, 2: 4, 4: 5, 6: 6}
    if qc == 0:
        return [(j, edge[2 * j]) for j in range(4)]
    if qc == 1:
        return [(j, edge[2 * j - 2]) for j in range(4)]
    if qc == 14:
        return [(12 + j, edge[2 * j - 4]) for j in range(4)]
    if qc == 15:
        return [(12 + j, edge[2 * j - 6]) for j in range(4)]
    out = []
    for j in range(5):
        b = 2 * j - 4
        ty = 7 if j == 0 else (8 if j == 4 else edge[b])
        out.append((qc - 2 + j, ty))
    return out


def build_program(nl=DEPTH, do_sample=True, do_prompt=True, stop_at=0):
    nc = bass.Bass("TRN2", target_bir_lowering=False)
    din = {}

    def dram_in(name, shape):
        din[name] = nc.dram_tensor(name, list(shape), F32, kind="ExternalInput").ap()
        return din[name]

    xp_in = dram_in("xp", [NPSEQ * LP, D])
    xs_in = dram_in("xs", [LS, D])
    ck_in = dram_in("ck", [DEPTH, PAST, 512])
    cv_in = dram_in("cv", [DEPTH, PAST, 512])
    st_in = dram_in("st", [DEPTH * 2 * 512])
    cond_in = dram_in("cond", [2, D])
    w_ada = dram_in("w_ada", [DEPTH, D, NMOD * D])
    vec_in = {}
    for nm, n in [("b_ada", NMOD * D), ("g_pre_mix", D), ("g_post_mix", D), ("g_pre_ffn", D), ("g_post_ffn", D),
                  ("pool_scale", 512), ("lru_conv_w", 4 * 512), ("lru_conv_b", 512), ("lru_gate_b", 4 * 512),
                  ("lru_lambda", 2 * 512), ("cm_dw_w", 31 * 512), ("cm_dw_b", 512), ("cm_ln_g", 512),
                  ("cm_ln_b", 512), ("cm_pw_b", 512)]:
        vec_in[nm] = dram_in(nm, [DEPTH, n])
    w_in = dram_in("w_in", [DEPTH, D, DIN])
    pool_w = dram_in("pool_w", [DEPTH, 4, 128, 128])
    gate_w = dram_in("lru_gate_w", [DEPTH, 2, 2, 8, 64, 64])
    pw_w = dram_in("cm_pw_w", [DEPTH, 512, 512])
    rpb_in = dram_in("na_rpb", [DEPTH, 120, 31])
    w_out = dram_in("w_out", [DEPTH, D, D])
    w1_in = dram_in("ffn_w1", [DEPTH, D, DFF])
    w3_in = dram_in("ffn_w3", [DEPTH, D, DFF])
    w2_in = dram_in("ffn_w2", [DEPTH, DFF, D])
    cst_in = dram_in("cst", [128, 384])
    edger_in = dram_in("edger", [128, 64])
    maskr_in = dram_in("maskr", [128, 9, 128])

    yp_out = nc.dram_tensor("yp", [NPSEQ * LP, D], F32, kind="ExternalOutput").ap()
    ys_out = nc.dram_tensor("ys", [LS, D], F32, kind="ExternalOutput").ap()
    nk_out = nc.dram_tensor("nk", [NPSEQ, DEPTH, LP, 512], F32, kind="ExternalOutput").ap()
    nv_out = nc.dram_tensor("nv", [NPSEQ, DEPTH, LP, 512], F32, kind="ExternalOutput").ap()
    nst_out = nc.dram_tensor("nst", [NPSEQ, DEPTH, 2, 512], F32, kind="ExternalOutput").ap()

    segP = Seg('P', NPSEQ * LP, LP, 0)
    segS = Seg('S', LS, LS, 1)
    segs = ([segP] if do_prompt else []) + ([segS] if do_sample else [])
    xT = {s.name: nc.dram_tensor("xT_" + s.name, [D, s.T], F32, kind="Internal").ap() for s in (segP, segS)}
    yT = {s.name: nc.dram_tensor("yT_" + s.name, [D, s.T], BF16, kind="Internal").ap() for s in (segP, segS)}
    oT = nc.dram_tensor("oT", [D, 1024], F32, kind="Internal").ap()
    zpd = nc.dram_tensor("zpd", [120, 128], F32, kind="Internal").ap()
    x_in = {'P': xp_in, 'S': xs_in}
    y_out = {'P': yp_out, 'S': ys_out}

    es = ExitStack()
    with es:
        T = Trk(nc, es)
        T.stop_at = stop_at
        uid = [0]
        try:
            _emit_all(locals())
        except StopBuild:
            pass
        T.finish()
    nc._trk_ninst = T.ninst
    nc._trk_phases = T.phases
    return nc


def _emit_all(env):
    nc = env['nc']; T = env['T']; uid = env['uid']; es = env['es']
    segs = env['segs']; nl = env['nl']
    if True:
        xp_in = env['xp_in']; xs_in = env['xs_in']; ck_in = env['ck_in']; cv_in = env['cv_in']; st_in = env['st_in']
        cond_in = env['cond_in']; w_ada = env['w_ada']; vec_in = env['vec_in']; w_in = env['w_in']; pool_w = env['pool_w']
        gate_w = env['gate_w']; pw_w = env['pw_w']; rpb_in = env['rpb_in']; w_out = env['w_out']; w1_in = env['w1_in']
        w3_in = env['w3_in']; w2_in = env['w2_in']; cst_in = env['cst_in']; edger_in = env['edger_in']; maskr_in = env['maskr_in']
        yp_out = env['yp_out']; ys_out = env['ys_out']; nk_out = env['nk_out']; nv_out = env['nv_out']; nst_out = env['nst_out']
        segP = env['segP']; segS = env['segS']; xT = env['xT']; yT = env['yT']; oT = env['oT']; zpd = env['zpd']
        x_in = env['x_in']; y_out = env['y_out']

        def sb(stack, name, shape, dt):
            uid[0] += 1
            return stack.enter_context(nc.sbuf_tensor("%s_%d" % (name, uid[0]), list(shape), dt))

        psb = [es.enter_context(nc.psum_tensor("psb%d" % i, [128, 512], F32)) for i in range(8)]
        psi = [0]

        def bank():
            i = psi[0] % 6
            psi[0] += 1
            return psb[i], ('ps', i)

        cst = sb(es, "cst", [128, 384], F32)
        T.dma('sp', cst[:], cst_in[:, :], writes=['cst'])
        ident = cst[:, 0:128]
        ones_f = cst[:, 256:384]
        cstb = sb(es, "cstb", [128, 384], BF16)
        T.op('dve', lambda e: e.tensor_copy(out=cstb[:], in_=cst[:]), reads=['cst'], writes=['cstb'])
        ident_b = cstb[:, 0:128]
        flip_b = cstb[:, 128:256]
        epst = sb(es, "epst", [128, 1], F32)
        T.op('pool', lambda e: e.memset(epst[:], EPS), writes=['epst'])
        edger = sb(es, "edger", [128, 4, 16], F32)
        T.dma('sp', edger[:].rearrange("p a b -> p (a b)"), edger_in[:, :], writes=['edger'])
        pv = sb(es, "pv", [128, 3, 128], F32)
        mod = sb(es, "mod", [128, 96, 2], F32)
        modv = sb(es, "modv", [128, 4, 16, 2], F32)
        condT = sb(es, "condT", [128, 16, 2], BF16)
        h0all = sb(es, "h0all", [128, 32], F32)
        nsp = sb(es, "nsp", [128, 2, 8], F32)
        gw = sb(es, "gw", [128, 16, 128], BF16)

        def pvc(name, idx):
            ti, r0, n = PV_ROWS[name]
            return pv[:, ti, r0 + idx:r0 + idx + 1]

        def pvr(name):
            ti, r0, n = PV_ROWS[name]
            return pv[:, ti, r0:r0 + n]

        CONST_R = ['cst', 'cstb', 'epst']

        def evac_engine(i):
            return 'act' if i % 2 == 0 else 'dve'

        def copy_op(e, out, in_, reads, writes):
            if e == 'act':
                T.op('act', lambda g: g.activation(out=out, in_=in_, func=AF.Identity), reads=reads, writes=writes)
            else:
                T.op(e, lambda g: g.tensor_copy(out=out, in_=in_), reads=reads, writes=writes)

        with ExitStack() as ph:
            xin = [sb(ph, "xin", [128, D], F32) for _ in range(2)]
            xo = [sb(ph, "xo", [128, NCH, 128], F32) for _ in range(2)]
            it = 0
            for s in segs:
                xTv = xT[s.name].rearrange("(c p) t -> p c t", p=128)
                for tc in range(s.T // 128):
                    r = it % 2
                    it += 1
                    T.dma('sp', xin[r][:], x_in[s.name][tc * 128:(tc + 1) * 128, :], writes=[('xin', r)])
                    for g in range(4):
                        ps, pt = bank()

                        def emit(e, ps=ps, r=r, g=g):
                            ins = None
                            for j in range(4):
                                c = 4 * g + j
                                ins = e.transpose(out=ps[:, j * 128:(j + 1) * 128], in_=xin[r][:, c * 128:(c + 1) * 128],
                                                  identity=ident)
                            return ins
                        T.op('pe', emit, reads=[('xin', r), 'cst'], writes=[pt])
                        copy_op(evac_engine(g), xo[r][:, 4 * g:4 * g + 4, :],
                                ps[:, :].rearrange("p (a b) -> p a b", b=128), [pt], [('xo', r, g)])
                    T.dma('sp', xTv[:, :, tc * 128:(tc + 1) * 128], xo[r][:],
                          reads=[('xo', r, g) for g in range(4)], writes=[T.new('xT0')])
            cs = sb(ph, "cs", [2, D], F32)
            T.dma('sp', cs[:], cond_in[:, :], writes=['cs'])
            ps, pt = bank()

            def emit(e, ps=ps):
                ins = None
                for c in range(16):
                    ins = e.transpose(out=ps[:, c * 2:(c + 1) * 2], in_=cs[0:2, c * 128:(c + 1) * 128],
                                      identity=cst[0:2, 0:2])
                return ins
            T.op('pe', emit, reads=['cs', 'cst'], writes=[pt])
            T.op('act', lambda e: e.activation(out=condT[:].rearrange("p a b -> p (a b)"), in_=ps[:, 0:32], func=AF.Silu),
                 reads=[pt], writes=['condT'])
            sst = sb(ph, "sst", [32, 128], F32)
            T.dma('sp', sst[:], st_in.rearrange("(n p) -> n p", p=128), writes=['sst'])
            ps, pt = bank()
            T.op('pe', lambda e: e.transpose(out=ps[:, 0:32], in_=sst[:, :], identity=cst[0:32, 0:32]),
                 reads=['sst', 'cst'], writes=[pt])
            T.op('dve', lambda e: e.tensor_copy(out=h0all[:], in_=ps[:, 0:32]), reads=[pt], writes=['h0all'])
            T.barrier()

        def load_w(dst, src, tok):
            T.dma('pool', dst, src, writes=[tok])

        def norm_phase(stack, s, l, t0, ntile, gi, shi, hT, htok, NT=256):
            j = s.cond
            xTv = xT[s.name].rearrange("(c p) t -> p c t", p=128)
            xt = [sb(stack, "n_xt", [128, NCH, NT], F32) for _ in range(2)]
            sq = [sb(stack, "n_sq", [128, NCH, NT], F32) for _ in range(2)]
            rstd = [sb(stack, "n_rstd", [128, NT], F32) for _ in range(2)]
            npp = 512 // NT
            npi = ntile * npp
            pbank = {}

            def n_a1(pi):
                ti = pi // npp
                tg = t0 + ti
                c0 = tg * 512 + (pi % npp) * NT
                r = pi % 2
                for g4 in range(4):
                    T.dma('sp', xt[r][:, 4 * g4:4 * g4 + 4, :], xTv[:, 4 * g4:4 * g4 + 4, c0:c0 + NT],
                          reads=[('xT', s.name, tg)], writes=[('n_xt', r, g4)])
                xtoks = [('n_xt', r, g4) for g4 in range(4)]
                T.op('act', lambda e: e.activation(out=sq[r][:], in_=xt[r][:], func=AF.Square),
                     reads=xtoks, writes=[('n_sq', r)])
                ps, pt = bank()
                pbank[pi] = (ps, pt)

                def emit(e):
                    ins = None
                    for c in range(NCH):
                        ins = e.matmul(ps[:, 0:NT], lhsT=ones_f, rhs=sq[r][:, c, :], start=(c == 0), stop=(c == NCH - 1))
                    return ins
                T.op('pe', emit, reads=[('n_sq', r), 'cst'], writes=[pt])

            def n_a2(pi):
                r = pi % 2
                ps, pt = pbank[pi]
                T.op('act', lambda e: e.activation(out=rstd[r][:], in_=ps[:, 0:NT], func=AF.Sqrt, bias=epst[:, 0:1],
                                                   scale=1.0 / D), reads=[pt, 'epst'], writes=[('n_rstd', r)])
                T.op('dve', lambda e: e.reciprocal(out=rstd[r][:], in_=rstd[r][:]),
                     reads=[('n_rstd', r)], writes=[('n_rstd', r)])

            def n_b(pi):
                ti = pi // npp
                h0 = ti * 512 + (pi % npp) * NT
                r = pi % 2
                xtoks = [('n_xt', r, g4) for g4 in range(4)]
                T.op('dve', lambda e: e.tensor_tensor(out=sq[r][:], in0=xt[r][:],
                                                      in1=rstd[r][:, :].unsqueeze(1).to_broadcast([128, NCH, NT]),
                                                      op=ALU.mult),
                     reads=xtoks + [('n_rstd', r)], writes=[('n_sq', r)])
                for c in range(NCH):
                    gsc = modv[:, gi, c, j:j + 1]
                    shc = mod[:, shi * 16 + c, j:j + 1]
                    dst = hT[:, c, h0:h0 + NT]
                    if c % 2 == 0:
                        T.op('act', lambda e, dst=dst, c=c, gsc=gsc, shc=shc: e.activation(
                            out=dst, in_=sq[r][:, c, :], func=AF.Identity, bias=shc, scale=gsc),
                            reads=[('n_sq', r), 'modv', 'mod'], writes=[(htok, ti)])
                    else:
                        T.op('dve', lambda e, dst=dst, c=c, gsc=gsc, shc=shc: e.tensor_scalar(
                            out=dst, in0=sq[r][:, c, :], scalar1=gsc, scalar2=shc, op0=ALU.mult, op1=ALU.add),
                            reads=[('n_sq', r), 'modv', 'mod'], writes=[(htok, ti)])

            n_a1(0)
            n_a2(0)
            for pi in range(npi):
                if pi + 1 < npi:
                    n_a1(pi + 1)
                n_b(pi)
                if pi + 1 < npi:
                    n_a2(pi + 1)

        def rstd_from_ps(ps, pt, dst, dtok, n=512, scale=1.0 / D):
            T.op('act', lambda e: e.activation(out=dst[:, 0:n], in_=ps[:, 0:n], func=AF.Sqrt, bias=epst[:, 0:1], scale=scale),
                 reads=[pt, 'epst'], writes=[dtok])
            T.op('dve', lambda e: e.reciprocal(out=dst[:, 0:n], in_=dst[:, 0:n]), reads=[dtok], writes=[dtok])

        for l in range(nl):
            with ExitStack() as ph:
                stg = sb(ph, "pstg", [128, 3, 128], F32)
                T.op('pool', lambda e: e.memset(stg[:], 0.0), writes=['pstg'])
                for nm, (ti, r0, n) in PV_ROWS.items():
                    T.dma('sp', stg[r0:r0 + n, ti, :], vec_in[nm][l].rearrange("(n p) -> n p", p=128),
                          reads=['pstg'], writes=[T.new('pstg_d')])
                for ti in range(3):
                    ps, pt = bank()
                    T.op('pe', lambda e, ps=ps, ti=ti: e.transpose(out=ps[:, 0:128], in_=stg[:, ti, :], identity=ident),
                         reads=['pstg', 'cst'] + T.all('pstg_d'), writes=[pt])
                    copy_op('dve', pv[:, ti, :], ps[:, 0:128], [pt], ['pv'])
                lam = pvr("lru_lambda")
                tmp = sb(ph, "lamtmp", [128, 8], F32)
                T.op('act', lambda e: e.activation(out=tmp[:], in_=lam, func=AF.Exp, scale=-1.0), reads=['pv'], writes=['lamtmp'])
                T.op('act', lambda e: e.activation(out=tmp[:], in_=tmp[:], func=AF.Ln, bias=1.0, scale=1.0),
                     reads=['lamtmp'], writes=['lamtmp'])
                T.op('dve', lambda e: e.tensor_scalar(out=nsp[:, 0, :], in0=tmp[:], scalar1=-8.0, scalar2=None, op0=ALU.mult),
                     reads=['lamtmp'], writes=['nsp'])
                T.op('dve', lambda e: e.tensor_scalar(out=nsp[:, 1, :], in0=tmp[:], scalar1=-16.0, scalar2=None, op0=ALU.mult),
                     reads=['lamtmp'], writes=['nsp'])
                T.op('pool', lambda e: e.memset(gw[:], 0.0), writes=['gw'])
                for d in range(2):
                    for g in range(2):
                        for k in range(8):
                            po = (k % 2) * 64
                            T.dma('pool', gw[po:po + 64, (d * 2 + g) * 4 + k // 2, po:po + 64], gate_w[l, d, g, k],
                                  reads=['gw'], writes=[T.new('gw_d')])
                wsl = [sb(ph, "wada", [128, 16, 1024], BF16) for _ in range(3)]
                wv = w_ada[l].rearrange("(k p) n -> p k n", p=128)
                psm, ptm = bank()
                for si in range(12):
                    r = si % 3
                    load_w(wsl[r][:], wv[:, :, si * 1024:(si + 1) * 1024], ('wada', r))

                    def emit(e, r=r, si=si):
                        ins = None
                        for jn in range(8):
                            n = si * 8 + jn
                            for k in range(16):
                                ins = e.matmul(psm[:, n * 2:(n + 1) * 2], lhsT=wsl[r][:, k, jn * 128:(jn + 1) * 128],
                                               rhs=condT[:, k, :], start=(k == 0), stop=(k == 15))
                        return ins
                    T.op('pe', emit, reads=[('wada', r), 'condT'], writes=[ptm])
                bada = pvr("b_ada")
                T.op('dve', lambda e: e.tensor_tensor(out=mod[:], in0=psm[:, 0:192].rearrange("p (a b) -> p a b", b=2),
                                                      in1=bada.unsqueeze(2).to_broadcast([128, 96, 2]), op=ALU.add),
                     reads=[ptm, 'pv'], writes=['mod'])
                for a, (sci, gname) in enumerate([(1, "g_pre_mix"), (2, "g_post_mix"), (4, "g_pre_ffn"), (5, "g_post_ffn")]):
                    gv = pvr(gname).unsqueeze(2).to_broadcast([128, 16, 2])
                    src = mod[:, sci * 16:(sci + 1) * 16, :]
                    if a % 2 == 0:
                        T.op('dve', lambda e, a=a, src=src, gv=gv: e.scalar_tensor_tensor(
                            out=modv[:, a, :, :], in0=src, scalar=1.0, in1=gv, op0=ALU.add, op1=ALU.mult),
                            reads=['mod', 'pv'], writes=['modv'])
                    else:
                        T.op('dve', lambda e, a=a, src=src, gv=gv: e.tensor_tensor(
                            out=modv[:, a, :, :], in0=src, in1=gv, op=ALU.mult), reads=['mod', 'pv'], writes=['modv'])
                T.barrier()

            for s in segs:
                j = s.cond
                Tn, L, nseq = s.T, s.L, s.nseq
                yTd = yT[s.name]
                spt = max(1, 512 // L)
                ntl = min(L, 512)

                def tview(buf3, padl, ti):
                    if L >= 512:
                        sq_ = (ti * 512) // L
                        off = (ti * 512) % L
                        return buf3[:, sq_, padl + off:padl + off + 512]
                    return buf3[:, ti * spt:(ti + 1) * spt, padl:padl + L]

                def pview(ps):
                    if L >= 512:
                        return ps[:, :]
                    return ps[:, :].rearrange("p (a b) -> p a b", b=L)

                with ExitStack() as ph:
                    hT = sb(ph, "hT", [128, NCH, Tn], BF16)
                    with ExitStack() as sub:
                        norm_phase(sub, s, l, 0, s.ntile, 0, 0, hT, 'hT')
                        T.barrier()
                    wslab = [sb(ph, "wslab", [128, 16, 512], BF16) for _ in range(2)]
                    wiv = w_in[l].rearrange("(k p) n -> p k n", p=128)
                    wcnt = [0]

                    def get_slab(si):
                        r = wcnt[0] % 2
                        wcnt[0] += 1
                        load_w(wslab[r][:], wiv[:, :, si * 512:(si + 1) * 512], ('wslab', r))
                        return wslab[r], ('wslab', r)

                    def proj_fm(slab, stok, jc, ti):
                        ps, pt = bank()

                        def emit(e):
                            ins = None
                            for k in range(16):
                                ins = e.matmul(ps[:, :], lhsT=slab[:, k, jc * 128:(jc + 1) * 128],
                                               rhs=hT[:, k, ti * 512:(ti + 1) * 512], start=(k == 0), stop=(k == 15))
                            return ins
                        T.op('pe', emit, reads=[stok, ('hT', ti)], writes=[pt])
                        return ps, pt

                    with ExitStack() as sub:
                        Lp = L + 16
                        up = sb(sub, "up", [128, nseq, Lp], F32)
                        lv = [sb(sub, "lv", [128, nseq, Lp], F32) for _ in range(2)]
                        pooled = sb(sub, "pooled", [128, Tn], BF16)
                        ych = sb(sub, "ychA", [128, Tn], BF16)
                        pwt = sb(sub, "poolw", [128, 4, 128], BF16)
                        load_w(pwt[:], pool_w[l].rearrange("g c d -> c g d"), 'poolw')
                        T.op('pool', lambda e: e.memset(up[:], 0.0), writes=['up'])
                        slab, stok = get_slab(0)
                        for g in range(4):
                            w = 2 << g
                            for ti in range(s.ntile):
                                ps, pt = proj_fm(slab, stok, g, ti)
                                copy_op(evac_engine(ti), tview(up, 8, ti), pview(ps), [pt], ['up'])
                            cur, ctok = up, 'up'
                            m = 1
                            lo, hi = 0, Lp
                            step = 0
                            while m < w:
                                dst = lv[step % 2]
                                dtok = ('lv', step % 2)
                                if m == 1:
                                    nlo, nhi = lo + 1, hi
                                    a0 = cur[:, :, nlo - 1:nhi - 1]
                                    a1 = cur[:, :, nlo:nhi]
                                else:
                                    h2 = m // 2
                                    nlo, nhi = lo + h2, hi - h2
                                    a0 = cur[:, :, nlo - h2:nhi - h2]
                                    a1 = cur[:, :, nlo + h2:nhi + h2]
                                T.op('dve', lambda e, dst=dst, a0=a0, a1=a1, nlo=nlo, nhi=nhi: e.tensor_tensor(
                                    out=dst[:, :, nlo:nhi], in0=a0, in1=a1, op=ALU.add), reads=[ctok], writes=[dtok])
                                cur, ctok = dst, dtok
                                lo, hi = nlo, nhi
                                m *= 2
                                step += 1
                            T.op('dve', lambda e, cur=cur, w=w: e.tensor_scalar(
                                out=cur[:, :, 8:8 + L], in0=cur[:, :, 8:8 + L], scalar1=1.0 / w, scalar2=None, op0=ALU.mult),
                                reads=[ctok], writes=[ctok])
                            T.op('dve', lambda e, cur=cur, g=g: e.tensor_tensor(
                                out=cur[:, :, 8:16], in0=cur[:, :, 8:16],
                                in1=edger[:, g, 0:8].unsqueeze(1).to_broadcast([128, nseq, 8]), op=ALU.mult),
                                reads=[ctok, 'edger'], writes=[ctok])
                            T.op('dve', lambda e, cur=cur, g=g: e.tensor_tensor(
                                out=cur[:, :, L:L + 8], in0=cur[:, :, L:L + 8],
                                in1=edger[:, g, 8:16].unsqueeze(1).to_broadcast([128, nseq, 8]), op=ALU.mult),
                                reads=[ctok, 'edger'], writes=[ctok])
                            T.op('dve', lambda e, cur=cur: e.tensor_tensor(
                                out=pooled[:, :].rearrange("p (a b) -> p a b", b=L), in0=cur[:, :, 8:8 + L],
                                in1=up[:, :, 8:8 + L], op=ALU.subtract), reads=[ctok, 'up'], writes=['pooled'])
                            for ti in range(s.ntile):
                                ps, pt = bank()
                                T.op('pe', lambda e, ps=ps, g=g, ti=ti: e.matmul(
                                    ps[:, :], lhsT=pwt[:, g, :], rhs=pooled[:, ti * 512:(ti + 1) * 512], start=True, stop=True),
                                    reads=['poolw', 'pooled'], writes=[pt])
                                T.op('act', lambda e, ps=ps, g=g, ti=ti: e.activation(
                                    out=ych[:, ti * 512:(ti + 1) * 512], in_=ps[:, :], func=AF.Identity,
                                    scale=pvc("pool_scale", g)), reads=[pt, 'pv'], writes=['ychA'])
                            T.dma('sp', yTd[g * 128:(g + 1) * 128, :], ych[:, :], reads=['ychA'], writes=[T.new('yT_' + s.name)])
                        T.barrier()

                    with ExitStack() as sub:
                        xbp = sb(sub, "xbp", [128, nseq, L + 3], F32)
                        xf = sb(sub, "xf", [128, nseq, L], F32)
                        xfb = sb(sub, "xfb", [128, Tn], BF16)
                        G = [sb(sub, "G%d" % i, [128, nseq, L], F32) for i in range(4)]
                        om = sb(sub, "om", [128, nseq, L], F32)
                        hf = sb(sub, "hf", [128, nseq, L], F32)
                        gg = sb(sub, "gg", [128, nseq, L], F32)
                        ych = sb(sub, "ychB", [128, Tn], BF16)
                        T.op('pool', lambda e: e.memset(xbp[:], 0.0), writes=['xbp'])
                        slab_x, stx = get_slab(1)
                        slab_g, stg_ = get_slab(2)

                        def flat(b):
                            return b[:].rearrange("p a b -> p (a b)")
                        for cb in range(4):
                            for ti in range(s.ntile):
                                ps, pt = proj_fm(slab_x, stx, cb, ti)
                                copy_op(evac_engine(ti), tview(xbp, 2, ti), pview(ps), [pt], ['xbp'])
                            for ti in range(s.ntile):
                                ps, pt = proj_fm(slab_g, stg_, cb, ti)
                                gv = tview(gg, 0, ti)
                                ov = tview(om, 0, ti)
                                T.op('act', lambda e, ov=ov, ps=ps: e.activation(out=ov, in_=pview(ps), func=AF.Square),
                                     reads=[pt], writes=['om'])
                                T.op('dve', lambda e, ov=ov: e.tensor_scalar(out=ov, in0=ov, scalar1=0.044715, scalar2=1.0,
                                                                            op0=ALU.mult, op1=ALU.add), reads=['om'], writes=['om'])
                                T.op('dve', lambda e, ov=ov, ps=ps: e.tensor_tensor(out=ov, in0=ov, in1=pview(ps), op=ALU.mult),
                                     reads=['om', pt], writes=['om'])
                                T.op('act', lambda e, ov=ov: e.activation(out=ov, in_=ov, func=AF.Sigmoid, scale=1.5957691216057308),
                                     reads=['om'], writes=['om'])
                                T.op('dve', lambda e, ov=ov, gv=gv, ps=ps: e.tensor_tensor(out=gv, in0=ov, in1=pview(ps), op=ALU.mult),
                                     reads=['om', pt], writes=['gg'])
                            T.op('dve', lambda e, cb=cb: e.tensor_scalar(
                                out=xf[:], in0=xbp[:, :, 0:L], scalar1=pvc("lru_conv_w", 0 * 4 + cb), scalar2=pvc("lru_conv_b", cb),
                                op0=ALU.mult, op1=ALU.add), reads=['xbp', 'pv'], writes=['xf'])
                            for jt in range(1, 4):
                                T.op('dve', lambda e, cb=cb, jt=jt: e.scalar_tensor_tensor(
                                    out=xf[:], in0=xbp[:, :, jt:jt + L], scalar=pvc("lru_conv_w", jt * 4 + cb), in1=xf[:],
                                    op0=ALU.mult, op1=ALU.add), reads=['xbp', 'xf', 'pv'], writes=['xf'])
                            T.op('act', lambda e: e.activation(out=xfb[:, :], in_=flat(xf), func=AF.Identity),
                                 reads=['xf'], writes=['xfb'])
                            for dg in range(4):
                                for ti in range(s.ntile):
                                    ps, pt = bank()
                                    T.op('pe', lambda e, ps=ps, dg=dg, ti=ti, cb=cb: e.matmul(
                                        ps[:, :], lhsT=gw[:, dg * 4 + cb, :], rhs=xfb[:, ti * 512:(ti + 1) * 512],
                                        start=True, stop=True), reads=['gw', 'xfb'], writes=[pt])
                                    T.op('act', lambda e, ps=ps, dg=dg, ti=ti, cb=cb: e.activation(
                                        out=tview(G[dg], 0, ti), in_=pview(ps), func=AF.Sigmoid,
                                        bias=pvc("lru_gate_b", dg * 4 + cb), scale=1.0), reads=[pt, 'pv'], writes=[('G', dg)])
                            for d in range(2):
                                R_, I_ = G[d * 2], G[d * 2 + 1]
                                rt, it_ = ('G', d * 2), ('G', d * 2 + 1)
                                T.op('act', lambda e, d=d, cb=cb, R_=R_: e.activation(
                                    out=om[:], in_=R_[:], func=AF.Exp, scale=nsp[:, 1, d * 4 + cb:d * 4 + cb + 1]),
                                    reads=[rt, 'nsp'], writes=['om'])
                                T.op('act', lambda e, d=d, cb=cb, R_=R_: e.activation(
                                    out=R_[:], in_=R_[:], func=AF.Exp, scale=nsp[:, 0, d * 4 + cb:d * 4 + cb + 1]),
                                    reads=[rt, 'nsp'], writes=[rt])
                                T.op('dve', lambda e: e.tensor_scalar(out=om[:], in0=om[:], scalar1=-1.0, scalar2=1.0,
                                                                      op0=ALU.mult, op1=ALU.add), reads=['om'], writes=['om'])
                                T.op('dve', lambda e: e.tensor_scalar(out=om[:], in0=om[:], scalar1=1e-30, scalar2=None,
                                                                      op0=ALU.max), reads=['om'], writes=['om'])
                                T.op('act', lambda e: e.activation(out=om[:], in_=om[:], func=AF.Sqrt), reads=['om'], writes=['om'])
                                T.op('dve', lambda e, I_=I_: e.tensor_tensor(out=I_[:], in0=I_[:], in1=om[:], op=ALU.mult),
                                     reads=[it_, 'om'], writes=[it_])
                                T.op('dve', lambda e, I_=I_: e.tensor_tensor(out=I_[:], in0=I_[:], in1=xf[:], op=ALU.mult),
                                     reads=[it_, 'xf'], writes=[it_])
                                dst = hf if d == 0 else om
                                dtok = 'hf' if d == 0 else 'om'
                                for sq_ in range(nseq):
                                    if s.sample:
                                        col = l * 8 + d * 4 + cb
                                        init = h0all[:, col:col + 1]
                                    else:
                                        init = 0.0
                                    if d == 0:
                                        T.op('dve', lambda e, sq_=sq_, init=init, R_=R_, I_=I_, dst=dst: e.tensor_tensor_scan(
                                            out=dst[:, sq_, :], data0=R_[:, sq_, :], data1=I_[:, sq_, :], initial=init,
                                            op0=ALU.mult, op1=ALU.add), reads=[rt, it_, 'h0all'], writes=[dtok])
                                    else:
                                        T.op('dve', lambda e, sq_=sq_, init=init, R_=R_, I_=I_, dst=dst: e.tensor_tensor_scan(
                                            out=dst[:, sq_, ::-1], data0=R_[:, sq_, ::-1], data1=I_[:, sq_, ::-1], initial=init,
                                            op0=ALU.mult, op1=ALU.add), reads=[rt, it_, 'h0all'], writes=[dtok])
                                if not s.sample:
                                    src = hf[:, :, L - 1] if d == 0 else om[:, :, 0]
                                    dsto = bass.AP(nst_out.tensor, l * 1024 + d * 512 + cb * 128, [[1, 128], [DEPTH * 1024, nseq]])
                                    T.dma('sp', dsto, src, reads=[dtok], writes=[], allow_slow_non_contiguous=True)
                            T.op('dve', lambda e: e.tensor_tensor(out=hf[:], in0=hf[:], in1=om[:], op=ALU.add),
                                 reads=['hf', 'om'], writes=['hf'])
                            T.op('dve', lambda e: e.tensor_tensor(out=ych[:, :], in0=flat(hf), in1=flat(gg), op=ALU.mult),
                                 reads=['hf', 'gg'], writes=['ychB'])
                            T.dma('sp', yTd[(4 + cb) * 128:(5 + cb) * 128, :], ych[:, :], reads=['ychB'], writes=[T.new('yT_' + s.name)])
                        T.barrier()

                    with ExitStack() as sub:
                        cpad = sb(sub, "cpad", [128, 4, nseq, L + 30], BF16)
                        dgm = sb(sub, "dgm", [128, 124, 128], BF16)
                        sg = [sb(sub, "sg", [128, 512], F32) for _ in range(2)]
                        NP_ = 256
                        cvt_r = [sb(sub, "cvt", [128, 4, NP_], F32) for _ in range(2)]
                        sqt = sb(sub, "sqt", [128, 4, NP_], F32)
                        mt_r = [sb(sub, "mt", [128, NP_], F32) for _ in range(2)]
                        m2_r = [sb(sub, "m2", [128, NP_], F32) for _ in range(2)]
                        rs_r = [sb(sub, "rsC", [128, NP_], F32) for _ in range(2)]
                        sl_r = [sb(sub, "sl", [128, 4, NP_], BF16) for _ in range(2)]
                        ychp = [sb(sub, "ychC", [128, 4, NP_], BF16) for _ in range(2)]
                        pww = sb(sub, "pww", [128, 4, 512], BF16)
                        load_w(pww[:], pw_w[l].rearrange("(c p) e -> p c e", p=128), 'pww')
                        T.op('pool', lambda e: e.memset(cpad[:], 0.0), writes=['cpad'])
                        dww = pvr("cm_dw_w")
                        for idx in range(124):
                            T.op('pool' if idx % 2 else 'dve', lambda e, idx=idx: e.tensor_scalar(
                                out=dgm[:, idx, :], in0=ident_b, scalar1=dww[:, idx:idx + 1], scalar2=None, op0=ALU.mult),
                                reads=['cstb', 'pv'], writes=['dgm'])
                        slab_a, sta = get_slab(3)
                        slab_g, stg_ = get_slab(4)
                        for cc in range(4):
                            for ti in range(s.ntile):
                                psa, pta = proj_fm(slab_a, sta, cc, ti)
                                psg, ptg = proj_fm(slab_g, stg_, cc, ti)
                                r = ti % 2
                                T.op('act', lambda e, r=r, psg=psg: e.activation(out=sg[r][:], in_=psg[:, :], func=AF.Sigmoid),
                                     reads=[ptg], writes=[('sg', r)])
                                sgv = sg[r][:, :] if L >= 512 else sg[r][:, :].rearrange("p (a b) -> p a b", b=L)
                                T.op('dve', lambda e, cc=cc, ti=ti, psa=psa, sgv=sgv: e.tensor_tensor(
                                    out=tview(cpad[:, cc], 15, ti), in0=pview(psa), in1=sgv, op=ALU.mult),
                                    reads=[pta, ('sg', r)], writes=['cpad'])
                        pieces = []
                        for sq_ in range(nseq):
                            for off in range(0, L, NP_):
                                pieces.append((sq_, off))
                        n = NP_

                        def cm_s1(pci):
                            sq_, off = pieces[pci]
                            rr = pci % 2
                            cvt = cvt_r[rr]
                            for cc in range(4):
                                ps, pt = bank()

                                def emit(e, ps=ps, cc=cc, sq_=sq_, off=off):
                                    ins = None
                                    for jt in range(31):
                                        ins = e.matmul(ps[:, 0:n], lhsT=dgm[:, jt * 4 + cc, :],
                                                       rhs=cpad[:, cc, sq_, off + jt:off + jt + n], start=(jt == 0), stop=(jt == 30))
                                    return ins
                                T.op('pe', emit, reads=['dgm', 'cpad'], writes=[pt])
                                T.op('act', lambda e, ps=ps, cc=cc, cvt=cvt: e.activation(
                                    out=cvt[:, cc, 0:n], in_=ps[:, 0:n], func=AF.Identity, bias=pvc("cm_dw_b", cc), scale=1.0),
                                    reads=[pt, 'pv'], writes=[('cvt', rr)])

                        def cm_s2(pci):
                            rr = pci % 2
                            cvt, mt, m2, rs, sl = cvt_r[rr], mt_r[rr], m2_r[rr], rs_r[rr], sl_r[rr]
                            ct, mtk, m2k, rsk, slk = ('cvt', rr), ('mt', rr), ('m2', rr), ('rsC', rr), ('sl', rr)
                            T.op('dve', lambda e: e.tensor_tensor(out=sqt[:, :, 0:n], in0=cvt[:, :, 0:n], in1=cvt[:, :, 0:n],
                                                                  op=ALU.mult), reads=[ct], writes=['sqt'])
                            psm_, ptm_ = bank()
                            pss_, pts_ = bank()

                            def emit(e):
                                ins = None
                                for cc in range(4):
                                    ins = e.matmul(psm_[:, 0:n], lhsT=ones_f, rhs=cvt[:, cc, 0:n], start=(cc == 0), stop=(cc == 3))
                                return ins
                            T.op('pe', emit, reads=[ct, 'cst'], writes=[ptm_])

                            def emit(e):
                                ins = None
                                for cc in range(4):
                                    ins = e.matmul(pss_[:, 0:n], lhsT=ones_f, rhs=sqt[:, cc, 0:n], start=(cc == 0), stop=(cc == 3))
                                return ins
                            T.op('pe', emit, reads=['sqt', 'cst'], writes=[pts_])
                            T.op('act', lambda e: e.activation(out=mt[:, 0:n], in_=psm_[:, 0:n], func=AF.Identity,
                                                               scale=1.0 / 512), reads=[ptm_], writes=[mtk])
                            T.op('dve', lambda e: e.tensor_tensor(out=m2[:, 0:n], in0=mt[:, 0:n], in1=mt[:, 0:n], op=ALU.mult),
                                 reads=[mtk], writes=[m2k])
                            T.op('dve', lambda e: e.scalar_tensor_tensor(
                                out=m2[:, 0:n], in0=pss_[:, 0:n], scalar=1.0 / 512, in1=m2[:, 0:n], op0=ALU.mult, op1=ALU.subtract),
                                reads=[pts_, m2k], writes=[m2k])
                            T.op('act', lambda e: e.activation(out=rs[:, 0:n], in_=m2[:, 0:n], func=AF.Sqrt, bias=epst[:, 0:1],
                                                               scale=1.0), reads=[m2k, 'epst'], writes=[rsk])
                            T.op('dve', lambda e: e.reciprocal(out=rs[:, 0:n], in_=rs[:, 0:n]), reads=[rsk], writes=[rsk])
                            T.op('dve', lambda e: e.tensor_tensor(
                                out=cvt[:, :, 0:n], in0=cvt[:, :, 0:n], in1=mt[:, 0:n].unsqueeze(1).to_broadcast([128, 4, n]),
                                op=ALU.subtract), reads=[ct, mtk], writes=[ct])
                            T.op('dve', lambda e: e.tensor_tensor(
                                out=cvt[:, :, 0:n], in0=cvt[:, :, 0:n], in1=rs[:, 0:n].unsqueeze(1).to_broadcast([128, 4, n]),
                                op=ALU.mult), reads=[ct, rsk], writes=[ct])
                            for cc in range(4):
                                T.op('act', lambda e, cc=cc: e.activation(
                                    out=sl[:, cc, 0:n], in_=cvt[:, cc, 0:n], func=AF.Silu, bias=pvc("cm_ln_b", cc),
                                    scale=pvc("cm_ln_g", cc)), reads=[ct, 'pv'], writes=[slk])

                        def cm_s3(pci):
                            sq_, off = pieces[pci]
                            rr = pci % 2
                            sl = sl_r[rr]
                            slk = ('sl', rr)
                            t_abs = sq_ * L + off
                            ych = ychp[rr]
                            ytok = ('ychC', rr)
                            for ec in range(4):
                                ps, pt = bank()

                                def emit(e, ps=ps, ec=ec):
                                    ins = None
                                    for cc in range(4):
                                        ins = e.matmul(ps[:, 0:n], lhsT=pww[:, cc, ec * 128:(ec + 1) * 128], rhs=sl[:, cc, 0:n],
                                                       start=(cc == 0), stop=(cc == 3))
                                    return ins
                                T.op('pe', emit, reads=['pww', slk], writes=[pt])
                                T.op('act' if ec % 2 == 0 else 'dve', (lambda e, ps=ps, ec=ec: e.activation(
                                    out=ych[:, ec, 0:n], in_=ps[:, 0:n], func=AF.Identity, bias=pvc("cm_pw_b", ec), scale=1.0))
                                    if ec % 2 == 0 else (lambda e, ps=ps, ec=ec: e.tensor_scalar(
                                        out=ych[:, ec, 0:n], in0=ps[:, 0:n], scalar1=pvc("cm_pw_b", ec), scalar2=None,
                                        op0=ALU.add)), reads=[pt, 'pv'], writes=[ytok])
                            T.dma('sp', yTd[8 * 128:12 * 128, t_abs:t_abs + n].rearrange("(e p) t -> p e t", p=128), ych[:, :, 0:n],
                                  reads=[ytok], writes=[T.new('yT_' + s.name)])

                        cm_s1(0)
                        for pci in range(len(pieces)):
                            if pci + 1 < len(pieces):
                                cm_s1(pci + 1)
                            cm_s2(pci)
                            cm_s3(pci)
                        T.barrier()

                    with ExitStack() as sub:
                        nTk = Tn // 128
                        if s.sample:
                            Rt = sb(sub, "Rt", [128, 9, 8, 128], BF16)
                            ckT = sb(sub, "ckT", [128, 4, 512], BF16)
                            cvb = sb(sub, "cvb", [128, 4, 512], BF16)
                            T.dma('pool', cvb[:], cv_in[l].rearrange("(a p) f -> p a f", p=128), writes=['cvb'])
                            with ExitStack() as sub2:
                                ckf = sb(sub2, "ckf", [128, 4, 512], F32)
                                maskr = sb(sub2, "maskr", [128, 9, 128], F32)
                                rp = sb(sub2, "rp", [120, 31], F32)
                                zp = sb(sub2, "zp", [120, 128], F32)
                                HK = sb(sub2, "HK", [128, 120, 64], F32)
                                T.dma('sp', ckf[:], ck_in[l].rearrange("(a p) f -> p a f", p=128), writes=['ckf'])
                                T.dma('sp', maskr[:], maskr_in[:, :, :], writes=['maskr'])
                                for c in range(4):
                                    ps, pt = bank()

                                    def emit(e, ps=ps, c=c):
                                        ins = None
                                        for a_ in range(4):
                                            ins = e.transpose(out=ps[:, a_ * 128:(a_ + 1) * 128], in_=ckf[:, a_, c * 128:(c + 1) * 128],
                                                              identity=ident)
                                        return ins
                                    T.op('pe', emit, reads=['ckf', 'cst'], writes=[pt])
                                    copy_op('act', ckT[:, c, :], ps[:, :], [pt], ['ckT'])
                                T.dma('sp', rp[:], rpb_in[l], writes=['rp'])
                                T.op('pool', lambda e: e.memset(zp[:], 0.0), writes=['zp'])
                                T.op('dve', lambda e: e.tensor_copy(out=zp[:, 48:79], in_=rp[:, ::-1]), reads=['rp', 'zp'], writes=['zp'])
                                T.dma('sp', zpd[:, :], zp[:], reads=['zp'], writes=['zpd'])
                                for half in range(2):
                                    for q4 in range(4):
                                        src = bass.AP(zpd.tensor, q4 * 30 * 128, [[1, 64], [128, 30], [1, 64]])
                                        T.dma('sp', HK[half * 64:(half + 1) * 64, q4 * 30:(q4 + 1) * 30, :], src,
                                              reads=['zpd'], writes=[T.new('HK')])
                                HK4 = HK[:].rearrange("p (h r) q -> p h r q", r=15)
                                k_ = 0
                                for ty, base in enumerate(RT_BASES):
                                    for pr in range(2):
                                        for qr in range(2):
                                            dr = base + pr - qr
                                            pa = (1 - pr) * 64
                                            eng = 'dve' if k_ % 2 == 0 else 'pool'
                                            k_ += 1
                                            mv = maskr[pa:pa + 64, ty, qr * 64:(qr + 1) * 64].unsqueeze(1).to_broadcast([64, 8, 64])
                                            ov = Rt[pa:pa + 64, ty, :, qr * 64:(qr + 1) * 64]
                                            if abs(dr) <= 7:
                                                T.op(eng, lambda e, ov=ov, mv=mv, pa=pa, dr=dr: e.tensor_tensor(
                                                    out=ov, in0=HK4[pa:pa + 64, :, dr + 7, :], in1=mv, op=ALU.add),
                                                    reads=T.all('HK') + ['maskr'], writes=[('Rt', k_)])
                                            else:
                                                T.op(eng, lambda e, ov=ov, mv=mv: e.tensor_copy(out=ov, in_=mv),
                                                     reads=['maskr'], writes=[('Rt', k_)])
                                T.barrier()
                        qT = sb(sub, "qT", [128, 4, Tn], BF16)
                        kT = sb(sub, "kT", [128, 4, Tn], BF16)
                        Vt = sb(sub, "Vt", [128, nTk, 512], BF16)
                        stage = [sb(sub, "kvst", [128, 512], F32) for _ in range(2)]
                        stc = [0]
                        slab_q, stq = get_slab(5)
                        for c in range(4):
                            for ti in range(s.ntile):
                                ps, pt = proj_fm(slab_q, stq, c, ti)
                                T.op('act', lambda e, ps=ps, c=c, ti=ti: e.activation(
                                    out=qT[:, c, ti * 512:(ti + 1) * 512], in_=ps[:, :], func=AF.Identity, scale=0.125),
                                    reads=[pt], writes=['qT'])
                        slab_k, stk = get_slab(6)
                        for c in range(4):
                            for ti in range(s.ntile):
                                ps, pt = proj_fm(slab_k, stk, c, ti)
                                copy_op('dve', kT[:, c, ti * 512:(ti + 1) * 512], ps[:, :], [pt], ['kT'])
                        T.mark(1)

                        def proj_tm(slab, stok, tk):
                            ps, pt = bank()

                            def emit(e):
                                ins = None
                                for k in range(16):
                                    ins = e.matmul(ps[:, :], lhsT=hT[:, k, tk * 128:(tk + 1) * 128], rhs=slab[:, k, :],
                                                   start=(k == 0), stop=(k == 15))
                                return ins
                            T.op('pe', emit, reads=[stok, ('hT', tk // 4)], writes=[pt])
                            return ps, pt
                        if not s.sample:
                            for tk in range(nTk):
                                ps, pt = proj_tm(slab_k, stk, tk)
                                r = stc[0] % 2
                                stc[0] += 1
                                copy_op('act', stage[r][:], ps[:, :], [pt], [('kvst', r)])
                                T.dma('sp', nk_out[tk // 2, l, (tk % 2) * 128:(tk % 2 + 1) * 128, :], stage[r][:],
                                      reads=[('kvst', r)], writes=[])
                        T.mark(2)
                        slab_v, stv = get_slab(7)
                        for tk in range(nTk):
                            ps, pt = proj_tm(slab_v, stv, tk)
                            copy_op('dve', Vt[:, tk, :], ps[:, :], [pt], ['Vt'])
                            if not s.sample:
                                r = stc[0] % 2
                                stc[0] += 1
                                copy_op('act', stage[r][:], ps[:, :], [pt], [('kvst', r)])
                                T.dma('sp', nv_out[tk // 2, l, (tk % 2) * 128:(tk % 2 + 1) * 128, :], stage[r][:],
                                      reads=[('kvst', r)], writes=[])
                        T.mark(3)
                        pT = [sb(sub, "pT", [128, 9 * 128], BF16) for _ in range(2)]
                        rsd = [sb(sub, "rsd", [64, 256], F32) for _ in range(2)]
                        ones64 = cstb[:, 256:320]
                        if not s.sample:
                            ychs = [sb(sub, "ychD", [128, 4, 256], BF16) for _ in range(2)]
                            it2 = 0
                            for sq_ in range(nseq):
                                ych = ychs[sq_ % 2]
                                ytok = ('ychD', sq_ % 2)
                                for h in range(8):
                                    c, po = h // 2, (h % 2) * 64
                                    r = it2 % 2
                                    it2 += 1
                                    ps, pt = bank()

                                    def emit(e, ps=ps, c=c, po=po, sq_=sq_):
                                        ins = None
                                        for kc in range(2):
                                            ins = e.matmul(ps[:, kc * 256:(kc + 1) * 256],
                                                           lhsT=kT[po:po + 64, c, sq_ * 256 + kc * 128:sq_ * 256 + (kc + 1) * 128],
                                                           rhs=qT[po:po + 64, c, sq_ * 256:(sq_ + 1) * 256], start=True, stop=True)
                                        return ins
                                    T.op('pe', emit, reads=['kT', 'qT'], writes=[pt])
                                    T.op('act', lambda e, ps=ps, r=r: e.activation(out=pT[r][:, 0:512], in_=ps[:, :], func=AF.Exp),
                                         reads=[pt], writes=[('pT', r)])
                                    T.mark(4)
                                    pso, pto = bank()

                                    def emit(e, pso=pso, r=r, h=h, sq_=sq_):
                                        ins = None
                                        for kc in range(2):
                                            ins = e.matmul(pso[0:64, 0:256], lhsT=Vt[:, sq_ * 2 + kc, h * 64:(h + 1) * 64],
                                                           rhs=pT[r][:, kc * 256:(kc + 1) * 256], start=(kc == 0), stop=(kc == 1))
                                        for kc in range(2):
                                            ins = e.matmul(pso[0:64, 256:512], lhsT=ones64,
                                                           rhs=pT[r][:, kc * 256:(kc + 1) * 256], start=(kc == 0), stop=(kc == 1))
                                        return ins
                                    T.op('pe', emit, reads=['Vt', ('pT', r), 'cstb'], writes=[pto])
                                    T.mark(5)
                                    T.op('dve', lambda e, pso=pso, r=r: e.reciprocal(out=rsd[r][:, 0:256], in_=pso[0:64, 256:512]),
                                         reads=[pto], writes=[('rsd', r)])
                                    T.mark(6)
                                    T.op('dve', lambda e, pso=pso, r=r, c=c, po=po, ych=ych: e.tensor_tensor(
                                        out=ych[po:po + 64, c, :], in0=pso[0:64, 0:256],
                                        in1=rsd[r][:, 0:256], op=ALU.mult), reads=[pto, ('rsd', r)], writes=[ytok])
                                T.dma('sp', yTd[12 * 128:16 * 128, sq_ * 256:(sq_ + 1) * 256].rearrange("(e p) t -> p e t", p=128), ych[:],
                                      reads=[ytok], writes=[T.new('yT_' + s.name)])
                        else:
                            ychs = [sb(sub, "ychD", [128, 4, 128], BF16) for _ in range(2)]
                            rt_all = [('Rt', k_) for k_ in range(1, 37)]
                            items = [(qc, h) for qc in range(16) for h in range(8)]

                            def att_s1(ii):
                                qc, h = items[ii]
                                nb = nb_config(qc)
                                ntile_ = len(nb) + 4
                                c, po = h // 2, (h % 2) * 64
                                r = ii % 2
                                nbk = (ntile_ + 3) // 4
                                banks = [bank() for _ in range(nbk)]
                                qv = qT[po:po + 64, c, qc * 128:(qc + 1) * 128]
                                for bi, (ps, pt) in enumerate(banks):
                                    def emit(e, ps=ps, bi=bi):
                                        ins = None
                                        for i in range(bi * 4, min(ntile_, bi * 4 + 4)):
                                            o = ps[:, (i % 4) * 128:(i % 4 + 1) * 128]
                                            if i < len(nb):
                                                kc, ty = nb[i]
                                                e.matmul(o, lhsT=kT[po:po + 64, c, kc * 128:(kc + 1) * 128], rhs=qv, start=True, stop=False)
                                                ins = e.matmul(o, lhsT=flip_b, rhs=Rt[:, ty, h, :], start=False, stop=True)
                                            else:
                                                a_ = i - len(nb)
                                                ins = e.matmul(o, lhsT=ckT[po:po + 64, c, a_ * 128:(a_ + 1) * 128], rhs=qv, start=True, stop=True)
                                        return ins
                                    T.op('pe', emit, reads=['kT', 'qT', 'ckT', 'cstb'] + rt_all, writes=[pt])
                                    n_ = min(ntile_, bi * 4 + 4) - bi * 4
                                    T.op('act', lambda e, ps=ps, bi=bi, n_=n_: e.activation(
                                        out=pT[r][:, bi * 512:bi * 512 + n_ * 128], in_=ps[:, 0:n_ * 128], func=AF.Exp),
                                        reads=[pt], writes=[('pT', r)])

                            def att_s2(ii):
                                qc, h = items[ii]
                                nb = nb_config(qc)
                                ntile_ = len(nb) + 4
                                c, po = h // 2, (h % 2) * 64
                                r = ii % 2
                                ych = ychs[qc % 2]
                                ytok = ('ychD', qc % 2)
                                pso, pto = psb[6 + ii % 2], ('ps', 6 + ii % 2)

                                def emit(e):
                                    ins = None
                                    for i in range(ntile_):
                                        if i < len(nb):
                                            lhs = Vt[:, nb[i][0], h * 64:(h + 1) * 64]
                                        else:
                                            lhs = cvb[:, i - len(nb), h * 64:(h + 1) * 64]
                                        ins = e.matmul(pso[0:64, 0:128], lhsT=lhs, rhs=pT[r][:, i * 128:(i + 1) * 128],
                                                       start=(i == 0), stop=(i == ntile_ - 1))
                                    for i in range(ntile_):
                                        ins = e.matmul(pso[0:64, 128:256], lhsT=ones64, rhs=pT[r][:, i * 128:(i + 1) * 128],
                                                       start=(i == 0), stop=(i == ntile_ - 1))
                                    return ins
                                T.op('pe', emit, reads=['Vt', 'cvb', ('pT', r), 'cstb'], writes=[pto])
                                T.op('dve', lambda e: e.reciprocal(out=rsd[r][:, 0:128], in_=pso[0:64, 128:256]),
                                     reads=[pto], writes=[('rsd', r)])
                                T.op('dve', lambda e: e.tensor_tensor(
                                    out=ych[po:po + 64, c, :], in0=pso[0:64, 0:128],
                                    in1=rsd[r][:, 0:128], op=ALU.mult), reads=[pto, ('rsd', r)], writes=[ytok])
                                if h == 7:
                                    T.dma('sp', yTd[12 * 128:16 * 128, qc * 128:(qc + 1) * 128].rearrange("(e p) t -> p e t", p=128), ych[:],
                                          reads=[ytok], writes=[T.new('yT_' + s.name)])

                            att_s1(0)
                            for ii in range(len(items)):
                                if ii + 1 < len(items):
                                    att_s1(ii + 1)
                                att_s2(ii)
                        T.barrier()
                    T.barrier()

                with ExitStack() as ph:
                    wo = sb(ph, "wo", [128, 16, D], BF16)
                    wov = w_out[l].rearrange("(k p) n -> p k n", p=128)
                    for q4 in range(2):
                        for kh in range(4):
                            load_w(wo[:, kh * 4:(kh + 1) * 4, q4 * 1024:(q4 + 1) * 1024],
                                   wov[:, kh * 4:(kh + 1) * 4, q4 * 1024:(q4 + 1) * 1024], ('wo', q4, kh))
                    yt = [sb(ph, "yt", [128, NCH, 512], BF16) for _ in range(2)]
                    NXC = 6
                    xc = [sb(ph, "o_xc", [128, 512], F32) for _ in range(NXC)]
                    mix = [sb(ph, "mix", [128, NCH, 512], F32) for _ in range(2)]
                    sqn = [sb(ph, "sqn", [128, 512], F32) for _ in range(2)]
                    rstd = [sb(ph, "o_rstd", [128, 512], F32) for _ in range(2)]
                    xTv = xT[s.name].rearrange("(c p) t -> p c t", p=128)
                    yTv = yTd.rearrange("(c p) t -> p c t", p=128)
                    xci = [0]

                    def o_load(ti):
                        r = ti % 2
                        T.dma('sp', yt[r][:], yTv[:, :, ti * 512:(ti + 1) * 512], reads=[], writes=[('yt', r)])

                    def o_mm(ti, n):
                        r = ti % 2
                        pss, ptss = psb[6 + r], ('ps', 6 + r)
                        ps, pt = bank()

                        def emit(e, ps=ps, n=n, r=r):
                            ins = None
                            for k in range(16):
                                ins = e.matmul(ps[:, :], lhsT=wo[:, k, n * 128:(n + 1) * 128], rhs=yt[r][:, k, :],
                                               start=(k == 0), stop=(k == 15))
                            return ins
                        T.op('pe', emit, reads=[('wo', n // 8, kh) for kh in range(4)] + [('yt', r)], writes=[pt])
                        r2 = n % 2
                        T.op('act', lambda e, ps=ps, r2=r2: e.activation(out=sqn[r2][:], in_=ps[:, :], func=AF.Square),
                             reads=[pt], writes=[('sqn', r2)])
                        copy_op('dve', mix[r][:, n, :], ps[:, :], [pt], [('mix', r)])
                        T.op('pe', lambda e, pss=pss, r2=r2, n=n: e.matmul(pss[:, :], lhsT=ones_f, rhs=sqn[r2][:],
                                                                         start=(n == 0), stop=(n == NCH - 1)),
                             reads=[('sqn', r2), 'cst'], writes=[ptss])
                        if n == NCH - 1:
                            rstd_from_ps(pss, ptss, rstd[r], ('o_rstd', r))

                    def o_post(ti, c):
                        r = ti % 2
                        q = xci[0] % NXC
                        xci[0] += 1
                        T.dma('sp', xc[q][:], xTv[:, c, ti * 512:(ti + 1) * 512], reads=[], writes=[('o_xc', q)])
                        T.op('dve', lambda e: e.tensor_tensor(out=mix[r][:, c, :], in0=mix[r][:, c, :], in1=rstd[r][:, :], op=ALU.mult),
                             reads=[('mix', r), ('o_rstd', r)], writes=[('mix', r)])
                        T.op('dve', lambda e: e.scalar_tensor_tensor(
                            out=xc[q][:], in0=mix[r][:, c, :], scalar=modv[:, 1, c, j:j + 1], in1=xc[q][:],
                            op0=ALU.mult, op1=ALU.add), reads=[('mix', r), ('o_xc', q), 'modv'], writes=[('o_xc', q)])
                        T.dma('pool', xTv[:, c, ti * 512:(ti + 1) * 512], xc[q][:], reads=[('o_xc', q)], writes=[T.new('o_st')])

                    o_load(0)
                    if s.ntile > 1:
                        o_load(1)
                    for n in range(NCH):
                        o_mm(0, n)
                    for ti in range(s.ntile):
                        if ti + 2 < s.ntile:
                            o_load(ti + 2)
                        for n in range(NCH):
                            if ti + 1 < s.ntile:
                                o_mm(ti + 1, n)
                            o_post(ti, n)
                    T.barrier()

                for blk in range(Tn // 1024):
                    with ExitStack() as ph:
                        h2T = sb(ph, "h2T", [128, NCH, 1024], BF16)
                        gT = sb(ph, "gT", [128, NF, 1024], BF16)
                        rstd2 = [sb(ph, "f_rstd", [128, 512], F32) for _ in range(2)]
                        with ExitStack() as sub:
                            norm_phase(sub, s, l, blk * 2, 2, 2, 3, h2T, 'h2T', NT=128)
                            T.barrier()
                        with ExitStack() as sub:
                            FSW = 256
                            w1s = [sb(sub, "w1s", [128, 16, FSW], BF16) for _ in range(2)]
                            w3s = [sb(sub, "w3s", [128, 16, FSW], BF16) for _ in range(2)]
                            sil = [sb(sub, "sil", [128, 512], F32) for _ in range(2)]
                            w1v = w1_in[l].rearrange("(k p) n -> p k n", p=128)
                            w3v = w3_in[l].rearrange("(k p) n -> p k n", p=128)
                            it2 = 0
                            for fs in range(DFF // FSW):
                                r = fs % 2
                                for kh in range(2):
                                    load_w(w1s[r][:, kh * 8:(kh + 1) * 8, :], w1v[:, kh * 8:(kh + 1) * 8, fs * FSW:(fs + 1) * FSW], ('w1s', r, kh))
                                    load_w(w3s[r][:, kh * 8:(kh + 1) * 8, :], w3v[:, kh * 8:(kh + 1) * 8, fs * FSW:(fs + 1) * FSW], ('w3s', r, kh))
                                for fc in range(FSW // 128):
                                    f = fs * (FSW // 128) + fc
                                    for ti in range(2):
                                        psa, pta = bank()
                                        psb_, ptb = bank()

                                        def emit(e, ps=psa, w=w1s[r], fc=fc, ti=ti):
                                            ins = None
                                            for k in range(16):
                                                ins = e.matmul(ps[:, :], lhsT=w[:, k, fc * 128:(fc + 1) * 128],
                                                               rhs=h2T[:, k, ti * 512:(ti + 1) * 512], start=(k == 0), stop=(k == 15))
                                            return ins
                                        T.op('pe', emit, reads=[('w1s', r, 0), ('w1s', r, 1), ('h2T', ti)], writes=[pta])

                                        def emit(e, ps=psb_, w=w3s[r], fc=fc, ti=ti):
                                            ins = None
                                            for k in range(16):
                                                ins = e.matmul(ps[:, :], lhsT=w[:, k, fc * 128:(fc + 1) * 128],
                                                               rhs=h2T[:, k, ti * 512:(ti + 1) * 512], start=(k == 0), stop=(k == 15))
                                            return ins
                                        T.op('pe', emit, reads=[('w3s', r, 0), ('w3s', r, 1), ('h2T', ti)], writes=[ptb])
                                        r2 = it2 % 2
                                        it2 += 1
                                        T.op('act', lambda e, psa=psa, r2=r2: e.activation(out=sil[r2][:], in_=psa[:, :], func=AF.Silu),
                                             reads=[pta], writes=[('sil', r2)])
                                        T.op('dve', lambda e, psb_=psb_, r2=r2, f=f, ti=ti: e.tensor_tensor(
                                            out=gT[:, f, ti * 512:(ti + 1) * 512], in0=sil[r2][:], in1=psb_[:, :], op=ALU.mult),
                                            reads=[('sil', r2), ptb], writes=[('gT', ti)])
                            T.barrier()
                        with ExitStack() as sub:
                            w2s = [sb(sub, "w2s", [128, NF, 256], BF16) for _ in range(2)]
                            ost = [sb(sub, "ost", [128, 512], F32) for _ in range(2)]
                            sqs = [sb(sub, "sqs", [128, 512], F32) for _ in range(2)]
                            w2v = w2_in[l].rearrange("(f p) n -> p f n", p=128)
                            pss = [psb[6], psb[7]]
                            ptss = [('ps', 6), ('ps', 7)]
                            psi2 = [0]

                            def bank6():
                                i = psi2[0] % 6
                                psi2[0] += 1
                                return psb[i], ('ps', i)
                            it2 = 0
                            for ns in range(8):
                                r = ns % 2
                                for g4 in range(4):
                                    load_w(w2s[r][:, g4 * 11:(g4 + 1) * 11, :], w2v[:, g4 * 11:(g4 + 1) * 11, ns * 256:(ns + 1) * 256],
                                           ('w2s', r, g4))
                                for nci in range(2):
                                    n = ns * 2 + nci
                                    for ti in range(2):
                                        ps, pt = bank6()

                                        def emit(e, ps=ps, r=r, nci=nci, ti=ti):
                                            ins = None
                                            for f in range(NF):
                                                ins = e.matmul(ps[:, :], lhsT=w2s[r][:, f, nci * 128:(nci + 1) * 128],
                                                               rhs=gT[:, f, ti * 512:(ti + 1) * 512], start=(f == 0), stop=(f == NF - 1))
                                            return ins
                                        T.op('pe', emit, reads=[('w2s', r, g4) for g4 in range(4)] + [('gT', ti)], writes=[pt])
                                        r2 = it2 % 2
                                        it2 += 1
                                        T.op('act', lambda e, ps=ps, r2=r2: e.activation(out=sqs[r2][:], in_=ps[:, :], func=AF.Square),
                                             reads=[pt], writes=[('sqs', r2)])
                                        copy_op('dve', ost[r2][:], ps[:, :], [pt], [('ost', r2)])
                                        T.dma('sp', oT[n * 128:(n + 1) * 128, ti * 512:(ti + 1) * 512], ost[r2][:],
                                              reads=[('ost', r2)], writes=[('oT', ti)])
                                        T.op('pe', lambda e, ti=ti, r2=r2, n=n: e.matmul(pss[ti][:, :], lhsT=ones_f, rhs=sqs[r2][:],
                                                                                        start=(n == 0), stop=(n == NCH - 1)),
                                             reads=[('sqs', r2), 'cst'], writes=[ptss[ti]])
                            for ti in range(2):
                                rstd_from_ps(pss[ti], ptss[ti], rstd2[ti], ('f_rstd', ti))
                            T.barrier()
                        with ExitStack() as sub:
                            ot = sb(sub, "ot", [128, NCH, 512], F32)
                            xt = sb(sub, "f_xt", [128, NCH, 512], F32)
                            xTv = xT[s.name].rearrange("(c p) t -> p c t", p=128)
                            oTv = oT.rearrange("(c p) t -> p c t", p=128)
                            for ti in range(2):
                                tg = blk * 2 + ti
                                T.dma('sp', ot[:], oTv[:, :, ti * 512:(ti + 1) * 512], reads=[('oT', ti)], writes=['ot'])
                                T.dma('sp', xt[:], xTv[:, :, tg * 512:(tg + 1) * 512], reads=[('xT', s.name, tg)], writes=['f_xt'])
                                T.op('dve', lambda e, ti=ti: e.tensor_tensor(
                                    out=ot[:], in0=ot[:], in1=rstd2[ti][:, :].unsqueeze(1).to_broadcast([128, NCH, 512]), op=ALU.mult),
                                    reads=['ot', ('f_rstd', ti)], writes=['ot'])
                                for c in range(NCH):
                                    T.op('dve', lambda e, c=c: e.scalar_tensor_tensor(
                                        out=xt[:, c, :], in0=ot[:, c, :], scalar=modv[:, 3, c, j:j + 1], in1=xt[:, c, :],
                                        op0=ALU.mult, op1=ALU.add), reads=['ot', 'f_xt', 'modv'], writes=['f_xt'])
                                T.dma('sp', xTv[:, :, tg * 512:(tg + 1) * 512], xt[:], reads=['f_xt'], writes=[('xT', s.name, tg)])
                            T.barrier()
                        T.barrier()

        with ExitStack() as ph:
            xin = [sb(ph, "fxin", [128, NCH, 128], F32) for _ in range(2)]
            xo = [sb(ph, "fxo", [128, D], F32) for _ in range(2)]
            it = 0
            for s in segs:
                xTv = xT[s.name].rearrange("(c p) t -> p c t", p=128)
                for tc in range(s.T // 128):
                    r = it % 2
                    it += 1
                    T.dma('sp', xin[r][:], xTv[:, :, tc * 128:(tc + 1) * 128], reads=[('xT', s.name, tc // 4)], writes=[('fxin', r)])
                    for g in range(4):
                        ps, pt = bank()

                        def emit(e, ps=ps, r=r, g=g):
                            ins = None
                            for jj in range(4):
                                ins = e.transpose(out=ps[:, jj * 128:(jj + 1) * 128], in_=xin[r][:, 4 * g + jj, :], identity=ident)
                            return ins
                        T.op('pe', emit, reads=[('fxin', r), 'cst'], writes=[pt])
                        copy_op(evac_engine(g), xo[r][:, g * 512:(g + 1) * 512], ps[:, :], [pt], [('fxo', r, g)])
                    T.dma('sp', y_out[s.name][tc * 128:(tc + 1) * 128, :], xo[r][:],
                          reads=[('fxo', r, g) for g in range(4)], writes=[])
            T.barrier()


_CACHE = {}


def make_in_maps(inputs, ncores=8):
    hc = host_consts()
    f = lambda a: np.ascontiguousarray(np.asarray(a, dtype=np.float32))
    shared = {}
    for nm in ["w_ada", "b_ada", "g_pre_mix", "g_post_mix", "g_pre_ffn", "g_post_ffn", "w_in", "pool_w", "pool_scale",
               "lru_conv_b", "lru_gate_w", "cm_dw_b", "cm_ln_g", "cm_ln_b", "cm_pw_w", "cm_pw_b", "w_out",
               "ffn_w1", "ffn_w3", "ffn_w2"]:
        shared[nm] = f(inputs[nm])
    shared["lru_conv_w"] = f(inputs["lru_conv_w"]).reshape(DEPTH, 4 * 512)
    shared["lru_gate_b"] = f(inputs["lru_gate_b"]).reshape(DEPTH, 4 * 512)
    shared["lru_lambda"] = f(inputs["lru_lambda"]).reshape(DEPTH, 2 * 512)
    shared["cm_dw_w"] = f(inputs["cm_dw_w"]).reshape(DEPTH, 31 * 512)
    shared["na_rpb"] = f(inputs["na_rpb"]).reshape(DEPTH, 120, 31)
    shared.update(hc)
    xp = f(inputs["x_prompt"])
    xs = f(inputs["x_sample"])
    ck = f(inputs["cache_k"])
    cv = f(inputs["cache_v"])
    st = f(inputs["state_lru"])
    c = f(inputs["c"])
    cctx = f(inputs["c_ctx"])
    nsmp = xs.shape[0]
    maps = []
    for i in range(ncores):
        si = i % nsmp
        m = dict(shared)
        m["xp"] = np.ascontiguousarray(xp[i * NPSEQ:(i + 1) * NPSEQ].reshape(NPSEQ * LP, D))
        m["xs"] = np.ascontiguousarray(xs[si])
        m["ck"] = np.ascontiguousarray(ck[si].reshape(DEPTH, PAST, 512))
        m["cv"] = np.ascontiguousarray(cv[si].reshape(DEPTH, PAST, 512))
        m["st"] = np.ascontiguousarray(st[si].reshape(-1))
        m["cond"] = np.ascontiguousarray(np.stack([cctx, c[si]], axis=0))
        maps.append(m)
    return maps


def kernel(**inputs):
    key = "full"
    if key not in _CACHE:
        _CACHE[key] = build_program()
    nc = _CACHE[key]
    maps = make_in_maps(inputs, 8)
    res = run_bass_kernel_spmd(nc, maps, core_ids=list(range(8)))
    rs = res.results
    B = 32
    y_prompt = np.concatenate([rs[i]["yp"].reshape(NPSEQ, LP, D) for i in range(8)], axis=0).astype(np.float32)
    y_sample = np.stack([rs[0]["ys"], rs[1]["ys"]], axis=0).astype(np.float32)
    nk = np.concatenate([rs[i]["nk"] for i in range(8)], axis=0).reshape(B, DEPTH, LP, 8, 64).astype(np.float32)
    nv = np.concatenate([rs[i]["nv"] for i in range(8)], axis=0).reshape(B, DEPTH, LP, 8, 64).astype(np.float32)
    nst = np.concatenate([rs[i]["nst"] for i in range(8)], axis=0).reshape(B, DEPTH, 2, 512).astype(np.float32)
    return (y_prompt, y_sample, nk, nv, nst)
```
